# Optimizing a Trainium2 kernel written in Bass

```python
import numpy as np
import jax, jax.numpy as jnp
from jax import lax

D_MODEL = 1024
BATCH = 2
SEQ = 8192
DEPTH = 1

GRID_W = 64
D_MIX = D_MODEL
D_ATT = D_MIX // 2
D_HG = D_MIX - D_ATT
ATT_HEADS = 8
ATT_HEAD_DIM = D_ATT // ATT_HEADS
HG_HEADS = 4
HG_HEAD_DIM = D_HG // HG_HEADS
WIN_ROWS_MAX = 8
WIN_COLS = 16
Q_BLOCK_W = 16
K_BLOCK_W = 32
HG_CHUNK = 64
IN_WIDTHS = (D_ATT, D_ATT, D_ATT, D_ATT, D_HG, D_HG, D_HG, D_HG, D_HG)
D_IN = sum(IN_WIDTHS)
LN_EPS = 1e-5
RMS_EPS = 1e-6
DEEPNORM_ALPHA = (2.0 * DEPTH) ** 0.25
DEEPNORM_BETA = (8.0 * DEPTH) ** -0.25

kernel_name = "hymba_natten_hgrn2_deepnorm_encoder"


def neighbourhood_attention(q, k, v, rpb):
    B, T, H, dh = q.shape
    rows = T // GRID_W
    kr = min(WIN_ROWS_MAX, rows)
    q = q.reshape(B, rows, GRID_W, H, dh)
    k = k.reshape(B, rows, GRID_W, H, dh)
    v = v.reshape(B, rows, GRID_W, H, dh)
    r = np.arange(rows)
    row_start = np.clip(r - kr // 2, 0, rows - kr)
    row_idx = row_start[:, None] + np.arange(kr)[None, :]
    dr = row_idx - r[:, None] + (WIN_ROWS_MAX - 1)
    k_rows = k[:, row_idx]
    v_rows = v[:, row_idx]
    scale = dh ** -0.5
    outs = []
    for j in range(GRID_W // Q_BLOCK_W):
        qc = np.arange(j * Q_BLOCK_W, (j + 1) * Q_BLOCK_W)
        kc0 = int(np.clip(j * Q_BLOCK_W - WIN_COLS // 2, 0, GRID_W - K_BLOCK_W))
        kc = np.arange(kc0, kc0 + K_BLOCK_W)
        col_start = np.clip(qc - WIN_COLS // 2, 0, GRID_W - WIN_COLS)
        in_win = (kc[None, :] >= col_start[:, None]) & (kc[None, :] < col_start[:, None] + WIN_COLS)
        dc = np.clip(kc[None, :] - qc[:, None], -(WIN_COLS - 1), WIN_COLS - 1) + (WIN_COLS - 1)
        q_blk = q[:, :, j * Q_BLOCK_W:(j + 1) * Q_BLOCK_W]
        k_blk = k_rows[:, :, :, kc0:kc0 + K_BLOCK_W]
        v_blk = v_rows[:, :, :, kc0:kc0 + K_BLOCK_W]
        s = jnp.einsum('brqhd,brakhd->bhrqak', q_blk, k_blk,
                       preferred_element_type=jnp.float32) * scale
        bias = rpb[:, dr[:, None, :, None], dc[None, :, None, :]]
        s = jnp.where(in_win[None, None, None, :, None, :], s + bias[None].astype(jnp.float32), -jnp.inf)
        p = jax.nn.softmax(s.reshape(B, H, rows, Q_BLOCK_W, kr * K_BLOCK_W), axis=-1)
        p = p.reshape(B, H, rows, Q_BLOCK_W, kr, K_BLOCK_W).astype(v.dtype)
        outs.append(jnp.einsum('bhrqak,brakhd->brqhd', p, v_blk))
    o = jnp.concatenate(outs, axis=2)
    return o.reshape(B, T, H * dh)


def hgrn2_chunk_scan(q, k, v, log_f):
    B, H, T, dk = q.shape
    dv = v.shape[-1]
    n = T // HG_CHUNK
    causal = np.tril(np.ones((HG_CHUNK, HG_CHUNK), dtype=bool))

    def to_chunks(a):
        return jnp.moveaxis(a.reshape(B, H, n, HG_CHUNK, a.shape[-1]), 2, 0)

    def step(S, inp):
        qc, kc, vc, gc = inp
        b = jnp.cumsum(gc, axis=2)
        b_last = b[:, :, -1:, :]
        o_inter = jnp.einsum('bhtd,bhde->bhte', qc * jnp.exp(b), S)
        diff = jnp.where(causal[:, :, None], b[:, :, :, None, :] - b[:, :, None, :, :], -jnp.inf)
        a = jnp.einsum('bhtd,bhtsd,bhsd->bhts', qc, jnp.exp(diff), kc)
        o_intra = jnp.einsum('bhts,bhse->bhte', a, vc)
        S = jnp.exp(b_last[:, :, 0, :, None]) * S + jnp.einsum('bhsd,bhse->bhde', kc * jnp.exp(b_last - b), vc)
        return S, o_inter + o_intra

    S0 = jnp.zeros((B, H, dk, dv), jnp.float32)
    _, o = lax.scan(step, S0, (to_chunks(q), to_chunks(k), to_chunks(v), to_chunks(log_f)))
    return jnp.moveaxis(o, 0, 2).reshape(B, H, T, dv)


def hgrn2_forget(z, lb_logits, layer):
    lb = jnp.cumsum(jax.nn.softmax(lb_logits.astype(jnp.float32), axis=0), axis=0)[layer]
    lb = lb.reshape(HG_HEADS, 1, HG_HEAD_DIM)
    f = lb + (1.0 - lb) * jax.nn.sigmoid(z)
    return jnp.log(f), 1.0 - f


def hybrid_layer(x, layer, w_in, b_in, rpb, lb_fwd_logits, lb_bwd_logits, hg_norm_gain,
                 w_out, b_out, ln_gain, ln_bias):
    B, T, _ = x.shape
    h = jnp.einsum('btd,de->bte', x, w_in) + b_in
    split_at = list(np.cumsum(IN_WIDTHS)[:-1])
    q_a, k_a, v_a, g_a, q_h, z_fwd, z_bwd, i_h, g_h = jnp.split(h, split_at, axis=-1)

    heads_a = lambda t: t.reshape(B, T, ATT_HEADS, ATT_HEAD_DIM)
    o_a = neighbourhood_attention(heads_a(q_a), heads_a(k_a), heads_a(v_a), rpb)
    o_a = o_a * jax.nn.silu(g_a)

    heads_h = lambda t: t.reshape(B, T, HG_HEADS, HG_HEAD_DIM).transpose(0, 2, 1, 3).astype(jnp.float32)
    qh, ih = heads_h(q_h), heads_h(i_h)
    logf_f, k_f = hgrn2_forget(heads_h(z_fwd), lb_fwd_logits, layer)
    logf_b, k_b = hgrn2_forget(heads_h(z_bwd), lb_bwd_logits, layer)
    o_fwd = hgrn2_chunk_scan(qh, k_f, ih, logf_f)
    flip = lambda t: jnp.flip(t, axis=2)
    o_bwd = flip(hgrn2_chunk_scan(flip(qh), flip(k_b), flip(ih), flip(logf_b)))
    o_h = (o_fwd + o_bwd).transpose(0, 2, 1, 3)
    o_h = o_h * lax.rsqrt(jnp.mean(jnp.square(o_h), axis=-1, keepdims=True) + RMS_EPS)
    o_h = o_h.reshape(B, T, D_HG) * hg_norm_gain.astype(jnp.float32)
    o_h = o_h.astype(x.dtype) * jax.nn.silu(g_h)

    y = jnp.einsum('bte,ed->btd', jnp.concatenate([o_a, o_h], axis=-1), w_out) + b_out
    r = (DEEPNORM_ALPHA * x + y).astype(jnp.float32)
    mu = jnp.mean(r, axis=-1, keepdims=True)
    var = jnp.mean(jnp.square(r - mu), axis=-1, keepdims=True)
    r = (r - mu) * lax.rsqrt(var + LN_EPS) * ln_gain.astype(jnp.float32) + ln_bias.astype(jnp.float32)
    return r.astype(x.dtype)


def setup_inputs(seed: int = 0) -> dict:
    key = jax.random.key(seed)
    ks = jax.random.split(key, 12)
    x = jax.random.normal(ks[0], (BATCH, SEQ, D_MODEL), jnp.float32)
    col_scale = np.ones((D_IN,), np.float32)
    off = np.cumsum((0,) + IN_WIDTHS)
    col_scale[off[2]:off[3]] = DEEPNORM_BETA
    col_scale[off[7]:off[8]] = DEEPNORM_BETA
    w_in = jax.random.normal(ks[1], (DEPTH, D_MODEL, D_IN), jnp.float32) * (D_MODEL ** -0.5) * jnp.asarray(col_scale)
    b_in = 0.02 * jax.random.normal(ks[2], (DEPTH, D_IN), jnp.float32)
    rpb = 0.1 * jax.random.normal(ks[3], (DEPTH, ATT_HEADS, 2 * WIN_ROWS_MAX - 1, 2 * WIN_COLS - 1), jnp.float32)
    lb_fwd_logits = 0.5 * jax.random.normal(ks[4], (DEPTH + 1, D_HG), jnp.float32)
    lb_bwd_logits = 0.5 * jax.random.normal(ks[5], (DEPTH + 1, D_HG), jnp.float32)
    hg_norm_gain = 1.0 + 0.02 * jax.random.normal(ks[6], (DEPTH, D_HG), jnp.float32)
    w_out = jax.random.normal(ks[7], (DEPTH, D_MIX, D_MODEL), jnp.float32) * (D_MIX ** -0.5) * DEEPNORM_BETA
    b_out = 0.02 * jax.random.normal(ks[8], (DEPTH, D_MODEL), jnp.float32)
    ln_gain = 1.0 + 0.02 * jax.random.normal(ks[9], (DEPTH, D_MODEL), jnp.float32)
    ln_bias = 0.02 * jax.random.normal(ks[10], (DEPTH, D_MODEL), jnp.float32)
    return {"x": x, "w_in": w_in, "b_in": b_in, "rpb": rpb,
            "lb_fwd_logits": lb_fwd_logits, "lb_bwd_logits": lb_bwd_logits,
            "hg_norm_gain": hg_norm_gain, "w_out": w_out, "b_out": b_out,
            "ln_gain": ln_gain, "ln_bias": ln_bias}


def reference(x, w_in, b_in, rpb, lb_fwd_logits, lb_bwd_logits, hg_norm_gain,
              w_out, b_out, ln_gain, ln_bias):
    for layer in range(DEPTH):
        x = hybrid_layer(x, layer, w_in[layer], b_in[layer], rpb[layer],
                         lb_fwd_logits, lb_bwd_logits, hg_norm_gain[layer],
                         w_out[layer], b_out[layer], ln_gain[layer], ln_bias[layer])
    return x
```

```python
import numpy as np
from contextlib import ExitStack
import concourse.bass as bass
import concourse.mybir as mybir
from concourse.bass_utils import run_bass_kernel_spmd

F32 = mybir.dt.float32
BF16 = mybir.dt.bfloat16
AF = mybir.ActivationFunctionType
ALU = mybir.AluOpType

NCORES = 8
SEQ = 8192
DM = 1024
HALO = 512
NLOC = 2048
NTOK = NLOC + 2 * HALO
NT = NTOK // 128
T0 = HALO // 128
T1 = T0 + NLOC // 128
ALPHA = 2.0 ** 0.25
LN_EPS = 1e-5
RMS_EPS = 1e-6
NEG = -30000.0
SKIP = set()

C_ID, C_UF, C_RF, C_UB, C_RB, C_IND = 0, 128, 256, 384, 512, 640
C_GAIN = 648
C_TM = C_GAIN + 512
C_BQK = C_TM + 24
NCST = C_BQK + 8


class _Op:
    __slots__ = ("e", "fn", "deps", "dma", "est", "lat", "idx", "tok", "start", "end", "wk")

    def __init__(self, e, fn, deps, dma, est, lat, idx):
        self.e, self.fn, self.deps, self.dma, self.est, self.lat, self.idx = e, fn, deps, dma, est, lat, idx
        self.tok = None
        self.start = self.end = 0.0


class Prog:
    EST = {"pe": 0.5, "act": 0.62, "dve": 0.7, "pool": 1.3, "sp": 0.1}
    HOP = 0.37
    SLACK = 0.3

    def __init__(self, nc):
        self.nc = nc
        self.eng = {"pe": nc.tensor, "act": nc.scalar, "dve": nc.vector, "pool": nc.gpsimd, "sp": nc.sync}
        self.segs = [[]]
        self.last_w = {}
        self.readers = {}
        self.n = 0
    mute = False

    def op(self, e, fn, r=(), w=(), dma=None, est=None):
        if self.mute:
            return None
        deps = []
        seen = set()

        def add(o):
            if o is not None and id(o) not in seen:
                seen.add(id(o))
                deps.append(o)
        for k in r:
            add(self.last_w.get(k))
        for k in w:
            add(self.last_w.get(k))
            for t in self.readers.get(k, ()):
                add(t)
        if dma is not None:
            lat = est if est is not None else (6.0 if e == "pool" else 3.0)
            dur = 1.0 if e == "pool" else 0.1
        else:
            dur = est if est is not None else self.EST[e]
            lat = dur
        o = _Op(e, fn, deps, dma, dur, lat, self.n)
        o.wk = tuple(w)
        self.n += 1
        self.segs[-1].append(o)
        for k in r:
            self.readers.setdefault(k, []).append(o)
        for k in w:
            self.last_w[k] = o
            self.readers[k] = []
        return o

    def barrier(self):
        self.segs.append([])

    def _schedule(self, seg):
        inseg = {id(o) for o in seg}
        ndep = {}
        users = {}
        for o in seg:
            c = 0
            for d_ in o.deps:
                if id(d_) in inseg:
                    c += 1
                    users.setdefault(id(d_), []).append(o)
            ndep[id(o)] = c
        tail = {}
        for o in sorted(seg, key=lambda o_: -o_.idx):
            t = 0.0
            for u in users.get(id(o), ()):
                t = max(t, tail[id(u)] + (0.0 if (u.e == o.e and o.dma is None) else self.HOP))
            tail[id(o)] = t + o.lat
        free = {e: 0.0 for e in self.eng}
        ready = {}
        cand = [o for o in seg if ndep[id(o)] == 0]
        for o in cand:
            ready[id(o)] = 0.0
        order = {e: [] for e in self.eng}
        done = 0
        while cand:
            tmin = min(max(free[o.e], ready[id(o)]) for o in cand)
            best, bkey = None, None
            for o in cand:
                st = max(free[o.e], ready[id(o)])
                if st > tmin + self.SLACK:
                    continue
                key = (-tail[id(o)], o.idx)
                if bkey is None or key < bkey:
                    best, bkey = o, key
            o = best
            cand.remove(o)
            o.start = max(free[o.e], ready[id(o)])
            free[o.e] = o.start + o.est
            o.end = o.start + o.lat
            order[o.e].append(o)
            done += 1
            for u in users.get(id(o), ()):
                t = o.end + (0.0 if (u.e == o.e and o.dma is None) else self.HOP)
                if t > ready.get(id(u), 0.0):
                    ready[id(u)] = t
                ndep[id(u)] -= 1
                if ndep[id(u)] == 0:
                    cand.append(u)
        assert done == len(seg), (done, len(seg))
        return order

    def emit(self, final_ops):
        nc = self.nc
        plan = {e: [] for e in self.eng}
        count = {e: 0 for e in self.eng}
        dma_cnt = {}
        waited = {e: {} for e in self.eng}
        semnames = set()
        for seg in self.segs:
            order = self._schedule(seg)
            allops = sorted(seg, key=lambda o: (o.start, o.idx))
            for o in allops:
                if o.dma is None:
                    count[o.e] += 1
                    o.tok = ("E_" + o.e, count[o.e], o.e)
                else:
                    dma_cnt[o.dma] = dma_cnt.get(o.dma, 0) + 1
                    o.tok = ("D_" + o.dma, 16 * dma_cnt[o.dma], None)
                semnames.add(o.tok[0])
            for e in self.eng:
                c0 = count[e] - sum(1 for o in order[e] if o.dma is None)
                for o in order[e]:
                    if o.dma is None:
                        c0 += 1
                        o.tok = ("E_" + e, c0, e)
            for e in self.eng:
                for o in order[e]:
                    need = {}
                    for d_ in o.deps:
                        s_, v, se = d_.tok
                        if se == e and e == "pe" and o.dma is None:
                            continue
                        if v > need.get(s_, 0):
                            need[s_] = v
                    waits = []
                    for s_, v in need.items():
                        if waited[e].get(s_, 0) >= v:
                            continue
                        waited[e][s_] = v
                        waits.append((s_, v))
                    plan[e].append((waits, o.fn, o.tok))
            toks = [("E_" + e, count[e]) for e in self.eng if count[e] > 0] + [("D_" + k, 16 * c) for k, c in dma_cnt.items()]
            for e in self.eng:
                waits = []
                for s_, v in toks:
                    if (s_ == "E_" + e and e == "pe") or waited[e].get(s_, 0) >= v:
                        continue
                    waited[e][s_] = v
                    waits.append((s_, v))
                if waits:
                    plan[e].append((waits, None, None))
        final_tokens = [o.tok for o in final_ops]
        with ExitStack() as es:
            sems = {n_: es.enter_context(nc.semaphore(n_)) for n_ in sorted(semnames)}
            block = es.enter_context(nc.Block())

            def run(e, eng):
                for waits, fn, tok in plan[e]:
                    for s_, v in waits:
                        eng.wait_ge(sems[s_], v)
                    if fn is None:
                        continue
                    ins = fn(eng)
                    ins.then_inc(sems[tok[0]], 16 if tok[2] is None else 1)
                if e == "sp":
                    for s_, v, _ in final_tokens:
                        eng.wait_ge(sems[s_], v)

            @block.tensor
            def _(eng):
                run("pe", eng)

            @block.scalar
            def _(eng):
                run("act", eng)

            @block.vector
            def _(eng):
                run("dve", eng)

            @block.gpsimd
            def _(eng):
                run("pool", eng)

            @block.sync
            def _(eng):
                run("sp", eng)


class Alloc:
    def __init__(self, nc, base, size):
        self.nc, self.base, self.size, self.off, self.n = nc, base, size, 0, 0

    def mark(self):
        return self.off

    def reset(self, m):
        self.off = m

    def __call__(self, name, shape, dtype):
        nb = int(np.prod(shape[1:])) * (4 if dtype == F32 else 2)
        nb = (nb + 63) // 64 * 64
        assert self.off + nb <= self.size, (name, self.off, nb, self.size)
        self.n += 1
        t = self.nc.alloc_sbuf_tensor_at(f"{name}_{self.n}", list(shape), dtype, offset=self.base + self.off)
        self.off += nb
        return t


def build_nc(debug=False, upto=9, ntl=99):
    nc = bass.Bass("TRN2", target_bir_lowering=False)
    d_xT = nc.dram_tensor("xT", [DM, NTOK], F32, kind="ExternalInput").ap()
    d_xN = nc.dram_tensor("xN", [NLOC, DM], F32, kind="ExternalInput").ap()
    d_win = nc.dram_tensor("w_in", [DM, 4608], F32, kind="ExternalInput").ap()
    d_wout = nc.dram_tensor("w_out", [DM, DM], F32, kind="ExternalInput").ap()
    d_brow = nc.dram_tensor("brow", [1, 4608 + 1024], F32, kind="ExternalInput").ap()
    d_cst = nc.dram_tensor("cst", [128, NCST], F32, kind="ExternalInput").ap()
    d_lbl = nc.dram_tensor("lbl", [128, 6 * 512], F32, kind="ExternalInput").ap()
    d_lnp = nc.dram_tensor("lnp", [128, 2 * 1024], F32, kind="ExternalInput").ap()
    d_tabi = nc.dram_tensor("tabi", [8, 128, 5 * 128], F32, kind="ExternalInput").ap()
    d_tabb = nc.dram_tensor("tabb", [4, 8, 128, 7 * 128], F32, kind="ExternalInput").ap()
    d_out = nc.dram_tensor("out", [NLOC, DM], F32, kind="ExternalOutput").ap()
    d_dbg = None
    if debug:
        d_dbg = nc.dram_tensor("dbg", [NLOC, 1024], F32, kind="ExternalOutput").ap()

    arena = nc.alloc_sbuf_tensor("arena", [128, 207 * 1024], mybir.dt.uint8)
    base = nc.lookup_mloc(arena).addr
    A = Alloc(nc, base, 207 * 1024)
    P = Prog(nc)

    B = [nc.alloc_psum_tensor(f"pb{i}", [128, 512], F32) for i in range(7)]
    B7 = nc.alloc_psum_tensor("pb7", [128, 512], F32)
    BB = B + [B7]

    CST = A("cst", [128, NCST], F32)
    BIAS16 = A("bias16", [1, 2048], BF16)
    ONES16 = A("ones16", [1, 128], BF16)
    IDENT16 = A("ident16", [128, 128], BF16)
    W = A("w", [128, 8, 2048], BF16)
    _xt_off = A.mark()
    XT = [A(f"xt{i}", [128, 8, 512], BF16) for i in range(2)]
    LNP = nc.alloc_sbuf_tensor_at("lnp_alias", [128, 2048], F32, offset=A.base + _xt_off + 8192)
    XN32 = [nc.alloc_sbuf_tensor_at(f"xn_alias{i}", [128, 1024], F32, offset=A.base + _xt_off + 4096 * i) for i in range(2)]
    OH16 = A("oh16", [128, 16, 512], BF16)
    OA16 = A("oa16", [128, 16, 512], BF16)
    phase_mark = A.mark()

    w_in_v = d_win.rearrange("(k p) c -> p k c", p=128)
    xT_v = d_xT.rearrange("(k p) t -> p k t", p=128)

    def load_w(dst_c0, src_ap, ncols):
        P.op("pool", lambda e: e.dma_start(out=W[:, :, dst_c0:dst_c0 + ncols], in_=src_ap),
             w=[f"W{dst_c0 // 512}" for _ in range(1)] if ncols == 512 else [f"W{i}" for i in range(dst_c0 // 512, (dst_c0 + ncols) // 512)],
             dma=f"w{dst_c0 // 512}")

    def load_bias(dst_c0, src_c0, ncols):
        P.op("pool", lambda e: e.dma_start(out=BIAS16[0:1, dst_c0:dst_c0 + ncols], in_=d_brow[0:1, src_c0:src_c0 + ncols]),
             w=[f"BIAS{dst_c0 // 512 + i}" for i in range(ncols // 512)], dma=f"bias{dst_c0 // 512}")

    def load_xt(buf, g):
        P.op("pool", lambda e: e.dma_start(out=XT[buf][:, :, :], in_=xT_v[:, :, g * 512:(g + 1) * 512]),
             w=[f"XT{buf}"], dma=f"xt{buf}")

    def proj_n(bank, buf, tl, wslot, bias_slot):
        def fn(e):
            ins = None
            for k in range(8):
                ins = e.matmul(BB[bank][:, :], lhsT=XT[buf][:, k, tl * 128:(tl + 1) * 128],
                               rhs=W[:, k, wslot * 512:(wslot + 1) * 512], start=(k == 0), stop=(bias_slot is None and k == 7))
            if bias_slot is not None:
                ins = e.matmul(BB[bank][:, :], lhsT=ONES16[0:1, :], rhs=BIAS16[0:1, bias_slot * 512:(bias_slot + 1) * 512],
                               start=False, stop=True)
            return ins
        rk = [f"XT{buf}", f"W{wslot}"] + ([f"BIAS{bias_slot}", "ONES16"] if bias_slot is not None else [])
        P.op("pe", fn, r=rk, w=[f"B{bank}"], est=2.1 if bias_slot is not None else 1.8)

    P.op("sp", lambda e: e.dma_start(out=CST[:, :], in_=d_cst[:, :]), w=["CST"], dma="cst")
    P.op("dve", lambda e: e.memset(ONES16[:, :], 1.0), w=["ONES16"])
    P.op("dve", lambda e: e.tensor_copy(out=IDENT16[:, :], in_=CST[:, C_ID:C_ID + 128]), r=["CST"], w=["IDENT16"])

    LB = [A(f"lb{d}", [128, 512], F32) for d in range(2)]
    OML = [A(f"oml{d}", [128, 512], F32) for d in range(2)]
    _of_off = A.mark()
    OF32 = A("of32", [128, 16, 512], F32)
    LBL = nc.alloc_sbuf_tensor_at("lbl_alias", [128, 2048], F32, offset=A.base + _of_off)
    G2s = [A(f"g2{i}", [128, 512], F32) for i in range(2)]
    SGT = A("sgt", [128, 512], F32)
    ONE32 = A("one32", [128, 8], F32)
    QST = A("qst", [128, 16, 512], BF16)
    SGc = [A(f"sgc{i}", [128, 512], BF16) for i in range(3)]
    A32 = A("a32", [128, 512], F32)
    K32 = A("k32", [128, 512], F32)
    EB32 = A("eb32", [128, 512], F32)
    ENB32 = A("enb32", [128, 512], F32)
    QE16 = A("qe16", [128, 512], BF16)
    KE16 = A("ke16", [128, 512], BF16)
    V16 = A("v16", [128, 512], BF16)
    QET16 = A("qet16", [128, 512], BF16)
    KET16 = A("ket16", [128, 512], BF16)
    AT16 = A("at16", [128, 512], BF16)
    ST32 = A("st32", [128, 512], F32)
    S16 = A("s16", [128, 512], BF16)
    DEC = A("dec", [128, 8], F32)
    O32 = [A(f"o32{i}", [128, 512], F32) for i in range(2)]
    ON32 = [A(f"on32{i}", [128, 512], F32) for i in range(2)]
    BQ, BI = O32[0], ON32[0]
    KBQ, KBI = ["O320"], [f"ON320{h_}" for h_ in range(4)]
    SS = [A(f"ss{i}", [128, 4], F32) for i in range(2)]
    RSTD = [A(f"rstd{i}", [128, 4], F32) for i in range(2)]

    P.op("sp", lambda e: e.dma_start(out=LBL[:, :], in_=d_lbl[:, 0:2048]), w=["LBL"], dma="lbl")
    P.op("dve", lambda e: e.memset(ONE32[:, :], 1.0), w=["ONE32"])
    P.op("sp", lambda e: e.dma_start(out=BQ[:, :], in_=d_lbl[:, 2048:2560]), w=KBQ, dma="bq")
    P.op("sp", lambda e: e.dma_start(out=BI[:, :], in_=d_lbl[:, 2560:3072]), w=KBI, dma="bi")
    for d in range(2):
        o = d * 1024
        P.op("dve", lambda e, o=o: e.tensor_sub(out=LBL[:, o:o + 512], in0=LBL[:, o:o + 512], in1=LBL[:, o + 512:o + 1024]),
             r=["LBL"], w=["LBL"])
        P.op("act", lambda e, o=o, d=d: e.activation(out=LB[d][:, :], in_=LBL[:, o:o + 512], func=AF.Sigmoid),
             r=["LBL"], w=[f"LB{d}"])
        P.op("dve", lambda e, d=d: e.tensor_scalar(out=OML[d][:, :], in0=LB[d][:, :], scalar1=-1.0, scalar2=1.0,
                                                    op0=ALU.mult, op1=ALU.add), r=[f"LB{d}"], w=[f"OML{d}"])

    if upto == 0:
        P.emit([])
        return nc

    A32 = [A32, A("a32b", [128, 512], F32)]
    K32 = [K32, A("k32b", [128, 512], F32)]
    V16 = [V16, A("v16b", [128, 512], BF16), A("v16c", [128, 512], BF16)]
    Q32 = [None, None]
    QET16 = [QET16, A("qet16b", [128, 512], BF16)]
    AT16 = [AT16, A("at16b", [128, 512], BF16)]
    DEC = [DEC, A("decb", [128, 8], F32), A("decc", [128, 8], F32)]
    EB32 = [EB32, A("eb32b", [128, 512], F32)]
    ENB32 = [ENB32, A("enb32b", [128, 512], F32)]
    QE16 = [QE16, A("qe16b", [128, 512], BF16)]
    KE16 = [KE16, A("ke16b", [128, 512], BF16)]
    KET16 = [KET16, A("ket16b", [128, 512], BF16)]
    T1s = [A(f"t1{i}", [128, 512], F32) for i in range(2)]

    def hg_pass(d):
        zc = 2560 if d == 0 else 3072
        tiles = list(range(0, T1)) if d == 0 else list(range(NT - 1, T0 - 1, -1))
        tiles = tiles[:ntl]
        gorder = []
        for T in tiles:
            if T // 4 not in gorder:
                gorder.append(T // 4)
        gbuf = {g: i % 2 for i, g in enumerate(gorder)}
        load_xt(gbuf[gorder[0]], gorder[0])
        if d == 0:
            load_w(512, w_in_v[:, :, zc:zc + 512], 512)
            load_bias(512, zc, 512)
            load_w(1024, w_in_v[:, :, 3584:4096], 512)
            load_bias(1024, 3584, 512)
            load_w(0, w_in_v[:, :, 2048:2560], 512)
            load_bias(0, 2048, 512)
            load_w(1536, w_in_v[:, :, 4096:4608], 512)
            load_bias(1536, 4096, 512)
        else:
            load_w(512, w_in_v[:, :, zc:zc + 512], 512)
            load_bias(512, zc, 512)
            load_w(0, w_in_v[:, :, 512:1024], 512)
        P.op("dve", lambda e: e.memset(ST32[:, :], 0.0), w=[f"ST32h{h_}" for h_ in range(4)])
        P.op("dve", lambda e: e.memset(S16[:, :], 0.0), w=["S16"])
        UD = C_UF if d == 0 else C_UB
        RD = C_RF if d == 0 else C_RB
        corder = (0, 1) if d == 0 else (1, 0)
        first_tile = {g: [t for t in tiles if t // 4 == g][0] for g in gorder}
        KVB = {0: 6, 1: 3}
        prevdec = [None]

        def a_steps(i):
            T = tiles[i]
            pp, p3 = i % 2, i % 3
            g = T // 4
            buf = gbuf[g]
            tl = T % 4
            local = T0 <= T < T1
            li = T - T0
            a32, k32, v16, q32 = A32[pp], K32[pp], V16[p3], Q32[pp]
            ka, kk, kv_, kq = f"A32{pp}", f"K32{pp}", f"V16{p3}", f"Q32{pp}"

            def s1():
                gi = gorder.index(g)
                if T == first_tile[g] and gi + 1 < len(gorder):
                    load_xt(gbuf[gorder[gi + 1]], gorder[gi + 1])
                proj_n(0, buf, tl, 1, 1)
                P.op("act", lambda e: e.activation(out=a32[:, :], in_=B[0][:, :], func=AF.Exp, scale=-1.0), r=["B0"], w=[ka])

            def s2():
                if local and d == 1:
                    return
                if local:
                    proj_n(7, buf, tl, 2, None)
                    P.op("dve", lambda e: e.tensor_tensor(out=OA16[:, li, :], in0=B7[:, :], in1=BI[:, :], op=ALU.add),
                         r=["B7"] + KBI, w=[f"OAV{li}"])
                else:
                    proj_n(7, buf, tl, 2, 2)
                    P.op("act", lambda e: e.activation(out=v16[:, :], in_=B7[:, :], func=AF.Identity,
                                                       scale=CST[:, C_TM + T:C_TM + T + 1]), r=["B7", "CST"], w=[kv_])

            def s3():
                if local and d == 0:
                    proj_n(0, buf, tl, 0, None)
                    P.op("dve", lambda e: e.tensor_tensor(out=QST[:, li, :], in0=B[0][:, :], in1=BQ[:, :], op=ALU.add),
                         r=["B0"] + KBQ, w=[f"QST{li}"])

            def s4():
                if local and d == 1:
                    proj_n(7, buf, tl, 3, 3)
                    P.op("act", lambda e: e.activation(out=SGT[:, :], in_=B7[:, :], func=AF.Exp, scale=-1.0), r=["B7"], w=["SGT"])
                    P.op("act", lambda e: e.activation(out=SGT[:, :], in_=SGT[:, :], func=AF.Ln, bias=ONE32[:, 0:1]),
                         r=["SGT", "ONE32"], w=["SGT"])
                    P.op("act", lambda e: e.activation(out=SGT[:, :], in_=SGT[:, :], func=AF.Exp, scale=-1.0), r=["SGT"], w=["SGT"])
                    P.op("dve", lambda e: e.tensor_tensor(out=SGT[:, :], in0=B7[:, :], in1=SGT[:, :], op=ALU.mult),
                         r=["B7", "SGT"], w=["SGT"])
                    P.op("pool", lambda e: e.tensor_tensor(out=SGc[p3][:, :], in0=SGT[:, :], in1=CST[:, C_GAIN:C_GAIN + 512], op=ALU.mult),
                         r=["SGT", "CST"], w=[f"SGc{p3}"])

            def s5():
                G2, ER32B = G2s[pp], T1s[pp]
                kg2, kt1 = f"G2{pp}", f"T1{pp}"
                P.op("act", lambda e: e.activation(out=G2[:, :], in_=a32[:, :], func=AF.Ln, bias=ONE32[:, 0:1]),
                     r=[ka, "ONE32"], w=[kg2])
                P.op("pool" if d == 0 else "dve", lambda e: e.tensor_tensor(out=ER32B[:, :], in0=a32[:, :], in1=LB[d][:, :], op=ALU.mult),
                     r=[ka, f"LB{d}"], w=[kt1])
                P.op("act", lambda e: e.activation(out=ER32B[:, :], in_=ER32B[:, :], func=AF.Ln, bias=ONE32[:, 0:1]),
                     r=[kt1, "ONE32"], w=[kt1])
                P.op("pool", lambda e: e.tensor_tensor(out=a32[:, :], in0=ER32B[:, :], in1=G2[:, :], op=ALU.subtract),
                     r=[kt1, kg2, ka], w=[ka])
                P.op("act", lambda e: e.activation(out=G2[:, :], in_=G2[:, :], func=AF.Exp, scale=-1.0), r=[kg2], w=[kg2])
                P.op("pool", lambda e: e.tensor_tensor(out=G2[:, :], in0=G2[:, :], in1=OML[d][:, :], op=ALU.mult),
                     r=[kg2, f"OML{d}"], w=[kg2])
                P.op("pool" if d == 0 else "dve", lambda e: e.tensor_tensor(out=k32[:, :], in0=OML[d][:, :], in1=G2[:, :], op=ALU.subtract),
                     r=[kg2, f"OML{d}"], w=[kk])
            return [s1, s2, s3, s4, s5]

        def b1_steps(i):
            T = tiles[i]
            pp, p3 = i % 2, i % 3
            local = T0 <= T < T1
            a32, k32, v16, q32 = A32[pp], K32[pp], V16[p3], Q32[pp]
            ka, kk, kv_, kq = f"A32{pp}", f"K32{pp}", f"V16{p3}", f"Q32{pp}"
            qet, at, dec = QET16[pp], AT16[pp], DEC[p3]
            kqet, kat, kdec = f"QET{pp}", f"AT{pp}", f"DEC{p3}"
            vref, vkey = (OA16[:, T - T0, :], f"OAV{T - T0}") if local else (v16, kv_)
            EB32_, ENB32_, QE16_, KE16_, KET16_ = EB32[pp], ENB32[pp], QE16[pp], KE16[pp], KET16[pp]
            keb, kenb, kqe, kke, kket = f"EB32{pp}", f"ENB32{pp}", f"QE16{pp}", f"KE16{pp}", f"KET16{pp}"

            def s1():
                def blast(e):
                    ins = None
                    for h in range(4):
                        ins = e.matmul(B[1][:, 2 * h:2 * h + 2], lhsT=a32[:, h * 128:(h + 1) * 128],
                                       rhs=CST[:, C_IND:C_IND + 2], start=True, stop=True)
                    return ins
                P.op("pe", blast, r=[ka, "CST"], w=["B1"], est=0.45)
                P.op("act", lambda e: e.activation(out=dec[:, :], in_=B[1][:, 0:8], func=AF.Exp), r=["B1"], w=[kdec])
                P.op("pe", lambda e: e.matmul(B[2][:, :], lhsT=CST[:, UD:UD + 128], rhs=a32[:, :], start=True, stop=True),
                     r=[ka, "CST"], w=["B2"], est=0.9)
                P.op("act", lambda e: e.activation(out=ENB32_[:, :], in_=B[2][:, :], func=AF.Exp, scale=-1.0),
                     r=["B2"], w=[kenb])
                if local:
                    P.op("act", lambda e: e.activation(out=EB32_[:, :], in_=B[2][:, :], func=AF.Exp), r=["B2"], w=[keb])

            def s2():
                P.op("pool" if local else "dve", lambda e: e.tensor_tensor(out=KE16_[:, :], in0=k32[:, :], in1=ENB32_[:, :], op=ALU.mult),
                     r=[kk, kenb], w=[kke])
                if local:
                    P.op("dve", lambda e: e.tensor_tensor(out=QE16_[:, :], in0=QST[:, T - T0, :], in1=EB32_[:, :], op=ALU.mult),
                         r=[f"QST{T - T0}", keb], w=[kqe])

            def s3():
                if not local:
                    return

                def trq(e):
                    ins = None
                    for h in range(4):
                        ins = e.matmul(B[4][:, h * 128:(h + 1) * 128], lhsT=QE16_[:, h * 128:(h + 1) * 128],
                                       rhs=IDENT16[:, :], start=True, stop=True)
                    return ins

                def trk(e):
                    ins = None
                    for h in range(4):
                        ins = e.matmul(B[4][:, h * 128:(h + 1) * 128], lhsT=KE16_[:, h * 128:(h + 1) * 128],
                                       rhs=IDENT16[:, :], start=True, stop=True)
                    return ins
                P.op("pe", trq, r=[kqe, "IDENT16"], w=["B4"], est=0.27)
                P.op("act", lambda e: e.activation(out=qet[:, :], in_=B[4][:, :], func=AF.Identity), r=["B4"], w=[kqet])
                P.op("pe", trk, r=[kke, "IDENT16"], w=["B4"], est=0.27)
                P.op("act", lambda e: e.activation(out=KET16_[:, :], in_=B[4][:, :], func=AF.Identity), r=["B4"], w=[kket])

            def s4():
                if not local:
                    return

                def amat(e):
                    ins = None
                    for h in range(4):
                        ins = e.matmul(B[4][:, h * 128:(h + 1) * 128], lhsT=KET16_[:, h * 128:(h + 1) * 128],
                                       rhs=qet[:, h * 128:(h + 1) * 128], start=True, stop=True)
                    return ins
                P.op("pe", amat, r=[kqet, kket], w=["B4"], est=0.27)
                P.op("dve", lambda e: e.tensor_tensor(
                    out=at[:, :].rearrange("p (h t) -> p h t", h=4),
                    in0=B[4][:, :].rearrange("p (h t) -> p h t", h=4),
                    in1=CST[:, UD:UD + 128].unsqueeze(1).broadcast_to([128, 4, 128]), op=ALU.mult),
                    r=["B4", "CST"], w=[kat])

            def s5():
                def kv(e):
                    ins = None
                    for c in range(2):
                        for h in range(4):
                            ins = e.matmul(B[KVB[c]][:, h * 128:(h + 1) * 128],
                                           lhsT=KE16_[64 * c:64 * c + 64, h * 128:(h + 1) * 128],
                                           rhs=vref[64 * c:64 * c + 64, h * 128:(h + 1) * 128], start=True, stop=True)
                    return ins
                P.op("pe", kv, r=[kke, vkey], w=["B6", "B3"], est=0.6)
            return [s1, s2, s3, s4, s5]

        def b2_steps(i):
            T = tiles[i]
            pp, p3 = i % 2, i % 3
            local = T0 <= T < T1
            li = T - T0
            v16 = V16[p3]
            kv_ = f"V16{p3}"
            qet, at, dec = QET16[pp], AT16[pp], DEC[p3]
            kqet, kat, kdec = f"QET{pp}", f"AT{pp}", f"DEC{p3}"
            vref, vkey = (OA16[:, li, :], f"OAV{li}") if local else (v16, kv_)

            def chunk(ci):
                c = corder[ci]
                if local:
                    def omat(e):
                        ins = None
                        if ci == 0:
                            for h in range(4):
                                ins = e.matmul(B[5][:, h * 128:(h + 1) * 128], lhsT=at[:, h * 128:(h + 1) * 128],
                                               rhs=vref[:, h * 128:(h + 1) * 128], start=(h == 0), stop=False,
                                               skip_group_check=True)
                        for h in range(4):
                            ins = e.matmul(B[5][64 * c:64 * c + 64, h * 128:(h + 1) * 128],
                                           lhsT=qet[:, h * 128 + 64 * c:h * 128 + 64 * c + 64],
                                           rhs=S16[:, h * 128:(h + 1) * 128], start=False, stop=True,
                                           skip_group_check=True, tile_position=(0, 64 * c))
                        return ins
                    P.op("pe", omat, r=[kat, vkey, kqet, "S16"], w=["B5"], est=0.5)
                pdec, pkey, pc = prevdec[0] if prevdec[0] is not None else (dec, kdec, c)
                for h in range(4):
                    P.op("dve", lambda e, h=h: e.scalar_tensor_tensor(
                        out=ST32[:, h * 128:(h + 1) * 128], in0=ST32[:, h * 128:(h + 1) * 128],
                        scalar=pdec[:, 2 * h + pc:2 * h + pc + 1], in1=B[KVB[c]][:, h * 128:(h + 1) * 128],
                        op0=ALU.mult, op1=ALU.add), r=[f"ST32h{h}", pkey, f"B{KVB[c]}"], w=[f"ST32h{h}"], est=0.4)
                P.op("dve", lambda e: e.tensor_tensor(
                    out=S16[:, :].rearrange("p (h t) -> p h t", h=4),
                    in0=ST32[:, :].rearrange("p (h t) -> p h t", h=4),
                    in1=dec[:, :].rearrange("p (h c) -> p h c", c=2)[:, :, c:c + 1].broadcast_to([128, 4, 128]), op=ALU.mult),
                    r=[f"ST32h{h_}" for h_ in range(4)] + [kdec], w=["S16"])
                prevdec[0] = (dec, kdec, c)

            def s1():
                chunk(0)

            def s2():
                chunk(1)

            def s3():
                if not local:
                    return
                if d == 0:
                    P.op("dve", lambda e: e.tensor_copy(out=OF32[:, li, :], in_=B[5][:, :]), r=["B5"], w=[f"OF{li}"])
                else:
                    o32, on32, ss, rstd = O32[pp], ON32[pp], SS[pp], RSTD[pp]
                    ko, kon, kss, krs = f"O32{pp}", f"ON32{pp}", f"SS{pp}", f"RSTD{pp}"
                    P.op("dve", lambda e: e.tensor_tensor(out=o32[:, :], in0=B[5][:, :], in1=OF32[:, li, :], op=ALU.add),
                         r=["B5", f"OF{li}"], w=[ko])
                    P.op("pool", lambda e: e.memset(ss[:, :], 0.0), w=[kss + str(h_) for h_ in range(4)])
                    for h in range(4):
                        P.op("act", lambda e, h=h: e.activation(out=on32[:, h * 128:(h + 1) * 128], in_=o32[:, h * 128:(h + 1) * 128],
                                                                func=AF.Square, accum_out=ss[:, h:h + 1]),
                             r=[ko], w=[kon + str(h), kss + str(h)], est=0.35)
                    P.op("dve", lambda e: e.tensor_scalar(out=rstd[:, :], in0=ss[:, :], scalar1=1.0 / 128.0, scalar2=RMS_EPS,
                                                          op0=ALU.mult, op1=ALU.add), r=[kss + str(h_) for h_ in range(4)], w=[krs], est=0.2)
                    P.op("act", lambda e: e.activation(out=rstd[:, :], in_=rstd[:, :], func=AF.Ln), r=[krs], w=[krs], est=0.2)
                    P.op("act", lambda e: e.activation(out=rstd[:, :], in_=rstd[:, :], func=AF.Exp, scale=-0.5), r=[krs], w=[krs], est=0.2)
                    P.op("pool", lambda e: e.tensor_tensor(
                        out=on32[:, :].rearrange("p (h t) -> p h t", h=4),
                        in0=o32[:, :].rearrange("p (h t) -> p h t", h=4),
                        in1=rstd[:, :].unsqueeze(2).broadcast_to([128, 4, 128]), op=ALU.mult),
                        r=[ko, krs] + [kon + str(h_) for h_ in range(4)], w=[kon + str(h_) for h_ in range(4)])
                    P.op("pool", lambda e: e.tensor_tensor(out=OH16[:, li, :], in0=on32[:, :], in1=SGc[p3][:, :], op=ALU.mult),
                         r=[kon + str(h_) for h_ in range(4)] + [f"SGc{p3}"], w=[f"OH{li}"])
            return [s1, s2, s3]

        n = len(tiles)
        nop = lambda: None
        for it in range(-2, n):
            b2 = b2_steps(it) if it >= 0 else [nop] * 3
            b1 = b1_steps(it + 1) if 0 <= it + 1 < n else [nop] * 5
            a = a_steps(it + 2) if it + 2 < n else [nop] * 5
            for st in (b2[0], a[0], b1[0], b2[1], a[1], b1[1], b2[2], a[2], b1[2], a[3], b1[3], a[4], b1[4]):
                st()

    hg_pass(0)
    if upto == 1:
        P.emit([])
        return nc
    hg_pass(1)
    if upto == 2:
        P.emit([])
        return nc
    load_xt(1, 0)
    load_w(1024, w_in_v[:, :, 1024:1536], 512)

    P.barrier()
    A.reset(phase_mark)

    KT = A("kt", [128, 4, NTOK], BF16)
    VA = A("va", [128, NT * 8, 65], BF16)
    QTG = A("qtg", [128, 4, 512], BF16)
    SGA = A("sga", [128, 4, 512], BF16)
    EBI = A("ebi", [128, 8, 5 * 128], BF16)
    EBH = [A(f"ebh{i}", [128, 7 * 128], BF16) for i in range(3)]
    STG = [A(f"stg{i}", [128, 7 * 128], F32) for i in range(2)]
    PX32s = [A(f"px32{i}", [128, 1024], F32) for i in range(3)]
    P16 = [A(f"p16{i}", [128, 1024], BF16) for i in range(3)]
    RDEN = A("rden", [128, 8], F32)
    ONE32a = A("one32a", [128, 8], F32)
    SGTa = A("sgta", [128, 512], F32)
    P.op("dve", lambda e: e.memset(ONE32a[:, :], 1.0), w=["ONE32a"])
    OA32 = A("oa32", [128, 512], F32)
    b_qk = CST[:, C_BQK:C_BQK + 8]

    load_bias(0, 1024, 512)
    load_w(1536, w_in_v[:, :, 0:512], 512)
    load_w(512, w_in_v[:, :, 1536:2048], 512)
    load_bias(512, 1536, 512)
    P.op("pool", lambda e: e.memset(VA[:, :, 64:65], 1.0), w=["VA1"])

    def load_tab(dst, dram_ap, nj, key, sidx):
        for h in range(8):
            sb = (sidx[0]) % 2
            sidx[0] += 1
            P.op("sp", lambda e, h=h, sb=sb: e.dma_start(out=STG[sb][:, 0:nj * 128], in_=dram_ap[h]),
                 w=[f"STG{sb}"], dma=f"stg{sb}")
            P.op("act", lambda e, h=h, sb=sb: e.activation(out=dst[:, h, 0:nj * 128], in_=STG[sb][:, 0:nj * 128], func=AF.Exp),
                 r=[f"STG{sb}"], w=[key])
    sidx = [0]
    load_tab(EBI, d_tabi, 5, "EBI", sidx)

    def proj_t(bank, buf, wc0, dst_fn, bias_col, keyw):
        def fn(e):
            ins = None
            for k in range(8):
                ins = e.matmul(BB[bank][:, :], lhsT=W[:, k, wc0:wc0 + 128], rhs=XT[buf][:, k, :],
                               start=(k == 0), stop=(k == 7))
            return ins
        P.op("pe", fn, r=[f"XT{buf}", f"W{wc0 // 512}"], w=[f"B{bank}"], est=1.8)
        P.op("act", lambda e: e.activation(out=dst_fn, in_=BB[bank][:, :], func=AF.Identity,
                                           bias=b_qk[:, bias_col:bias_col + 1]), r=[f"B{bank}", "CST"], w=[keyw])

    def a1_group(g, xbuf=1):
        if g > 0:
            load_xt(xbuf, g)
        for m4 in range(4):
            proj_t(7, xbuf, m4 * 128, KT[:, m4, g * 512:(g + 1) * 512], 4 + m4, f"KT{g}")
        for tl in range(4):
            T = g * 4 + tl
            if T < 2 or T > 21:
                continue
            proj_n(2, xbuf, tl, 2, 0)
            P.op("dve", lambda e, T=T: e.tensor_copy(out=VA[:, T * 8:(T + 1) * 8, 0:64], in_=B[2][:, :].rearrange("p (h c) -> p h c", h=8)),
                 r=["B2"], w=[f"VA{T}"])
    for g in range(3):
        a1_group(g, 1 - g % 2)

    w_out_v = d_wout.rearrange("(k p) c -> p k c", p=128)
    hcount = [0]
    ringc = [0]
    var_js = {0: list(range(-2, 4)), 1: list(range(-3, 4)), 14: list(range(-3, 4)), 15: list(range(-3, 3))}
    for g in range(1, 5):
        buf = 0
        load_xt(0, g)
        for m4 in range(4):
            proj_t(m4 % 2, buf, 1536 + m4 * 128, QTG[:, m4, :], m4, "QTG")
        for tl in range(4):
            proj_n(2, buf, tl, 1, 1)
            P.op("act", lambda e: e.activation(out=SGTa[:, :], in_=B[2][:, :], func=AF.Exp, scale=-1.0), r=["B2"], w=["SGTa"])
            P.op("act", lambda e: e.activation(out=SGTa[:, :], in_=SGTa[:, :], func=AF.Ln, bias=ONE32a[:, 0:1]),
                 r=["SGTa", "ONE32a"], w=["SGTa"])
            P.op("act", lambda e: e.activation(out=SGTa[:, :], in_=SGTa[:, :], func=AF.Exp, scale=-1.0), r=["SGTa"], w=["SGTa"])
            P.op("dve", lambda e, tl=tl: e.tensor_tensor(out=SGA[:, tl, :], in0=B[2][:, :], in1=SGTa[:, :], op=ALU.mult),
                 r=["B2", "SGTa"], w=[f"SGA{tl}"])
        if g + 2 <= 5:
            a1_group(g + 2)
        if g == 3:
            load_w(0, w_out_v[:, :, 0:512], 512)
            load_w(1024, w_out_v[:, :, 512:1024], 512)
            load_bias(1024, 4608, 1024)
        for tl in range(4):
            T = g * 4 + tl
            m = T - T0
            if m in var_js:
                js = var_js[m]
                bi = {0: 0, 1: 1, 14: 2, 15: 3}[m]
                border, j0 = True, -3
            else:
                js = list(range(-2, 3))
                border, j0 = False, -2
            nj = len(js)
            for h in range(8):
                m4, half = h // 2, h % 2
                hc_ = hcount[0]
                hcount[0] += 1
                pb = hc_ % 3
                p0 = 64 * half
                sb0 = 3 if hc_ % 2 == 0 else 0
                PX32 = PX32s[pb]
                if border:
                    rs = ringc[0] % 3
                    ringc[0] += 1
                    sb = sidx[0] % 2
                    sidx[0] += 1
                    P.op("sp", lambda e, h=h, sb=sb, bi=bi: e.dma_start(out=STG[sb][:, :], in_=d_tabb[bi][h]),
                         w=[f"STG{sb}"], dma=f"stg{sb}")
                    P.op("act", lambda e, sb=sb, rs=rs: e.activation(out=EBH[rs][:, :], in_=STG[sb][:, :], func=AF.Exp),
                         r=[f"STG{sb}"], w=[f"EBH{rs}"])
                    tabv, tkey = EBH[rs], f"EBH{rs}"
                else:
                    tabv, tkey = EBI[:, h, :], "EBI"

                def smat(e, js=js, m4=m4, p0=p0, T=T, tl=tl, sb0=sb0):
                    ins = None
                    for idx, j in enumerate(js):
                        bk = B[sb0 + idx // 4]
                        ins = e.matmul(bk[:, (idx % 4) * 128:(idx % 4 + 1) * 128],
                                       lhsT=KT[p0:p0 + 64, m4, (T + j) * 128:(T + j + 1) * 128],
                                       rhs=QTG[p0:p0 + 64, m4, tl * 128:(tl + 1) * 128], start=True, stop=True)
                    return ins
                P.op("pe", smat, r=["QTG"] + [f"KT{(T + j) // 4}" for j in js], w=[f"B{sb0}", f"B{sb0 + 1}"], est=0.5)
                n0 = min(nj, 4)
                P.op("act", lambda e, n0=n0, PX32=PX32, sb0=sb0: e.activation(out=PX32[:, 0:n0 * 128], in_=B[sb0][:, 0:n0 * 128],
                                                                             func=AF.Exp, scale=0.125),
                     r=[f"B{sb0}"], w=[f"PXa{pb}"])
                if nj > 4:
                    P.op("act", lambda e, nj=nj, PX32=PX32, sb0=sb0: e.activation(
                        out=PX32[:, 512:512 + (nj - 4) * 128], in_=B[sb0 + 1][:, 0:(nj - 4) * 128],
                        func=AF.Exp, scale=0.125), r=[f"B{sb0 + 1}"], w=[f"PXb{pb}"])
                c0 = (js[0] - j0) * 128
                P.op("dve", lambda e, pb=pb, nj=nj, c0=c0, tabv=tabv, PX32=PX32: e.tensor_tensor(
                    out=P16[pb][:, 0:nj * 128], in0=PX32[:, 0:nj * 128], in1=tabv[:, c0:c0 + nj * 128], op=ALU.mult),
                    r=[f"PXa{pb}", f"PXb{pb}", tkey], w=[f"P16{pb}"], est=0.95)

                def pv(e, js=js, pb=pb, h=h, T=T):
                    ins = None
                    bk = B[5 + h // 4]
                    hc = (h % 4) * 65
                    for idx, j in enumerate(js):
                        ins = e.matmul(bk[:, hc:hc + 65], lhsT=P16[pb][:, idx * 128:(idx + 1) * 128],
                                       rhs=VA[:, (T + j) * 8 + h, :], start=(idx == 0), stop=(idx == len(js) - 1))
                    return ins
                P.op("pe", pv, r=[f"P16{pb}", "VA1"] + [f"VA{T + j}" for j in js], w=[f"B{5 + h // 4}"], est=0.45)
            for hb in range(2):
                P.op("dve", lambda e, hb=hb: e.reciprocal(out=RDEN[:, 4 * hb:4 * hb + 4].unsqueeze(2),
                                                          in_=B[5 + hb][:, 0:260].rearrange("p (h c) -> p h c", h=4)[:, :, 64:65]),
                     r=[f"B{5 + hb}"], w=["RDEN"])
                P.op("dve", lambda e, hb=hb: e.tensor_tensor(
                    out=OA32[:, hb * 256:(hb + 1) * 256].rearrange("p (h c) -> p h c", h=4),
                    in0=B[5 + hb][:, 0:260].rearrange("p (h c) -> p h c", h=4)[:, :, 0:64],
                    in1=RDEN[:, 4 * hb:4 * hb + 4].unsqueeze(2).broadcast_to([128, 4, 64]), op=ALU.mult),
                    r=[f"B{5 + hb}", "RDEN"], w=[f"OA32{hb}"])
            P.op("pool", lambda e, m=m, tl=tl: e.tensor_tensor(out=OA16[:, m, :], in0=OA32[:, :], in1=SGA[:, tl, :], op=ALU.mult),
                 r=["OA320", "OA321", f"SGA{tl}"], w=[f"OA{m}"])

    P.op("sp", lambda e: e.dma_start(out=LNP[:, :], in_=d_lnp[:, :]), w=["XT1", "LNP"], dma="lnp")
    for m in range(2):
        P.op("sp", lambda e, m=m: e.dma_start(out=XN32[m][:, :], in_=d_xN[m * 128:(m + 1) * 128, :]),
             w=["XT0", f"XN{m}"] if m == 0 else [f"XN{m}"], r=[] if m == 0 else ["XT0"], dma=f"xn{m}")
    if upto == 3:
        P.emit([])
        return nc
    P.barrier()
    A.reset(phase_mark)

    ND = 4
    R32s = [A(f"r32{i}", [128, 1024], F32) for i in range(ND)]
    RN32s = [A(f"rn32{i}", [128, 1024], F32) for i in range(ND)]
    JKa = [A(f"jka{i}", [128, 1024], F32) for i in range(ND)]
    JKb = [A(f"jkb{i}", [128, 1024], F32) for i in range(ND)]
    OT16s = [A(f"ot16{i}", [128, 8, 128], BF16) for i in range(ND)]
    STATs = [A(f"stat{i}", [128, 8], F32) for i in range(ND)]
    XNs = [XN32[0], XN32[1]] + [A(f"xn{i}", [128, 1024], F32) for i in range(2, ND)]
    EPSC = A("epsc", [128, 8], F32)
    P.op("dve", lambda e: e.memset(EPSC[:, :], LN_EPS), w=["EPSC"])

    final = []
    def c_tile(m):
        xb = m % ND
        R32, RN32, OT16, STAT, XN, Ja, Jb = R32s[xb], RN32s[xb], OT16s[xb], STATs[xb], XNs[xb], JKa[xb], JKb[xb]
        sx = str(xb)
        yb = (0, 1) if m % 2 == 0 else (3, 4)
        tb = (7, 2) if m % 2 == 0 else (5, 6)
        if m >= 2:
            P.op("sp", lambda e: e.dma_start(out=XN[:, :], in_=d_xN[m * 128:(m + 1) * 128, :]),
                 w=[f"XN{xb}"], dma=f"xn{xb}")
        if debug and debug != 3:
            P.op("pool", lambda e: e.dma_start(out=d_dbg[m * 128:(m + 1) * 128, 0:512], in_=OA16[:, m, :]),
                 r=[f"OA{m}"], dma="dbg")
            if debug not in (2, 3):
                P.op("pool", lambda e: e.dma_start(out=d_dbg[m * 128:(m + 1) * 128, 512:1024], in_=OH16[:, m, :]),
                     r=[f"OH{m}"], dma="dbg")

        def trc(e):
            ins = None
            for k in range(8):
                src = OA16[:, m, k * 128:(k + 1) * 128] if k < 4 else OH16[:, m, (k - 4) * 128:(k - 3) * 128]
                dst = BB[tb[0]][:, k * 128:(k + 1) * 128] if k < 4 else BB[tb[1]][:, (k - 4) * 128:(k - 3) * 128]
                ins = e.matmul(dst, lhsT=src, rhs=IDENT16[:, :], start=True, stop=True)
            return ins
        P.op("pe", trc, r=[f"OA{m}", f"OH{m}", "IDENT16"], w=[f"B{tb[0]}", f"B{tb[1]}"], est=0.5)
        P.op("act", lambda e: e.activation(out=OT16[:, 0:4, :], in_=BB[tb[0]][:, :].rearrange("p (k t) -> p k t", k=4), func=AF.Identity),
             r=[f"B{tb[0]}"], w=["OT16a_" + sx])
        P.op("dve", lambda e: e.tensor_copy(out=OT16[:, 4:8, :], in_=BB[tb[1]][:, :].rearrange("p (k t) -> p k t", k=4)),
             r=[f"B{tb[1]}"], w=["OT16b_" + sx])
        for hf in range(2):
            def ymat(e, hf=hf):
                ins = None
                for k in range(8):
                    ins = e.matmul(BB[yb[hf]][:, :], lhsT=OT16[:, k, :], rhs=W[:, k, hf * 1024:hf * 1024 + 512],
                                   start=(k == 0), stop=False)
                ins = e.matmul(BB[yb[hf]][:, :], lhsT=ONES16[0:1, :], rhs=BIAS16[0:1, 1024 + hf * 512:1024 + (hf + 1) * 512],
                               start=False, stop=True)
                return ins
            P.op("pe", ymat, r=["OT16a_" + sx, "OT16b_" + sx, f"W{2 * hf}", f"BIAS{2 + hf}", "ONES16"], w=[f"B{yb[hf]}"], est=2.1)
            P.op("dve", lambda e, hf=hf: e.scalar_tensor_tensor(
                out=R32[:, hf * 512:(hf + 1) * 512], in0=XN[:, hf * 512:(hf + 1) * 512], scalar=ALPHA,
                in1=BB[yb[hf]][:, :], op0=ALU.mult, op1=ALU.add), r=[f"XN{xb}", f"B{yb[hf]}"], w=[f"R32{hf}_" + sx])
        P.op("pool", lambda e: e.memset(STAT[:, 0:2], 0.0), w=["STAT0_" + sx, "STAT1_" + sx], est=0.2)
        P.op("act", lambda e: e.activation(out=Ja[:, :], in_=R32[:, :], func=AF.Identity, accum_out=STAT[:, 0:1]),
             r=["R320_" + sx, "R321_" + sx], w=["JKa_" + sx, "STAT0_" + sx], est=1.2)
        P.op("act", lambda e: e.activation(out=Jb[:, :], in_=R32[:, :], func=AF.Square, accum_out=STAT[:, 1:2]),
             r=["R320_" + sx, "R321_" + sx], w=["JKb_" + sx, "STAT1_" + sx], est=1.2)
        P.op("dve", lambda e: e.tensor_scalar(out=STAT[:, 2:4], in0=STAT[:, 0:2], scalar1=1.0 / 1024.0, scalar2=None, op0=ALU.mult),
             r=["STAT0_" + sx, "STAT1_" + sx], w=["STAT2_" + sx], est=0.15)
        P.op("dve", lambda e: e.tensor_scalar(out=STAT[:, 4:5], in0=STAT[:, 2:3], scalar1=STAT[:, 2:3], scalar2=STAT[:, 3:4],
                                              op0=ALU.mult, op1=ALU.subtract), r=["STAT2_" + sx], w=["STAT4_" + sx], est=0.15)
        P.op("act", lambda e: e.activation(out=STAT[:, 5:6], in_=STAT[:, 4:5], func=AF.Ln, scale=-1.0, bias=EPSC[:, 0:1]),
             r=["STAT4_" + sx, "EPSC"], w=["STAT5_" + sx], est=0.15)
        P.op("act", lambda e: e.activation(out=STAT[:, 5:6], in_=STAT[:, 5:6], func=AF.Exp, scale=-0.5), r=["STAT5_" + sx], w=["STAT5_" + sx], est=0.15)
        P.op("dve", lambda e: e.scalar_tensor_tensor(out=STAT[:, 6:7], in0=STAT[:, 2:3], scalar=-1.0, in1=STAT[:, 5:6],
                                                      op0=ALU.mult, op1=ALU.mult), r=["STAT2_" + sx, "STAT5_" + sx], w=["STAT6_" + sx], est=0.15)
        P.op("act", lambda e: e.activation(out=RN32[:, :], in_=R32[:, :], func=AF.Identity, scale=STAT[:, 5:6], bias=STAT[:, 6:7]),
             r=["R320_" + sx, "R321_" + sx, "STAT5_" + sx, "STAT6_" + sx], w=["RN32_" + sx], est=1.2)
        P.op("dve", lambda e: e.tensor_tensor(out=RN32[:, :], in0=RN32[:, :], in1=LNP[:, 0:1024], op=ALU.mult),
             r=["RN32_" + sx, "LNP"], w=["RN32_" + sx], est=1.4)
        P.op("pool", lambda e: e.tensor_tensor(out=RN32[:, :], in0=RN32[:, :], in1=LNP[:, 1024:2048], op=ALU.add),
             r=["RN32_" + sx, "LNP"], w=["RN32_" + sx], est=2.5)
        tok = P.op("sp", lambda e: e.dma_start(out=d_out[m * 128:(m + 1) * 128, :], in_=RN32[:, :]),
                   r=["RN32_" + sx], dma=f"st{xb}")
        final.append(tok)
    for m in range(16):
        c_tile(m)
    if debug:
        final.extend(o for seg in P.segs for o in seg if o.dma == "dbg")
    P.emit(final)
    return nc


def _const_block():
    c = np.zeros((128, NCST), np.float32)
    c[:, C_ID:C_ID + 128] = np.eye(128, dtype=np.float32)
    s = np.arange(128)[:, None]
    t = np.arange(128)[None, :]
    same = (s // 64) == (t // 64)
    c[:, C_UF:C_UF + 128] = (same & (s <= t)).astype(np.float32)
    c[:, C_RF:C_RF + 128] = (same & (s > t)).astype(np.float32)
    c[:, C_UB:C_UB + 128] = (same & (s >= t)).astype(np.float32)
    c[:, C_RB:C_RB + 128] = (same & (s < t)).astype(np.float32)
    c[:, C_IND + 0] = (np.arange(128) < 64).astype(np.float32)
    c[:, C_IND + 1] = (np.arange(128) >= 64).astype(np.float32)
    return c


def _att_table(rpb, R0, m, js):
    p = np.arange(128)
    kr_i, kc = p // 64, p % 64
    qr_i, qc = p // 64, p % 64
    out = np.full((8, 128, len(js), 128), NEG, np.float32)
    qr = R0 + 2 * m + qr_i
    rs = np.clip(qr - 4, 0, 120)
    cs = np.clip(qc - 8, 0, 48)
    for ji, j in enumerate(js):
        kr = R0 + 2 * m + 2 * j + kr_i
        ok = (kr[:, None] >= 0) & (kr[:, None] < 128) & (kr[:, None] >= rs[None, :]) & (kr[:, None] < rs[None, :] + 8) \
            & (kc[:, None] >= cs[None, :]) & (kc[:, None] < cs[None, :] + 16)
        dr = np.clip(kr[:, None] - qr[None, :] + 7, 0, 14)
        dc = np.clip(kc[:, None] - qc[None, :], -15, 15) + 15
        g = rpb[:, dr, dc]
        out[:, :, ji, :] = np.where(ok[None], g, NEG)
    return out


_NC_CACHE = {}


def _prep_inputs(x, w_in, b_in, rpb, lb_fwd_logits, lb_bwd_logits, hg_norm_gain, w_out, b_out, ln_gain, ln_bias):
    x = np.asarray(x, np.float32)
    w_in0 = np.ascontiguousarray(np.asarray(w_in, np.float32)[0])
    w_out0 = np.ascontiguousarray(np.asarray(w_out, np.float32)[0])
    b_in0 = np.asarray(b_in, np.float32)[0]
    rpb0 = np.asarray(rpb, np.float32)[0]
    brow = np.concatenate([b_in0, np.asarray(b_out, np.float32)[0]])[None, :].astype(np.float32)
    lbf = np.asarray(lb_fwd_logits, np.float32)
    lbb = np.asarray(lb_bwd_logits, np.float32)
    lbl = np.ascontiguousarray(np.broadcast_to(
        np.concatenate([lbf[0], lbf[1], lbb[0], lbb[1], b_in0[2048:2560], b_in0[3584:4096]])[None, :], (128, 3072))).astype(np.float32)
    lnp = np.ascontiguousarray(np.broadcast_to(
        np.concatenate([np.asarray(ln_gain, np.float32)[0], np.asarray(ln_bias, np.float32)[0]])[None, :], (128, 2048))).astype(np.float32)
    cbase = _const_block()
    cbase[:, C_GAIN:C_GAIN + 512] = np.asarray(hg_norm_gain, np.float32)[0][None, :]
    cbase[:, C_BQK:C_BQK + 4] = b_in0[0:512].reshape(4, 128).T
    cbase[:, C_BQK + 4:C_BQK + 8] = b_in0[512:1024].reshape(4, 128).T
    in_maps = []
    for c in range(NCORES):
        b, s = c // 4, c % 4
        t0 = s * NLOC
        lo, hi = max(0, t0 - HALO), min(SEQ, t0 + NLOC + HALO)
        xT = np.zeros((DM, NTOK), np.float32)
        xT[:, lo - (t0 - HALO):hi - (t0 - HALO)] = x[b, lo:hi, :].T
        cst = cbase.copy()
        tpos = t0 - HALO + np.arange(NT)[None, :] * 128 + np.arange(128)[:, None]
        cst[:, C_TM:C_TM + NT] = ((tpos >= 0) & (tpos < SEQ)).astype(np.float32)
        R0 = 32 * s
        tabi = _att_table(rpb0, R0, 7, list(range(-2, 3))).reshape(8, 128, 5 * 128)
        tabb = np.stack([_att_table(rpb0, R0, m, list(range(-3, 4))).reshape(8, 128, 7 * 128) for m in (0, 1, 14, 15)])
        in_maps.append({
            "xT": xT, "xN": np.ascontiguousarray(x[b, t0:t0 + NLOC, :]), "w_in": w_in0, "w_out": w_out0,
            "brow": brow, "cst": cst, "lbl": lbl, "lnp": lnp,
            "tabi": np.ascontiguousarray(tabi), "tabb": np.ascontiguousarray(tabb),
        })
    return in_maps


def kernel(x, w_in, b_in, rpb, lb_fwd_logits, lb_bwd_logits, hg_norm_gain, w_out, b_out, ln_gain, ln_bias):
    in_maps = _prep_inputs(x, w_in, b_in, rpb, lb_fwd_logits, lb_bwd_logits, hg_norm_gain, w_out, b_out, ln_gain, ln_bias)
    if "nc" not in _NC_CACHE:
        _NC_CACHE["nc"] = build_nc()
    res = run_bass_kernel_spmd(_NC_CACHE["nc"], in_maps, core_ids=list(range(NCORES)))
    out = np.zeros((2, SEQ, DM), np.float32)
    for c in range(NCORES):
        b, s = c // 4, c % 4
        out[b, s * NLOC:(s + 1) * NLOC, :] = res.results[c]["out"]
    return out
```

```python
import numpy as np
from contextlib import ExitStack
import concourse.bass as bass
import concourse.mybir as mybir
from concourse.bass_utils import run_bass_kernel_spmd

F32 = mybir.dt.float32
BF16 = mybir.dt.bfloat16
AF = mybir.ActivationFunctionType
ALU = mybir.AluOpType

NCORES = 8
SEQ = 8192
DM = 1024
HALO = 512
NLOC = 2048
NTOK = NLOC + 2 * HALO
NT = NTOK // 128
T0 = HALO // 128
T1 = T0 + NLOC // 128
ALPHA = 2.0 ** 0.25
LN_EPS = 1e-5
RMS_EPS = 1e-6
NEG = -30000.0
SKIP = set()

C_ID, C_UF, C_RF, C_UB, C_RB, C_IND = 0, 128, 256, 384, 512, 640
C_GAIN = 648
C_TM = C_GAIN + 512
C_BQK = C_TM + 24
NCST = C_BQK + 8


class _Op:
    __slots__ = ("e", "fn", "deps", "dma", "est", "lat", "idx", "tok", "start", "end", "wk")

    def __init__(self, e, fn, deps, dma, est, lat, idx):
        self.e, self.fn, self.deps, self.dma, self.est, self.lat, self.idx = e, fn, deps, dma, est, lat, idx
        self.tok = None
        self.start = self.end = 0.0


class Prog:
    EST = {"pe": 0.5, "act": 0.62, "dve": 0.7, "pool": 1.3, "sp": 0.1}
    HOP = 0.4
    SLACK = 0.3

    def __init__(self, nc):
        self.nc = nc
        self.eng = {"pe": nc.tensor, "act": nc.scalar, "dve": nc.vector, "pool": nc.gpsimd, "sp": nc.sync}
        self.segs = [[]]
        self.last_w = {}
        self.readers = {}
        self.n = 0
    mute = False

    def op(self, e, fn, r=(), w=(), dma=None, est=None):
        if self.mute:
            return None
        deps = []
        seen = set()

        def add(o):
            if o is not None and id(o) not in seen:
                seen.add(id(o))
                deps.append(o)
        for k in r:
            add(self.last_w.get(k))
        for k in w:
            add(self.last_w.get(k))
            for t in self.readers.get(k, ()):
                add(t)
        if dma is not None:
            lat = est if est is not None else (6.0 if e == "pool" else 3.0)
            dur = 1.0 if e == "pool" else 0.1
        else:
            dur = est if est is not None else self.EST[e]
            lat = dur
        o = _Op(e, fn, deps, dma, dur, lat, self.n)
        o.wk = tuple(w)
        self.n += 1
        self.segs[-1].append(o)
        for k in r:
            self.readers.setdefault(k, []).append(o)
        for k in w:
            self.last_w[k] = o
            self.readers[k] = []
        return o

    def barrier(self):
        self.segs.append([])

    def _schedule(self, seg):
        inseg = {id(o) for o in seg}
        ndep = {}
        users = {}
        for o in seg:
            c = 0
            for d_ in o.deps:
                if id(d_) in inseg:
                    c += 1
                    users.setdefault(id(d_), []).append(o)
            ndep[id(o)] = c
        tail = {}
        for o in sorted(seg, key=lambda o_: -o_.idx):
            t = 0.0
            for u in users.get(id(o), ()):
                t = max(t, tail[id(u)] + (0.0 if (u.e == o.e and o.dma is None) else self.HOP))
            tail[id(o)] = t + o.lat
        free = {e: 0.0 for e in self.eng}
        ready = {}
        cand = [o for o in seg if ndep[id(o)] == 0]
        for o in cand:
            ready[id(o)] = 0.0
        order = {e: [] for e in self.eng}
        done = 0
        while cand:
            tmin = min(max(free[o.e], ready[id(o)]) for o in cand)
            best, bkey = None, None
            for o in cand:
                st = max(free[o.e], ready[id(o)])
                if st > tmin + self.SLACK:
                    continue
                key = (-tail[id(o)], o.idx)
                if bkey is None or key < bkey:
                    best, bkey = o, key
            o = best
            cand.remove(o)
            o.start = max(free[o.e], ready[id(o)])
            free[o.e] = o.start + o.est
            o.end = o.start + o.lat
            order[o.e].append(o)
            done += 1
            for u in users.get(id(o), ()):
                t = o.end + (0.0 if (u.e == o.e and o.dma is None) else self.HOP)
                if t > ready.get(id(u), 0.0):
                    ready[id(u)] = t
                ndep[id(u)] -= 1
                if ndep[id(u)] == 0:
                    cand.append(u)
        assert done == len(seg), (done, len(seg))
        return order

    def emit(self, final_ops):
        nc = self.nc
        plan = {e: [] for e in self.eng}
        count = {e: 0 for e in self.eng}
        dma_cnt = {}
        waited = {e: {} for e in self.eng}
        semnames = set()
        for seg in self.segs:
            order = self._schedule(seg)
            allops = sorted(seg, key=lambda o: (o.start, o.idx))
            for o in allops:
                if o.dma is None:
                    count[o.e] += 1
                    o.tok = ("E_" + o.e, count[o.e], o.e)
                else:
                    dma_cnt[o.dma] = dma_cnt.get(o.dma, 0) + 1
                    o.tok = ("D_" + o.dma, 16 * dma_cnt[o.dma], None)
                semnames.add(o.tok[0])
            for e in self.eng:
                c0 = count[e] - sum(1 for o in order[e] if o.dma is None)
                for o in order[e]:
                    if o.dma is None:
                        c0 += 1
                        o.tok = ("E_" + e, c0, e)
            for e in self.eng:
                for o in order[e]:
                    need = {}
                    for d_ in o.deps:
                        s_, v, se = d_.tok
                        if se == e and e == "pe" and o.dma is None:
                            continue
                        if v > need.get(s_, 0):
                            need[s_] = v
                    waits = []
                    for s_, v in need.items():
                        if waited[e].get(s_, 0) >= v:
                            continue
                        waited[e][s_] = v
                        waits.append((s_, v))
                    plan[e].append((waits, o.fn, o.tok))
            toks = [("E_" + e, count[e]) for e in self.eng if count[e] > 0] + [("D_" + k, 16 * c) for k, c in dma_cnt.items()]
            for e in self.eng:
                waits = []
                for s_, v in toks:
                    if (s_ == "E_" + e and e == "pe") or waited[e].get(s_, 0) >= v:
                        continue
                    waited[e][s_] = v
                    waits.append((s_, v))
                if waits:
                    plan[e].append((waits, None, None))
        final_tokens = [o.tok for o in final_ops]
        with ExitStack() as es:
            sems = {n_: es.enter_context(nc.semaphore(n_)) for n_ in sorted(semnames)}
            block = es.enter_context(nc.Block())

            def run(e, eng):
                for waits, fn, tok in plan[e]:
                    for s_, v in waits:
                        eng.wait_ge(sems[s_], v)
                    if fn is None:
                        continue
                    ins = fn(eng)
                    ins.then_inc(sems[tok[0]], 16 if tok[2] is None else 1)
                if e == "sp":
                    for s_, v, _ in final_tokens:
                        eng.wait_ge(sems[s_], v)

            @block.tensor
            def _(eng):
                run("pe", eng)

            @block.scalar
            def _(eng):
                run("act", eng)

            @block.vector
            def _(eng):
                run("dve", eng)

            @block.gpsimd
            def _(eng):
                run("pool", eng)

            @block.sync
            def _(eng):
                run("sp", eng)


class Alloc:
    def __init__(self, nc, base, size):
        self.nc, self.base, self.size, self.off, self.n = nc, base, size, 0, 0

    def mark(self):
        return self.off

    def reset(self, m):
        self.off = m

    def __call__(self, name, shape, dtype):
        nb = int(np.prod(shape[1:])) * (4 if dtype == F32 else 2)
        nb = (nb + 63) // 64 * 64
        assert self.off + nb <= self.size, (name, self.off, nb, self.size)
        self.n += 1
        t = self.nc.alloc_sbuf_tensor_at(f"{name}_{self.n}", list(shape), dtype, offset=self.base + self.off)
        self.off += nb
        return t


def build_nc(debug=False, upto=9, ntl=99):
    nc = bass.Bass("TRN2", target_bir_lowering=False)
    d_xT = nc.dram_tensor("xT", [DM, NTOK], F32, kind="ExternalInput").ap()
    d_xN = nc.dram_tensor("xN", [NLOC, DM], F32, kind="ExternalInput").ap()
    d_win = nc.dram_tensor("w_in", [DM, 4608], F32, kind="ExternalInput").ap()
    d_wout = nc.dram_tensor("w_out", [DM, DM], F32, kind="ExternalInput").ap()
    d_brow = nc.dram_tensor("brow", [1, 4608 + 1024], F32, kind="ExternalInput").ap()
    d_cst = nc.dram_tensor("cst", [128, NCST], F32, kind="ExternalInput").ap()
    d_lbl = nc.dram_tensor("lbl", [128, 6 * 512], F32, kind="ExternalInput").ap()
    d_lnp = nc.dram_tensor("lnp", [128, 2 * 1024], F32, kind="ExternalInput").ap()
    d_tabi = nc.dram_tensor("tabi", [8, 128, 5 * 128], F32, kind="ExternalInput").ap()
    d_tabb = nc.dram_tensor("tabb", [4, 8, 128, 7 * 128], F32, kind="ExternalInput").ap()
    d_out = nc.dram_tensor("out", [NLOC, DM], F32, kind="ExternalOutput").ap()
    d_dbg = None
    if debug:
        d_dbg = nc.dram_tensor("dbg", [NLOC, 1024], F32, kind="ExternalOutput").ap()

    arena = nc.alloc_sbuf_tensor("arena", [128, 207 * 1024], mybir.dt.uint8)
    base = nc.lookup_mloc(arena).addr
    A = Alloc(nc, base, 207 * 1024)
    P = Prog(nc)

    B = [nc.alloc_psum_tensor(f"pb{i}", [128, 512], F32) for i in range(7)]
    B7 = nc.alloc_psum_tensor("pb7", [128, 512], F32)
    BB = B + [B7]

    CST = A("cst", [128, NCST], F32)
    BIAS16 = A("bias16", [1, 2048], BF16)
    ONES16 = A("ones16", [1, 128], BF16)
    IDENT16 = A("ident16", [128, 128], BF16)
    W = A("w", [128, 8, 2048], BF16)
    _xt_off = A.mark()
    XT = [A(f"xt{i}", [128, 8, 512], BF16) for i in range(2)]
    LNP = nc.alloc_sbuf_tensor_at("lnp_alias", [128, 2048], F32, offset=A.base + _xt_off + 8192)
    XN32 = [nc.alloc_sbuf_tensor_at(f"xn_alias{i}", [128, 1024], F32, offset=A.base + _xt_off + 4096 * i) for i in range(2)]
    OH16 = A("oh16", [128, 16, 512], BF16)
    OA16 = A("oa16", [128, 16, 512], BF16)
    phase_mark = A.mark()

    w_in_v = d_win.rearrange("(k p) c -> p k c", p=128)
    xT_v = d_xT.rearrange("(k p) t -> p k t", p=128)

    def load_w(dst_c0, src_ap, ncols):
        P.op("pool", lambda e: e.dma_start(out=W[:, :, dst_c0:dst_c0 + ncols], in_=src_ap),
             w=[f"W{dst_c0 // 512}" for _ in range(1)] if ncols == 512 else [f"W{i}" for i in range(dst_c0 // 512, (dst_c0 + ncols) // 512)],
             dma=f"w{dst_c0 // 512}")

    def load_bias(dst_c0, src_c0, ncols):
        P.op("pool", lambda e: e.dma_start(out=BIAS16[0:1, dst_c0:dst_c0 + ncols], in_=d_brow[0:1, src_c0:src_c0 + ncols]),
             w=[f"BIAS{dst_c0 // 512 + i}" for i in range(ncols // 512)], dma=f"bias{dst_c0 // 512}")

    def load_xt(buf, g):
        P.op("pool", lambda e: e.dma_start(out=XT[buf][:, :, :], in_=xT_v[:, :, g * 512:(g + 1) * 512]),
             w=[f"XT{buf}"], dma=f"xt{buf}")

    def proj_n(bank, buf, tl, wslot, bias_slot):
        def fn(e):
            ins = None
            for k in range(8):
                ins = e.matmul(BB[bank][:, :], lhsT=XT[buf][:, k, tl * 128:(tl + 1) * 128],
                               rhs=W[:, k, wslot * 512:(wslot + 1) * 512], start=(k == 0), stop=(bias_slot is None and k == 7))
            if bias_slot is not None:
                ins = e.matmul(BB[bank][:, :], lhsT=ONES16[0:1, :], rhs=BIAS16[0:1, bias_slot * 512:(bias_slot + 1) * 512],
                               start=False, stop=True)
            return ins
        rk = [f"XT{buf}", f"W{wslot}"] + ([f"BIAS{bias_slot}", "ONES16"] if bias_slot is not None else [])
        P.op("pe", fn, r=rk, w=[f"B{bank}"], est=2.1 if bias_slot is not None else 1.8)

    P.op("sp", lambda e: e.dma_start(out=CST[:, :], in_=d_cst[:, :]), w=["CST"], dma="cst")
    P.op("dve", lambda e: e.memset(ONES16[:, :], 1.0), w=["ONES16"])
    P.op("dve", lambda e: e.tensor_copy(out=IDENT16[:, :], in_=CST[:, C_ID:C_ID + 128]), r=["CST"], w=["IDENT16"])

    LB = [A(f"lb{d}", [128, 512], F32) for d in range(2)]
    OML = [A(f"oml{d}", [128, 512], F32) for d in range(2)]
    _of_off = A.mark()
    OF32 = A("of32", [128, 16, 512], F32)
    LBL = nc.alloc_sbuf_tensor_at("lbl_alias", [128, 2048], F32, offset=A.base + _of_off)
    G2s = [A(f"g2{i}", [128, 512], F32) for i in range(2)]
    SGT = A("sgt", [128, 512], F32)
    ONE32 = A("one32", [128, 8], F32)
    QST = A("qst", [128, 16, 512], BF16)
    SGc = [A(f"sgc{i}", [128, 512], BF16) for i in range(3)]
    A32 = A("a32", [128, 512], F32)
    K32 = A("k32", [128, 512], F32)
    EB32 = A("eb32", [128, 512], F32)
    ENB32 = A("enb32", [128, 512], F32)
    QE16 = A("qe16", [128, 512], BF16)
    KE16 = A("ke16", [128, 512], BF16)
    V16 = A("v16", [128, 512], BF16)
    QET16 = A("qet16", [128, 512], BF16)
    KET16 = A("ket16", [128, 512], BF16)
    AT16 = A("at16", [128, 512], BF16)
    ST32 = A("st32", [128, 512], F32)
    S16 = A("s16", [128, 512], BF16)
    DEC = A("dec", [128, 8], F32)
    O32 = [A(f"o32{i}", [128, 512], F32) for i in range(2)]
    ON32 = [A(f"on32{i}", [128, 512], F32) for i in range(2)]
    BQ, BI = O32[0], ON32[0]
    KBQ, KBI = ["O320"], [f"ON320{h_}" for h_ in range(4)]
    SS = [A(f"ss{i}", [128, 4], F32) for i in range(2)]
    RSTD = [A(f"rstd{i}", [128, 4], F32) for i in range(2)]

    P.op("sp", lambda e: e.dma_start(out=LBL[:, :], in_=d_lbl[:, 0:2048]), w=["LBL"], dma="lbl")
    P.op("dve", lambda e: e.memset(ONE32[:, :], 1.0), w=["ONE32"])
    P.op("sp", lambda e: e.dma_start(out=BQ[:, :], in_=d_lbl[:, 2048:2560]), w=KBQ, dma="bq")
    P.op("sp", lambda e: e.dma_start(out=BI[:, :], in_=d_lbl[:, 2560:3072]), w=KBI, dma="bi")
    for d in range(2):
        o = d * 1024
        P.op("dve", lambda e, o=o: e.tensor_sub(out=LBL[:, o:o + 512], in0=LBL[:, o:o + 512], in1=LBL[:, o + 512:o + 1024]),
             r=["LBL"], w=["LBL"])
        P.op("act", lambda e, o=o, d=d: e.activation(out=LB[d][:, :], in_=LBL[:, o:o + 512], func=AF.Sigmoid),
             r=["LBL"], w=[f"LB{d}"])
        P.op("dve", lambda e, d=d: e.tensor_scalar(out=OML[d][:, :], in0=LB[d][:, :], scalar1=-1.0, scalar2=1.0,
                                                    op0=ALU.mult, op1=ALU.add), r=[f"LB{d}"], w=[f"OML{d}"])

    if upto == 0:
        P.emit([])
        return nc

    A32 = [A32, A("a32b", [128, 512], F32)]
    K32 = [K32, A("k32b", [128, 512], F32)]
    V16 = [V16, A("v16b", [128, 512], BF16), A("v16c", [128, 512], BF16)]
    Q32 = [None, None]
    QET16 = [QET16, A("qet16b", [128, 512], BF16)]
    AT16 = [AT16, A("at16b", [128, 512], BF16)]
    DEC = [DEC, A("decb", [128, 8], F32), A("decc", [128, 8], F32)]
    EB32 = [EB32, A("eb32b", [128, 512], F32)]
    ENB32 = [ENB32, A("enb32b", [128, 512], F32)]
    QE16 = [QE16, A("qe16b", [128, 512], BF16)]
    KE16 = [KE16, A("ke16b", [128, 512], BF16)]
    KET16 = [KET16, A("ket16b", [128, 512], BF16)]
    T1s = [A(f"t1{i}", [128, 512], F32) for i in range(2)]

    def hg_pass(d):
        zc = 2560 if d == 0 else 3072
        tiles = list(range(0, T1)) if d == 0 else list(range(NT - 1, T0 - 1, -1))
        tiles = tiles[:ntl]
        gorder = []
        for T in tiles:
            if T // 4 not in gorder:
                gorder.append(T // 4)
        gbuf = {g: i % 2 for i, g in enumerate(gorder)}
        load_xt(gbuf[gorder[0]], gorder[0])
        if d == 0:
            load_w(512, w_in_v[:, :, zc:zc + 512], 512)
            load_bias(512, zc, 512)
            load_w(1024, w_in_v[:, :, 3584:4096], 512)
            load_bias(1024, 3584, 512)
            load_w(0, w_in_v[:, :, 2048:2560], 512)
            load_bias(0, 2048, 512)
            load_w(1536, w_in_v[:, :, 4096:4608], 512)
            load_bias(1536, 4096, 512)
        else:
            load_w(512, w_in_v[:, :, zc:zc + 512], 512)
            load_bias(512, zc, 512)
            load_w(0, w_in_v[:, :, 512:1024], 512)
        P.op("dve", lambda e: e.memset(ST32[:, :], 0.0), w=[f"ST32h{h_}" for h_ in range(4)])
        P.op("dve", lambda e: e.memset(S16[:, :], 0.0), w=["S16"])
        UD = C_UF if d == 0 else C_UB
        RD = C_RF if d == 0 else C_RB
        corder = (0, 1) if d == 0 else (1, 0)
        first_tile = {g: [t for t in tiles if t // 4 == g][0] for g in gorder}
        KVB = {0: 6, 1: 3}
        prevdec = [None]

        def a_steps(i):
            T = tiles[i]
            pp, p3 = i % 2, i % 3
            g = T // 4
            buf = gbuf[g]
            tl = T % 4
            local = T0 <= T < T1
            li = T - T0
            a32, k32, v16, q32 = A32[pp], K32[pp], V16[p3], Q32[pp]
            ka, kk, kv_, kq = f"A32{pp}", f"K32{pp}", f"V16{p3}", f"Q32{pp}"

            def s1():
                gi = gorder.index(g)
                if T == first_tile[g] and gi + 1 < len(gorder):
                    load_xt(gbuf[gorder[gi + 1]], gorder[gi + 1])
                proj_n(0, buf, tl, 1, 1)
                P.op("act", lambda e: e.activation(out=a32[:, :], in_=B[0][:, :], func=AF.Exp, scale=-1.0), r=["B0"], w=[ka])

            def s2():
                if local and d == 1:
                    return
                if local:
                    proj_n(7, buf, tl, 2, None)
                    P.op("dve", lambda e: e.tensor_tensor(out=OA16[:, li, :], in0=B7[:, :], in1=BI[:, :], op=ALU.add),
                         r=["B7"] + KBI, w=[f"OAV{li}"])
                else:
                    proj_n(7, buf, tl, 2, 2)
                    P.op("act", lambda e: e.activation(out=v16[:, :], in_=B7[:, :], func=AF.Identity,
                                                       scale=CST[:, C_TM + T:C_TM + T + 1]), r=["B7", "CST"], w=[kv_])

            def s3():
                if local and d == 0:
                    proj_n(0, buf, tl, 0, None)
                    P.op("dve", lambda e: e.tensor_tensor(out=QST[:, li, :], in0=B[0][:, :], in1=BQ[:, :], op=ALU.add),
                         r=["B0"] + KBQ, w=[f"QST{li}"])

            def s4():
                if local and d == 1:
                    proj_n(7, buf, tl, 3, 3)
                    P.op("act", lambda e: e.activation(out=SGT[:, :], in_=B7[:, :], func=AF.Exp, scale=-1.0), r=["B7"], w=["SGT"])
                    P.op("act", lambda e: e.activation(out=SGT[:, :], in_=SGT[:, :], func=AF.Ln, bias=ONE32[:, 0:1]),
                         r=["SGT", "ONE32"], w=["SGT"])
                    P.op("act", lambda e: e.activation(out=SGT[:, :], in_=SGT[:, :], func=AF.Exp, scale=-1.0), r=["SGT"], w=["SGT"])
                    P.op("dve", lambda e: e.tensor_tensor(out=SGT[:, :], in0=B7[:, :], in1=SGT[:, :], op=ALU.mult),
                         r=["B7", "SGT"], w=["SGT"])
                    P.op("pool", lambda e: e.tensor_tensor(out=SGc[p3][:, :], in0=SGT[:, :], in1=CST[:, C_GAIN:C_GAIN + 512], op=ALU.mult),
                         r=["SGT", "CST"], w=[f"SGc{p3}"])

            def s5():
                G2, ER32B = G2s[pp], T1s[pp]
                kg2, kt1 = f"G2{pp}", f"T1{pp}"
                P.op("act", lambda e: e.activation(out=G2[:, :], in_=a32[:, :], func=AF.Ln, bias=ONE32[:, 0:1]),
                     r=[ka, "ONE32"], w=[kg2])
                P.op("pool" if d == 0 else "dve", lambda e: e.tensor_tensor(out=ER32B[:, :], in0=a32[:, :], in1=LB[d][:, :], op=ALU.mult),
                     r=[ka, f"LB{d}"], w=[kt1])
                P.op("act", lambda e: e.activation(out=ER32B[:, :], in_=ER32B[:, :], func=AF.Ln, bias=ONE32[:, 0:1]),
                     r=[kt1, "ONE32"], w=[kt1])
                P.op("pool", lambda e: e.tensor_tensor(out=a32[:, :], in0=ER32B[:, :], in1=G2[:, :], op=ALU.subtract),
                     r=[kt1, kg2, ka], w=[ka])
                P.op("act", lambda e: e.activation(out=G2[:, :], in_=G2[:, :], func=AF.Exp, scale=-1.0), r=[kg2], w=[kg2])
                P.op("pool", lambda e: e.tensor_tensor(out=G2[:, :], in0=G2[:, :], in1=OML[d][:, :], op=ALU.mult),
                     r=[kg2, f"OML{d}"], w=[kg2])
                P.op("pool" if d == 0 else "dve", lambda e: e.tensor_tensor(out=k32[:, :], in0=OML[d][:, :], in1=G2[:, :], op=ALU.subtract),
                     r=[kg2, f"OML{d}"], w=[kk])
            return [s1, s2, s3, s4, s5]

        def b1_steps(i):
            T = tiles[i]
            pp, p3 = i % 2, i % 3
            local = T0 <= T < T1
            a32, k32, v16, q32 = A32[pp], K32[pp], V16[p3], Q32[pp]
            ka, kk, kv_, kq = f"A32{pp}", f"K32{pp}", f"V16{p3}", f"Q32{pp}"
            qet, at, dec = QET16[pp], AT16[pp], DEC[p3]
            kqet, kat, kdec = f"QET{pp}", f"AT{pp}", f"DEC{p3}"
            vref, vkey = (OA16[:, T - T0, :], f"OAV{T - T0}") if local else (v16, kv_)
            EB32_, ENB32_, QE16_, KE16_, KET16_ = EB32[pp], ENB32[pp], QE16[pp], KE16[pp], KET16[pp]
            keb, kenb, kqe, kke, kket = f"EB32{pp}", f"ENB32{pp}", f"QE16{pp}", f"KE16{pp}", f"KET16{pp}"

            def s1():
                def blast(e):
                    ins = None
                    for h in range(4):
                        ins = e.matmul(B[1][:, 2 * h:2 * h + 2], lhsT=a32[:, h * 128:(h + 1) * 128],
                                       rhs=CST[:, C_IND:C_IND + 2], start=True, stop=True)
                    return ins
                P.op("pe", blast, r=[ka, "CST"], w=["B1"], est=0.45)
                P.op("act", lambda e: e.activation(out=dec[:, :], in_=B[1][:, 0:8], func=AF.Exp), r=["B1"], w=[kdec])
                P.op("pe", lambda e: e.matmul(B[2][:, :], lhsT=CST[:, UD:UD + 128], rhs=a32[:, :], start=True, stop=True),
                     r=[ka, "CST"], w=["B2"], est=0.9)
                P.op("act", lambda e: e.activation(out=ENB32_[:, :], in_=B[2][:, :], func=AF.Exp, scale=-1.0),
                     r=["B2"], w=[kenb])
                if local:
                    P.op("act", lambda e: e.activation(out=EB32_[:, :], in_=B[2][:, :], func=AF.Exp), r=["B2"], w=[keb])

            def s2():
                P.op("pool" if local else "dve", lambda e: e.tensor_tensor(out=KE16_[:, :], in0=k32[:, :], in1=ENB32_[:, :], op=ALU.mult),
                     r=[kk, kenb], w=[kke])
                if local:
                    P.op("dve", lambda e: e.tensor_tensor(out=QE16_[:, :], in0=QST[:, T - T0, :], in1=EB32_[:, :], op=ALU.mult),
                         r=[f"QST{T - T0}", keb], w=[kqe])

            def s3():
                if not local:
                    return

                def trq(e):
                    ins = None
                    for h in range(4):
                        ins = e.matmul(B[4][:, h * 128:(h + 1) * 128], lhsT=QE16_[:, h * 128:(h + 1) * 128],
                                       rhs=IDENT16[:, :], start=True, stop=True)
                    return ins

                def trk(e):
                    ins = None
                    for h in range(4):
                        ins = e.matmul(B[4][:, h * 128:(h + 1) * 128], lhsT=KE16_[:, h * 128:(h + 1) * 128],
                                       rhs=IDENT16[:, :], start=True, stop=True)
                    return ins
                P.op("pe", trq, r=[kqe, "IDENT16"], w=["B4"], est=0.27)
                P.op("act", lambda e: e.activation(out=qet[:, :], in_=B[4][:, :], func=AF.Identity), r=["B4"], w=[kqet])
                P.op("pe", trk, r=[kke, "IDENT16"], w=["B4"], est=0.27)
                P.op("act", lambda e: e.activation(out=KET16_[:, :], in_=B[4][:, :], func=AF.Identity), r=["B4"], w=[kket])

            def s4():
                if not local:
                    return

                def amat(e):
                    ins = None
                    for h in range(4):
                        ins = e.matmul(B[4][:, h * 128:(h + 1) * 128], lhsT=KET16_[:, h * 128:(h + 1) * 128],
                                       rhs=qet[:, h * 128:(h + 1) * 128], start=True, stop=True)
                    return ins
                P.op("pe", amat, r=[kqet, kket], w=["B4"], est=0.27)
                P.op("dve", lambda e: e.tensor_tensor(
                    out=at[:, :].rearrange("p (h t) -> p h t", h=4),
                    in0=B[4][:, :].rearrange("p (h t) -> p h t", h=4),
                    in1=CST[:, UD:UD + 128].unsqueeze(1).broadcast_to([128, 4, 128]), op=ALU.mult),
                    r=["B4", "CST"], w=[kat])

            def s5():
                def kv(e):
                    ins = None
                    for c in range(2):
                        for h in range(4):
                            ins = e.matmul(B[KVB[c]][:, h * 128:(h + 1) * 128],
                                           lhsT=KE16_[64 * c:64 * c + 64, h * 128:(h + 1) * 128],
                                           rhs=vref[64 * c:64 * c + 64, h * 128:(h + 1) * 128], start=True, stop=True)
                    return ins
                P.op("pe", kv, r=[kke, vkey], w=["B6", "B3"], est=0.6)
            return [s1, s2, s3, s4, s5]

        def b2_steps(i):
            T = tiles[i]
            pp, p3 = i % 2, i % 3
            local = T0 <= T < T1
            li = T - T0
            v16 = V16[p3]
            kv_ = f"V16{p3}"
            qet, at, dec = QET16[pp], AT16[pp], DEC[p3]
            kqet, kat, kdec = f"QET{pp}", f"AT{pp}", f"DEC{p3}"
            vref, vkey = (OA16[:, li, :], f"OAV{li}") if local else (v16, kv_)

            def chunk(ci):
                c = corder[ci]
                if local:
                    def omat(e):
                        ins = None
                        if ci == 0:
                            for h in range(4):
                                ins = e.matmul(B[5][:, h * 128:(h + 1) * 128], lhsT=at[:, h * 128:(h + 1) * 128],
                                               rhs=vref[:, h * 128:(h + 1) * 128], start=(h == 0), stop=False,
                                               skip_group_check=True)
                        for h in range(4):
                            ins = e.matmul(B[5][64 * c:64 * c + 64, h * 128:(h + 1) * 128],
                                           lhsT=qet[:, h * 128 + 64 * c:h * 128 + 64 * c + 64],
                                           rhs=S16[:, h * 128:(h + 1) * 128], start=False, stop=True,
                                           skip_group_check=True, tile_position=(0, 64 * c))
                        return ins
                    P.op("pe", omat, r=[kat, vkey, kqet, "S16"], w=["B5"], est=0.5)
                pdec, pkey, pc = prevdec[0] if prevdec[0] is not None else (dec, kdec, c)
                for h in range(4):
                    P.op("dve", lambda e, h=h: e.scalar_tensor_tensor(
                        out=ST32[:, h * 128:(h + 1) * 128], in0=ST32[:, h * 128:(h + 1) * 128],
                        scalar=pdec[:, 2 * h + pc:2 * h + pc + 1], in1=B[KVB[c]][:, h * 128:(h + 1) * 128],
                        op0=ALU.mult, op1=ALU.add), r=[f"ST32h{h}", pkey, f"B{KVB[c]}"], w=[f"ST32h{h}"], est=0.4)
                P.op("dve", lambda e: e.tensor_tensor(
                    out=S16[:, :].rearrange("p (h t) -> p h t", h=4),
                    in0=ST32[:, :].rearrange("p (h t) -> p h t", h=4),
                    in1=dec[:, :].rearrange("p (h c) -> p h c", c=2)[:, :, c:c + 1].broadcast_to([128, 4, 128]), op=ALU.mult),
                    r=[f"ST32h{h_}" for h_ in range(4)] + [kdec], w=["S16"])
                prevdec[0] = (dec, kdec, c)

            def s1():
                chunk(0)

            def s2():
                chunk(1)

            def s3():
                if not local:
                    return
                if d == 0:
                    P.op("dve", lambda e: e.tensor_copy(out=OF32[:, li, :], in_=B[5][:, :]), r=["B5"], w=[f"OF{li}"])
                else:
                    o32, on32, ss, rstd = O32[pp], ON32[pp], SS[pp], RSTD[pp]
                    ko, kon, kss, krs = f"O32{pp}", f"ON32{pp}", f"SS{pp}", f"RSTD{pp}"
                    P.op("dve", lambda e: e.tensor_tensor(out=o32[:, :], in0=B[5][:, :], in1=OF32[:, li, :], op=ALU.add),
                         r=["B5", f"OF{li}"], w=[ko])
                    P.op("pool", lambda e: e.memset(ss[:, :], 0.0), w=[kss + str(h_) for h_ in range(4)])
                    for h in range(4):
                        P.op("act", lambda e, h=h: e.activation(out=on32[:, h * 128:(h + 1) * 128], in_=o32[:, h * 128:(h + 1) * 128],
                                                                func=AF.Square, accum_out=ss[:, h:h + 1]),
                             r=[ko], w=[kon + str(h), kss + str(h)], est=0.35)
                    P.op("dve", lambda e: e.tensor_scalar(out=rstd[:, :], in0=ss[:, :], scalar1=1.0 / 128.0, scalar2=RMS_EPS,
                                                          op0=ALU.mult, op1=ALU.add), r=[kss + str(h_) for h_ in range(4)], w=[krs], est=0.2)
                    P.op("act", lambda e: e.activation(out=rstd[:, :], in_=rstd[:, :], func=AF.Ln), r=[krs], w=[krs], est=0.2)
                    P.op("act", lambda e: e.activation(out=rstd[:, :], in_=rstd[:, :], func=AF.Exp, scale=-0.5), r=[krs], w=[krs], est=0.2)
                    P.op("pool", lambda e: e.tensor_tensor(
                        out=on32[:, :].rearrange("p (h t) -> p h t", h=4),
                        in0=o32[:, :].rearrange("p (h t) -> p h t", h=4),
                        in1=rstd[:, :].unsqueeze(2).broadcast_to([128, 4, 128]), op=ALU.mult),
                        r=[ko, krs] + [kon + str(h_) for h_ in range(4)], w=[kon + str(h_) for h_ in range(4)])
                    P.op("pool", lambda e: e.tensor_tensor(out=OH16[:, li, :], in0=on32[:, :], in1=SGc[p3][:, :], op=ALU.mult),
                         r=[kon + str(h_) for h_ in range(4)] + [f"SGc{p3}"], w=[f"OH{li}"])
            return [s1, s2, s3]

        n = len(tiles)
        nop = lambda: None
        for it in range(-2, n):
            b2 = b2_steps(it) if it >= 0 else [nop] * 3
            b1 = b1_steps(it + 1) if 0 <= it + 1 < n else [nop] * 5
            a = a_steps(it + 2) if it + 2 < n else [nop] * 5
            for st in (b2[0], a[0], b1[0], b2[1], a[1], b1[1], b2[2], a[2], b1[2], a[3], b1[3], a[4], b1[4]):
                st()

    hg_pass(0)
    if upto == 1:
        P.emit([])
        return nc
    hg_pass(1)
    if upto == 2:
        P.emit([])
        return nc
    load_xt(1, 0)
    load_w(1024, w_in_v[:, :, 1024:1536], 512)

    P.barrier()
    A.reset(phase_mark)

    KT = A("kt", [128, 4, NTOK], BF16)
    VA = A("va", [128, NT * 8, 65], BF16)
    QTG = A("qtg", [128, 4, 512], BF16)
    SGA = A("sga", [128, 4, 512], BF16)
    EBI = A("ebi", [128, 8, 5 * 128], BF16)
    EBH = [A(f"ebh{i}", [128, 7 * 128], BF16) for i in range(3)]
    STG = [A(f"stg{i}", [128, 7 * 128], F32) for i in range(2)]
    PX32s = [A(f"px32{i}", [128, 1024], F32) for i in range(3)]
    P16 = [A(f"p16{i}", [128, 1024], BF16) for i in range(3)]
    RDEN = A("rden", [128, 8], F32)
    ONE32a = A("one32a", [128, 8], F32)
    SGTa = A("sgta", [128, 512], F32)
    P.op("dve", lambda e: e.memset(ONE32a[:, :], 1.0), w=["ONE32a"])
    OA32 = A("oa32", [128, 512], F32)
    b_qk = CST[:, C_BQK:C_BQK + 8]

    load_bias(0, 1024, 512)
    load_w(1536, w_in_v[:, :, 0:512], 512)
    load_w(512, w_in_v[:, :, 1536:2048], 512)
    load_bias(512, 1536, 512)
    P.op("pool", lambda e: e.memset(VA[:, :, 64:65], 1.0), w=["VA1"])

    def load_tab(dst, dram_ap, nj, key, sidx):
        for h in range(8):
            sb = (sidx[0]) % 2
            sidx[0] += 1
            P.op("sp", lambda e, h=h, sb=sb: e.dma_start(out=STG[sb][:, 0:nj * 128], in_=dram_ap[h]),
                 w=[f"STG{sb}"], dma=f"stg{sb}")
            P.op("act", lambda e, h=h, sb=sb: e.activation(out=dst[:, h, 0:nj * 128], in_=STG[sb][:, 0:nj * 128], func=AF.Exp),
                 r=[f"STG{sb}"], w=[key])
    sidx = [0]
    load_tab(EBI, d_tabi, 5, "EBI", sidx)

    def proj_t(bank, buf, wc0, dst_fn, bias_col, keyw):
        def fn(e):
            ins = None
            for k in range(8):
                ins = e.matmul(BB[bank][:, :], lhsT=W[:, k, wc0:wc0 + 128], rhs=XT[buf][:, k, :],
                               start=(k == 0), stop=(k == 7))
            return ins
        P.op("pe", fn, r=[f"XT{buf}", f"W{wc0 // 512}"], w=[f"B{bank}"], est=1.8)
        P.op("act", lambda e: e.activation(out=dst_fn, in_=BB[bank][:, :], func=AF.Identity,
                                           bias=b_qk[:, bias_col:bias_col + 1]), r=[f"B{bank}", "CST"], w=[keyw])

    def a1_group(g, xbuf=1):
        if g > 0:
            load_xt(xbuf, g)
        for m4 in range(4):
            proj_t(7, xbuf, m4 * 128, KT[:, m4, g * 512:(g + 1) * 512], 4 + m4, f"KT{g}")
        for tl in range(4):
            T = g * 4 + tl
            if T < 2 or T > 21:
                continue
            proj_n(2, xbuf, tl, 2, 0)
            P.op("dve", lambda e, T=T: e.tensor_copy(out=VA[:, T * 8:(T + 1) * 8, 0:64], in_=B[2][:, :].rearrange("p (h c) -> p h c", h=8)),
                 r=["B2"], w=[f"VA{T}"])
    for g in range(3):
        a1_group(g, 1 - g % 2)

    w_out_v = d_wout.rearrange("(k p) c -> p k c", p=128)
    hcount = [0]
    ringc = [0]
    var_js = {0: list(range(-2, 4)), 1: list(range(-3, 4)), 14: list(range(-3, 4)), 15: list(range(-3, 3))}
    for g in range(1, 5):
        buf = 0
        load_xt(0, g)
        for m4 in range(4):
            proj_t(m4 % 2, buf, 1536 + m4 * 128, QTG[:, m4, :], m4, "QTG")
        for tl in range(4):
            proj_n(2, buf, tl, 1, 1)
            P.op("act", lambda e: e.activation(out=SGTa[:, :], in_=B[2][:, :], func=AF.Exp, scale=-1.0), r=["B2"], w=["SGTa"])
            P.op("act", lambda e: e.activation(out=SGTa[:, :], in_=SGTa[:, :], func=AF.Ln, bias=ONE32a[:, 0:1]),
                 r=["SGTa", "ONE32a"], w=["SGTa"])
            P.op("act", lambda e: e.activation(out=SGTa[:, :], in_=SGTa[:, :], func=AF.Exp, scale=-1.0), r=["SGTa"], w=["SGTa"])
            P.op("dve", lambda e, tl=tl: e.tensor_tensor(out=SGA[:, tl, :], in0=B[2][:, :], in1=SGTa[:, :], op=ALU.mult),
                 r=["B2", "SGTa"], w=[f"SGA{tl}"])
        if g + 2 <= 5:
            a1_group(g + 2)
        if g == 3:
            load_w(0, w_out_v[:, :, 0:512], 512)
            load_w(1024, w_out_v[:, :, 512:1024], 512)
            load_bias(1024, 4608, 1024)
        for tl in range(4):
            T = g * 4 + tl
            m = T - T0
            if m in var_js:
                js = var_js[m]
                bi = {0: 0, 1: 1, 14: 2, 15: 3}[m]
                border, j0 = True, -3
            else:
                js = list(range(-2, 3))
                border, j0 = False, -2
            nj = len(js)
            for h in range(8):
                m4, half = h // 2, h % 2
                hc_ = hcount[0]
                hcount[0] += 1
                pb = hc_ % 3
                p0 = 64 * half
                sb0 = 3 if hc_ % 2 == 0 else 0
                PX32 = PX32s[pb]
                if border:
                    rs = ringc[0] % 3
                    ringc[0] += 1
                    sb = sidx[0] % 2
                    sidx[0] += 1
                    P.op("sp", lambda e, h=h, sb=sb, bi=bi: e.dma_start(out=STG[sb][:, :], in_=d_tabb[bi][h]),
                         w=[f"STG{sb}"], dma=f"stg{sb}")
                    P.op("act", lambda e, sb=sb, rs=rs: e.activation(out=EBH[rs][:, :], in_=STG[sb][:, :], func=AF.Exp),
                         r=[f"STG{sb}"], w=[f"EBH{rs}"])
                    tabv, tkey = EBH[rs], f"EBH{rs}"
                else:
                    tabv, tkey = EBI[:, h, :], "EBI"

                def smat(e, js=js, m4=m4, p0=p0, T=T, tl=tl, sb0=sb0):
                    ins = None
                    for idx, j in enumerate(js):
                        bk = B[sb0 + idx // 4]
                        ins = e.matmul(bk[:, (idx % 4) * 128:(idx % 4 + 1) * 128],
                                       lhsT=KT[p0:p0 + 64, m4, (T + j) * 128:(T + j + 1) * 128],
                                       rhs=QTG[p0:p0 + 64, m4, tl * 128:(tl + 1) * 128], start=True, stop=True)
                    return ins
                P.op("pe", smat, r=["QTG"] + [f"KT{(T + j) // 4}" for j in js], w=[f"B{sb0}", f"B{sb0 + 1}"], est=0.5)
                n0 = min(nj, 4)
                P.op("act", lambda e, n0=n0, PX32=PX32, sb0=sb0: e.activation(out=PX32[:, 0:n0 * 128], in_=B[sb0][:, 0:n0 * 128],
                                                                             func=AF.Exp, scale=0.125),
                     r=[f"B{sb0}"], w=[f"PXa{pb}"])
                if nj > 4:
                    P.op("act", lambda e, nj=nj, PX32=PX32, sb0=sb0: e.activation(
                        out=PX32[:, 512:512 + (nj - 4) * 128], in_=B[sb0 + 1][:, 0:(nj - 4) * 128],
                        func=AF.Exp, scale=0.125), r=[f"B{sb0 + 1}"], w=[f"PXb{pb}"])
                c0 = (js[0] - j0) * 128
                P.op("dve", lambda e, pb=pb, nj=nj, c0=c0, tabv=tabv, PX32=PX32: e.tensor_tensor(
                    out=P16[pb][:, 0:nj * 128], in0=PX32[:, 0:nj * 128], in1=tabv[:, c0:c0 + nj * 128], op=ALU.mult),
                    r=[f"PXa{pb}", f"PXb{pb}", tkey], w=[f"P16{pb}"], est=0.95)

                def pv(e, js=js, pb=pb, h=h, T=T):
                    ins = None
                    bk = B[5 + h // 4]
                    hc = (h % 4) * 65
                    for idx, j in enumerate(js):
                        ins = e.matmul(bk[:, hc:hc + 65], lhsT=P16[pb][:, idx * 128:(idx + 1) * 128],
                                       rhs=VA[:, (T + j) * 8 + h, :], start=(idx == 0), stop=(idx == len(js) - 1))
                    return ins
                P.op("pe", pv, r=[f"P16{pb}", "VA1"] + [f"VA{T + j}" for j in js], w=[f"B{5 + h // 4}"], est=0.45)
            for hb in range(2):
                P.op("dve", lambda e, hb=hb: e.reciprocal(out=RDEN[:, 4 * hb:4 * hb + 4].unsqueeze(2),
                                                          in_=B[5 + hb][:, 0:260].rearrange("p (h c) -> p h c", h=4)[:, :, 64:65]),
                     r=[f"B{5 + hb}"], w=["RDEN"])
                P.op("dve", lambda e, hb=hb: e.tensor_tensor(
                    out=OA32[:, hb * 256:(hb + 1) * 256].rearrange("p (h c) -> p h c", h=4),
                    in0=B[5 + hb][:, 0:260].rearrange("p (h c) -> p h c", h=4)[:, :, 0:64],
                    in1=RDEN[:, 4 * hb:4 * hb + 4].unsqueeze(2).broadcast_to([128, 4, 64]), op=ALU.mult),
                    r=[f"B{5 + hb}", "RDEN"], w=[f"OA32{hb}"])
            P.op("pool", lambda e, m=m, tl=tl: e.tensor_tensor(out=OA16[:, m, :], in0=OA32[:, :], in1=SGA[:, tl, :], op=ALU.mult),
                 r=["OA320", "OA321", f"SGA{tl}"], w=[f"OA{m}"])

    P.op("sp", lambda e: e.dma_start(out=LNP[:, :], in_=d_lnp[:, :]), w=["XT1", "LNP"], dma="lnp")
    for m in range(2):
        P.op("sp", lambda e, m=m: e.dma_start(out=XN32[m][:, :], in_=d_xN[m * 128:(m + 1) * 128, :]),
             w=["XT0", f"XN{m}"] if m == 0 else [f"XN{m}"], r=[] if m == 0 else ["XT0"], dma=f"xn{m}")
    if upto == 3:
        P.emit([])
        return nc
    P.barrier()
    A.reset(phase_mark)

    ND = 4
    R32s = [A(f"r32{i}", [128, 1024], F32) for i in range(ND)]
    RN32s = [A(f"rn32{i}", [128, 1024], F32) for i in range(ND)]
    JKa = [A(f"jka{i}", [128, 1024], F32) for i in range(ND)]
    JKb = [A(f"jkb{i}", [128, 1024], F32) for i in range(ND)]
    OT16s = [A(f"ot16{i}", [128, 8, 128], BF16) for i in range(ND)]
    STATs = [A(f"stat{i}", [128, 8], F32) for i in range(ND)]
    XNs = [XN32[0], XN32[1]] + [A(f"xn{i}", [128, 1024], F32) for i in range(2, ND)]
    EPSC = A("epsc", [128, 8], F32)
    P.op("dve", lambda e: e.memset(EPSC[:, :], LN_EPS), w=["EPSC"])

    final = []
    def c_tile(m):
        xb = m % ND
        R32, RN32, OT16, STAT, XN, Ja, Jb = R32s[xb], RN32s[xb], OT16s[xb], STATs[xb], XNs[xb], JKa[xb], JKb[xb]
        sx = str(xb)
        yb = (0, 1) if m % 2 == 0 else (3, 4)
        tb = (7, 2) if m % 2 == 0 else (5, 6)
        if m >= 2:
            P.op("sp", lambda e: e.dma_start(out=XN[:, :], in_=d_xN[m * 128:(m + 1) * 128, :]),
                 w=[f"XN{xb}"], dma=f"xn{xb}")
        if debug and debug != 3:
            P.op("pool", lambda e: e.dma_start(out=d_dbg[m * 128:(m + 1) * 128, 0:512], in_=OA16[:, m, :]),
                 r=[f"OA{m}"], dma="dbg")
            if debug not in (2, 3):
                P.op("pool", lambda e: e.dma_start(out=d_dbg[m * 128:(m + 1) * 128, 512:1024], in_=OH16[:, m, :]),
                     r=[f"OH{m}"], dma="dbg")

        def trc(e):
            ins = None
            for k in range(8):
                src = OA16[:, m, k * 128:(k + 1) * 128] if k < 4 else OH16[:, m, (k - 4) * 128:(k - 3) * 128]
                dst = BB[tb[0]][:, k * 128:(k + 1) * 128] if k < 4 else BB[tb[1]][:, (k - 4) * 128:(k - 3) * 128]
                ins = e.matmul(dst, lhsT=src, rhs=IDENT16[:, :], start=True, stop=True)
            return ins
        P.op("pe", trc, r=[f"OA{m}", f"OH{m}", "IDENT16"], w=[f"B{tb[0]}", f"B{tb[1]}"], est=0.5)
        P.op("act", lambda e: e.activation(out=OT16[:, 0:4, :], in_=BB[tb[0]][:, :].rearrange("p (k t) -> p k t", k=4), func=AF.Identity),
             r=[f"B{tb[0]}"], w=["OT16a_" + sx])
        P.op("dve", lambda e: e.tensor_copy(out=OT16[:, 4:8, :], in_=BB[tb[1]][:, :].rearrange("p (k t) -> p k t", k=4)),
             r=[f"B{tb[1]}"], w=["OT16b_" + sx])
        for hf in range(2):
            def ymat(e, hf=hf):
                ins = None
                for k in range(8):
                    ins = e.matmul(BB[yb[hf]][:, :], lhsT=OT16[:, k, :], rhs=W[:, k, hf * 1024:hf * 1024 + 512],
                                   start=(k == 0), stop=False)
                ins = e.matmul(BB[yb[hf]][:, :], lhsT=ONES16[0:1, :], rhs=BIAS16[0:1, 1024 + hf * 512:1024 + (hf + 1) * 512],
                               start=False, stop=True)
                return ins
            P.op("pe", ymat, r=["OT16a_" + sx, "OT16b_" + sx, f"W{2 * hf}", f"BIAS{2 + hf}", "ONES16"], w=[f"B{yb[hf]}"], est=2.1)
            P.op("dve", lambda e, hf=hf: e.scalar_tensor_tensor(
                out=R32[:, hf * 512:(hf + 1) * 512], in0=XN[:, hf * 512:(hf + 1) * 512], scalar=ALPHA,
                in1=BB[yb[hf]][:, :], op0=ALU.mult, op1=ALU.add), r=[f"XN{xb}", f"B{yb[hf]}"], w=[f"R32{hf}_" + sx])
        P.op("pool", lambda e: e.memset(STAT[:, 0:2], 0.0), w=["STAT0_" + sx, "STAT1_" + sx], est=0.2)
        P.op("act", lambda e: e.activation(out=Ja[:, :], in_=R32[:, :], func=AF.Identity, accum_out=STAT[:, 0:1]),
             r=["R320_" + sx, "R321_" + sx], w=["JKa_" + sx, "STAT0_" + sx], est=1.2)
        P.op("act", lambda e: e.activation(out=Jb[:, :], in_=R32[:, :], func=AF.Square, accum_out=STAT[:, 1:2]),
             r=["R320_" + sx, "R321_" + sx], w=["JKb_" + sx, "STAT1_" + sx], est=1.2)
        P.op("dve", lambda e: e.tensor_scalar(out=STAT[:, 2:4], in0=STAT[:, 0:2], scalar1=1.0 / 1024.0, scalar2=None, op0=ALU.mult),
             r=["STAT0_" + sx, "STAT1_" + sx], w=["STAT2_" + sx], est=0.15)
        P.op("dve", lambda e: e.tensor_scalar(out=STAT[:, 4:5], in0=STAT[:, 2:3], scalar1=STAT[:, 2:3], scalar2=STAT[:, 3:4],
                                              op0=ALU.mult, op1=ALU.subtract), r=["STAT2_" + sx], w=["STAT4_" + sx], est=0.15)
        P.op("act", lambda e: e.activation(out=STAT[:, 5:6], in_=STAT[:, 4:5], func=AF.Ln, scale=-1.0, bias=EPSC[:, 0:1]),
             r=["STAT4_" + sx, "EPSC"], w=["STAT5_" + sx], est=0.15)
        P.op("act", lambda e: e.activation(out=STAT[:, 5:6], in_=STAT[:, 5:6], func=AF.Exp, scale=-0.5), r=["STAT5_" + sx], w=["STAT5_" + sx], est=0.15)
        P.op("dve", lambda e: e.scalar_tensor_tensor(out=STAT[:, 6:7], in0=STAT[:, 2:3], scalar=-1.0, in1=STAT[:, 5:6],
                                                      op0=ALU.mult, op1=ALU.mult), r=["STAT2_" + sx, "STAT5_" + sx], w=["STAT6_" + sx], est=0.15)
        P.op("act", lambda e: e.activation(out=RN32[:, :], in_=R32[:, :], func=AF.Identity, scale=STAT[:, 5:6], bias=STAT[:, 6:7]),
             r=["R320_" + sx, "R321_" + sx, "STAT5_" + sx, "STAT6_" + sx], w=["RN32_" + sx], est=1.2)
        P.op("dve", lambda e: e.tensor_tensor(out=RN32[:, :], in0=RN32[:, :], in1=LNP[:, 0:1024], op=ALU.mult),
             r=["RN32_" + sx, "LNP"], w=["RN32_" + sx], est=1.4)
        P.op("pool", lambda e: e.tensor_tensor(out=RN32[:, :], in0=RN32[:, :], in1=LNP[:, 1024:2048], op=ALU.add),
             r=["RN32_" + sx, "LNP"], w=["RN32_" + sx], est=2.5)
        tok = P.op("sp", lambda e: e.dma_start(out=d_out[m * 128:(m + 1) * 128, :], in_=RN32[:, :]),
                   r=["RN32_" + sx], dma=f"st{xb}")
        final.append(tok)
    for m in range(16):
        c_tile(m)
    if debug:
        final.extend(o for seg in P.segs for o in seg if o.dma == "dbg")
    P.emit(final)
    return nc


def _const_block():
    c = np.zeros((128, NCST), np.float32)
    c[:, C_ID:C_ID + 128] = np.eye(128, dtype=np.float32)
    s = np.arange(128)[:, None]
    t = np.arange(128)[None, :]
    same = (s // 64) == (t // 64)
    c[:, C_UF:C_UF + 128] = (same & (s <= t)).astype(np.float32)
    c[:, C_RF:C_RF + 128] = (same & (s > t)).astype(np.float32)
    c[:, C_UB:C_UB + 128] = (same & (s >= t)).astype(np.float32)
    c[:, C_RB:C_RB + 128] = (same & (s < t)).astype(np.float32)
    c[:, C_IND + 0] = (np.arange(128) < 64).astype(np.float32)
    c[:, C_IND + 1] = (np.arange(128) >= 64).astype(np.float32)
    return c


def _att_table(rpb, R0, m, js):
    p = np.arange(128)
    kr_i, kc = p // 64, p % 64
    qr_i, qc = p // 64, p % 64
    out = np.full((8, 128, len(js), 128), NEG, np.float32)
    qr = R0 + 2 * m + qr_i
    rs = np.clip(qr - 4, 0, 120)
    cs = np.clip(qc - 8, 0, 48)
    for ji, j in enumerate(js):
        kr = R0 + 2 * m + 2 * j + kr_i
        ok = (kr[:, None] >= 0) & (kr[:, None] < 128) & (kr[:, None] >= rs[None, :]) & (kr[:, None] < rs[None, :] + 8) \
            & (kc[:, None] >= cs[None, :]) & (kc[:, None] < cs[None, :] + 16)
        dr = np.clip(kr[:, None] - qr[None, :] + 7, 0, 14)
        dc = np.clip(kc[:, None] - qc[None, :], -15, 15) + 15
        g = rpb[:, dr, dc]
        out[:, :, ji, :] = np.where(ok[None], g, NEG)
    return out


_NC_CACHE = {}


def _prep_inputs(x, w_in, b_in, rpb, lb_fwd_logits, lb_bwd_logits, hg_norm_gain, w_out, b_out, ln_gain, ln_bias):
    x = np.asarray(x, np.float32)
    w_in0 = np.ascontiguousarray(np.asarray(w_in, np.float32)[0])
    w_out0 = np.ascontiguousarray(np.asarray(w_out, np.float32)[0])
    b_in0 = np.asarray(b_in, np.float32)[0]
    rpb0 = np.asarray(rpb, np.float32)[0]
    brow = np.concatenate([b_in0, np.asarray(b_out, np.float32)[0]])[None, :].astype(np.float32)
    lbf = np.asarray(lb_fwd_logits, np.float32)
    lbb = np.asarray(lb_bwd_logits, np.float32)
    lbl = np.ascontiguousarray(np.broadcast_to(
        np.concatenate([lbf[0], lbf[1], lbb[0], lbb[1], b_in0[2048:2560], b_in0[3584:4096]])[None, :], (128, 3072))).astype(np.float32)
    lnp = np.ascontiguousarray(np.broadcast_to(
        np.concatenate([np.asarray(ln_gain, np.float32)[0], np.asarray(ln_bias, np.float32)[0]])[None, :], (128, 2048))).astype(np.float32)
    cbase = _const_block()
    cbase[:, C_GAIN:C_GAIN + 512] = np.asarray(hg_norm_gain, np.float32)[0][None, :]
    cbase[:, C_BQK:C_BQK + 4] = b_in0[0:512].reshape(4, 128).T
    cbase[:, C_BQK + 4:C_BQK + 8] = b_in0[512:1024].reshape(4, 128).T
    in_maps = []
    for c in range(NCORES):
        b, s = c // 4, c % 4
        t0 = s * NLOC
        lo, hi = max(0, t0 - HALO), min(SEQ, t0 + NLOC + HALO)
        xT = np.zeros((DM, NTOK), np.float32)
        xT[:, lo - (t0 - HALO):hi - (t0 - HALO)] = x[b, lo:hi, :].T
        cst = cbase.copy()
        tpos = t0 - HALO + np.arange(NT)[None, :] * 128 + np.arange(128)[:, None]
        cst[:, C_TM:C_TM + NT] = ((tpos >= 0) & (tpos < SEQ)).astype(np.float32)
        R0 = 32 * s
        tabi = _att_table(rpb0, R0, 7, list(range(-2, 3))).reshape(8, 128, 5 * 128)
        tabb = np.stack([_att_table(rpb0, R0, m, list(range(-3, 4))).reshape(8, 128, 7 * 128) for m in (0, 1, 14, 15)])
        in_maps.append({
            "xT": xT, "xN": np.ascontiguousarray(x[b, t0:t0 + NLOC, :]), "w_in": w_in0, "w_out": w_out0,
            "brow": brow, "cst": cst, "lbl": lbl, "lnp": lnp,
            "tabi": np.ascontiguousarray(tabi), "tabb": np.ascontiguousarray(tabb),
        })
    return in_maps


def kernel(x, w_in, b_in, rpb, lb_fwd_logits, lb_bwd_logits, hg_norm_gain, w_out, b_out, ln_gain, ln_bias):
    in_maps = _prep_inputs(x, w_in, b_in, rpb, lb_fwd_logits, lb_bwd_logits, hg_norm_gain, w_out, b_out, ln_gain, ln_bias)
    if "nc" not in _NC_CACHE:
        _NC_CACHE["nc"] = build_nc()
    res = run_bass_kernel_spmd(_NC_CACHE["nc"], in_maps, core_ids=list(range(NCORES)))
    out = np.zeros((2, SEQ, DM), np.float32)
    for c in range(NCORES):
        b, s = c // 4, c % 4
        out[b, s * NLOC:(s + 1) * NLOC, :] = res.results[c]["out"]
    return out
```

```python
import numpy as np
from contextlib import ExitStack
import concourse.bass as bass
import concourse.mybir as mybir
from concourse.bass_utils import run_bass_kernel_spmd

F32 = mybir.dt.float32
BF16 = mybir.dt.bfloat16
AF = mybir.ActivationFunctionType
ALU = mybir.AluOpType

NCORES = 8
SEQ = 8192
DM = 1024
HALO = 512
NLOC = 2048
NTOK = NLOC + 2 * HALO
NT = NTOK // 128
T0 = HALO // 128
T1 = T0 + NLOC // 128
ALPHA = 2.0 ** 0.25
LN_EPS = 1e-5
RMS_EPS = 1e-6
NEG = -30000.0
SKIP = set()

C_ID, C_UF, C_RF, C_UB, C_RB, C_IND = 0, 128, 256, 384, 512, 640
C_GAIN = 648
C_TM = C_GAIN + 512
C_BQK = C_TM + 24
NCST = C_BQK + 8


class _Op:
    __slots__ = ("e", "fn", "deps", "dma", "est", "lat", "idx", "tok", "start", "end", "wk")

    def __init__(self, e, fn, deps, dma, est, lat, idx):
        self.e, self.fn, self.deps, self.dma, self.est, self.lat, self.idx = e, fn, deps, dma, est, lat, idx
        self.tok = None
        self.start = self.end = 0.0


class Prog:
    EST = {"pe": 0.5, "act": 0.62, "dve": 0.7, "pool": 1.3, "sp": 0.1}
    HOP = 0.35
    SLACK = 0.3

    def __init__(self, nc):
        self.nc = nc
        self.eng = {"pe": nc.tensor, "act": nc.scalar, "dve": nc.vector, "pool": nc.gpsimd, "sp": nc.sync}
        self.segs = [[]]
        self.last_w = {}
        self.readers = {}
        self.n = 0
    mute = False

    def op(self, e, fn, r=(), w=(), dma=None, est=None):
        if self.mute:
            return None
        deps = []
        seen = set()

        def add(o):
            if o is not None and id(o) not in seen:
                seen.add(id(o))
                deps.append(o)
        for k in r:
            add(self.last_w.get(k))
        for k in w:
            add(self.last_w.get(k))
            for t in self.readers.get(k, ()):
                add(t)
        if dma is not None:
            lat = est if est is not None else (6.0 if e == "pool" else 3.0)
            dur = 1.0 if e == "pool" else 0.1
        else:
            dur = est if est is not None else self.EST[e]
            lat = dur
        o = _Op(e, fn, deps, dma, dur, lat, self.n)
        o.wk = tuple(w)
        self.n += 1
        self.segs[-1].append(o)
        for k in r:
            self.readers.setdefault(k, []).append(o)
        for k in w:
            self.last_w[k] = o
            self.readers[k] = []
        return o

    def barrier(self):
        self.segs.append([])

    def _schedule(self, seg):
        inseg = {id(o) for o in seg}
        ndep = {}
        users = {}
        for o in seg:
            c = 0
            for d_ in o.deps:
                if id(d_) in inseg:
                    c += 1
                    users.setdefault(id(d_), []).append(o)
            ndep[id(o)] = c
        tail = {}
        for o in sorted(seg, key=lambda o_: -o_.idx):
            t = 0.0
            for u in users.get(id(o), ()):
                t = max(t, tail[id(u)] + (0.0 if (u.e == o.e and o.dma is None) else self.HOP))
            tail[id(o)] = t + o.lat
        free = {e: 0.0 for e in self.eng}
        ready = {}
        cand = [o for o in seg if ndep[id(o)] == 0]
        for o in cand:
            ready[id(o)] = 0.0
        order = {e: [] for e in self.eng}
        done = 0
        while cand:
            tmin = min(max(free[o.e], ready[id(o)]) for o in cand)
            best, bkey = None, None
            for o in cand:
                st = max(free[o.e], ready[id(o)])
                if st > tmin + self.SLACK:
                    continue
                key = (-tail[id(o)], o.idx)
                if bkey is None or key < bkey:
                    best, bkey = o, key
            o = best
            cand.remove(o)
            o.start = max(free[o.e], ready[id(o)])
            free[o.e] = o.start + o.est
            o.end = o.start + o.lat
            order[o.e].append(o)
            done += 1
            for u in users.get(id(o), ()):
                t = o.end + (0.0 if (u.e == o.e and o.dma is None) else self.HOP)
                if t > ready.get(id(u), 0.0):
                    ready[id(u)] = t
                ndep[id(u)] -= 1
                if ndep[id(u)] == 0:
                    cand.append(u)
        assert done == len(seg), (done, len(seg))
        return order

    def emit(self, final_ops):
        nc = self.nc
        plan = {e: [] for e in self.eng}
        count = {e: 0 for e in self.eng}
        dma_cnt = {}
        waited = {e: {} for e in self.eng}
        semnames = set()
        for seg in self.segs:
            order = self._schedule(seg)
            allops = sorted(seg, key=lambda o: (o.start, o.idx))
            for o in allops:
                if o.dma is None:
                    count[o.e] += 1
                    o.tok = ("E_" + o.e, count[o.e], o.e)
                else:
                    dma_cnt[o.dma] = dma_cnt.get(o.dma, 0) + 1
                    o.tok = ("D_" + o.dma, 16 * dma_cnt[o.dma], None)
                semnames.add(o.tok[0])
            for e in self.eng:
                c0 = count[e] - sum(1 for o in order[e] if o.dma is None)
                for o in order[e]:
                    if o.dma is None:
                        c0 += 1
                        o.tok = ("E_" + e, c0, e)
            for e in self.eng:
                for o in order[e]:
                    need = {}
                    for d_ in o.deps:
                        s_, v, se = d_.tok
                        if se == e and e == "pe" and o.dma is None:
                            continue
                        if v > need.get(s_, 0):
                            need[s_] = v
                    waits = []
                    for s_, v in need.items():
                        if waited[e].get(s_, 0) >= v:
                            continue
                        waited[e][s_] = v
                        waits.append((s_, v))
                    plan[e].append((waits, o.fn, o.tok))
            toks = [("E_" + e, count[e]) for e in self.eng if count[e] > 0] + [("D_" + k, 16 * c) for k, c in dma_cnt.items()]
            for e in self.eng:
                waits = []
                for s_, v in toks:
                    if (s_ == "E_" + e and e == "pe") or waited[e].get(s_, 0) >= v:
                        continue
                    waited[e][s_] = v
                    waits.append((s_, v))
                if waits:
                    plan[e].append((waits, None, None))
        final_tokens = [o.tok for o in final_ops]
        with ExitStack() as es:
            sems = {n_: es.enter_context(nc.semaphore(n_)) for n_ in sorted(semnames)}
            block = es.enter_context(nc.Block())

            def run(e, eng):
                for waits, fn, tok in plan[e]:
                    for s_, v in waits:
                        eng.wait_ge(sems[s_], v)
                    if fn is None:
                        continue
                    ins = fn(eng)
                    ins.then_inc(sems[tok[0]], 16 if tok[2] is None else 1)
                if e == "sp":
                    for s_, v, _ in final_tokens:
                        eng.wait_ge(sems[s_], v)

            @block.tensor
            def _(eng):
                run("pe", eng)

            @block.scalar
            def _(eng):
                run("act", eng)

            @block.vector
            def _(eng):
                run("dve", eng)

            @block.gpsimd
            def _(eng):
                run("pool", eng)

            @block.sync
            def _(eng):
                run("sp", eng)


class Alloc:
    def __init__(self, nc, base, size):
        self.nc, self.base, self.size, self.off, self.n = nc, base, size, 0, 0

    def mark(self):
        return self.off

    def reset(self, m):
        self.off = m

    def __call__(self, name, shape, dtype):
        nb = int(np.prod(shape[1:])) * (4 if dtype == F32 else 2)
        nb = (nb + 63) // 64 * 64
        assert self.off + nb <= self.size, (name, self.off, nb, self.size)
        self.n += 1
        t = self.nc.alloc_sbuf_tensor_at(f"{name}_{self.n}", list(shape), dtype, offset=self.base + self.off)
        self.off += nb
        return t


def build_nc(debug=False, upto=9, ntl=99):
    nc = bass.Bass("TRN2", target_bir_lowering=False)
    d_xT = nc.dram_tensor("xT", [DM, NTOK], F32, kind="ExternalInput").ap()
    d_xN = nc.dram_tensor("xN", [NLOC, DM], F32, kind="ExternalInput").ap()
    d_win = nc.dram_tensor("w_in", [DM, 4608], F32, kind="ExternalInput").ap()
    d_wout = nc.dram_tensor("w_out", [DM, DM], F32, kind="ExternalInput").ap()
    d_brow = nc.dram_tensor("brow", [1, 4608 + 1024], F32, kind="ExternalInput").ap()
    d_cst = nc.dram_tensor("cst", [128, NCST], F32, kind="ExternalInput").ap()
    d_lbl = nc.dram_tensor("lbl", [128, 6 * 512], F32, kind="ExternalInput").ap()
    d_lnp = nc.dram_tensor("lnp", [128, 2 * 1024], F32, kind="ExternalInput").ap()
    d_tabi = nc.dram_tensor("tabi", [8, 128, 5 * 128], F32, kind="ExternalInput").ap()
    d_tabb = nc.dram_tensor("tabb", [4, 8, 128, 7 * 128], F32, kind="ExternalInput").ap()
    d_out = nc.dram_tensor("out", [NLOC, DM], F32, kind="ExternalOutput").ap()
    d_dbg = None
    if debug:
        d_dbg = nc.dram_tensor("dbg", [NLOC, 1024], F32, kind="ExternalOutput").ap()

    arena = nc.alloc_sbuf_tensor("arena", [128, 207 * 1024], mybir.dt.uint8)
    base = nc.lookup_mloc(arena).addr
    A = Alloc(nc, base, 207 * 1024)
    P = Prog(nc)

    B = [nc.alloc_psum_tensor(f"pb{i}", [128, 512], F32) for i in range(7)]
    B7 = nc.alloc_psum_tensor("pb7", [128, 512], F32)
    BB = B + [B7]

    CST = A("cst", [128, NCST], F32)
    BIAS16 = A("bias16", [1, 2048], BF16)
    ONES16 = A("ones16", [1, 128], BF16)
    IDENT16 = A("ident16", [128, 128], BF16)
    W = A("w", [128, 8, 2048], BF16)
    _xt_off = A.mark()
    XT = [A(f"xt{i}", [128, 8, 512], BF16) for i in range(2)]
    LNP = nc.alloc_sbuf_tensor_at("lnp_alias", [128, 2048], F32, offset=A.base + _xt_off + 8192)
    XN32 = [nc.alloc_sbuf_tensor_at(f"xn_alias{i}", [128, 1024], F32, offset=A.base + _xt_off + 4096 * i) for i in range(2)]
    OH16 = A("oh16", [128, 16, 512], BF16)
    OA16 = A("oa16", [128, 16, 512], BF16)
    phase_mark = A.mark()

    w_in_v = d_win.rearrange("(k p) c -> p k c", p=128)
    xT_v = d_xT.rearrange("(k p) t -> p k t", p=128)

    def load_w(dst_c0, src_ap, ncols):
        P.op("pool", lambda e: e.dma_start(out=W[:, :, dst_c0:dst_c0 + ncols], in_=src_ap),
             w=[f"W{dst_c0 // 512}" for _ in range(1)] if ncols == 512 else [f"W{i}" for i in range(dst_c0 // 512, (dst_c0 + ncols) // 512)],
             dma=f"w{dst_c0 // 512}")

    def load_bias(dst_c0, src_c0, ncols):
        P.op("pool", lambda e: e.dma_start(out=BIAS16[0:1, dst_c0:dst_c0 + ncols], in_=d_brow[0:1, src_c0:src_c0 + ncols]),
             w=[f"BIAS{dst_c0 // 512 + i}" for i in range(ncols // 512)], dma=f"bias{dst_c0 // 512}")

    def load_xt(buf, g):
        P.op("pool", lambda e: e.dma_start(out=XT[buf][:, :, :], in_=xT_v[:, :, g * 512:(g + 1) * 512]),
             w=[f"XT{buf}"], dma=f"xt{buf}")

    def proj_n(bank, buf, tl, wslot, bias_slot):
        def fn(e):
            ins = None
            for k in range(8):
                ins = e.matmul(BB[bank][:, :], lhsT=XT[buf][:, k, tl * 128:(tl + 1) * 128],
                               rhs=W[:, k, wslot * 512:(wslot + 1) * 512], start=(k == 0), stop=(bias_slot is None and k == 7))
            if bias_slot is not None:
                ins = e.matmul(BB[bank][:, :], lhsT=ONES16[0:1, :], rhs=BIAS16[0:1, bias_slot * 512:(bias_slot + 1) * 512],
                               start=False, stop=True)
            return ins
        rk = [f"XT{buf}", f"W{wslot}"] + ([f"BIAS{bias_slot}", "ONES16"] if bias_slot is not None else [])
        P.op("pe", fn, r=rk, w=[f"B{bank}"], est=2.1 if bias_slot is not None else 1.8)

    P.op("sp", lambda e: e.dma_start(out=CST[:, :], in_=d_cst[:, :]), w=["CST"], dma="cst")
    P.op("dve", lambda e: e.memset(ONES16[:, :], 1.0), w=["ONES16"])
    P.op("dve", lambda e: e.tensor_copy(out=IDENT16[:, :], in_=CST[:, C_ID:C_ID + 128]), r=["CST"], w=["IDENT16"])

    LB = [A(f"lb{d}", [128, 512], F32) for d in range(2)]
    OML = [A(f"oml{d}", [128, 512], F32) for d in range(2)]
    _of_off = A.mark()
    OF32 = A("of32", [128, 16, 512], F32)
    LBL = nc.alloc_sbuf_tensor_at("lbl_alias", [128, 2048], F32, offset=A.base + _of_off)
    G2s = [A(f"g2{i}", [128, 512], F32) for i in range(2)]
    SGT = A("sgt", [128, 512], F32)
    ONE32 = A("one32", [128, 8], F32)
    QST = A("qst", [128, 16, 512], BF16)
    SGc = [A(f"sgc{i}", [128, 512], BF16) for i in range(3)]
    A32 = A("a32", [128, 512], F32)
    K32 = A("k32", [128, 512], F32)
    EB32 = A("eb32", [128, 512], F32)
    ENB32 = A("enb32", [128, 512], F32)
    QE16 = A("qe16", [128, 512], BF16)
    KE16 = A("ke16", [128, 512], BF16)
    V16 = A("v16", [128, 512], BF16)
    QET16 = A("qet16", [128, 512], BF16)
    KET16 = A("ket16", [128, 512], BF16)
    AT16 = A("at16", [128, 512], BF16)
    ST32 = A("st32", [128, 512], F32)
    S16 = A("s16", [128, 512], BF16)
    DEC = A("dec", [128, 8], F32)
    O32 = [A(f"o32{i}", [128, 512], F32) for i in range(2)]
    ON32 = [A(f"on32{i}", [128, 512], F32) for i in range(2)]
    BQ, BI = O32[0], ON32[0]
    KBQ, KBI = ["O320"], [f"ON320{h_}" for h_ in range(4)]
    SS = [A(f"ss{i}", [128, 4], F32) for i in range(2)]
    RSTD = [A(f"rstd{i}", [128, 4], F32) for i in range(2)]

    P.op("sp", lambda e: e.dma_start(out=LBL[:, :], in_=d_lbl[:, 0:2048]), w=["LBL"], dma="lbl")
    P.op("dve", lambda e: e.memset(ONE32[:, :], 1.0), w=["ONE32"])
    P.op("sp", lambda e: e.dma_start(out=BQ[:, :], in_=d_lbl[:, 2048:2560]), w=KBQ, dma="bq")
    P.op("sp", lambda e: e.dma_start(out=BI[:, :], in_=d_lbl[:, 2560:3072]), w=KBI, dma="bi")
    for d in range(2):
        o = d * 1024
        P.op("dve", lambda e, o=o: e.tensor_sub(out=LBL[:, o:o + 512], in0=LBL[:, o:o + 512], in1=LBL[:, o + 512:o + 1024]),
             r=["LBL"], w=["LBL"])
        P.op("act", lambda e, o=o, d=d: e.activation(out=LB[d][:, :], in_=LBL[:, o:o + 512], func=AF.Sigmoid),
             r=["LBL"], w=[f"LB{d}"])
        P.op("dve", lambda e, d=d: e.tensor_scalar(out=OML[d][:, :], in0=LB[d][:, :], scalar1=-1.0, scalar2=1.0,
                                                    op0=ALU.mult, op1=ALU.add), r=[f"LB{d}"], w=[f"OML{d}"])

    if upto == 0:
        P.emit([])
        return nc

    A32 = [A32, A("a32b", [128, 512], F32)]
    K32 = [K32, A("k32b", [128, 512], F32)]
    V16 = [V16, A("v16b", [128, 512], BF16), A("v16c", [128, 512], BF16)]
    Q32 = [None, None]
    QET16 = [QET16, A("qet16b", [128, 512], BF16)]
    AT16 = [AT16, A("at16b", [128, 512], BF16)]
    DEC = [DEC, A("decb", [128, 8], F32), A("decc", [128, 8], F32)]
    EB32 = [EB32, A("eb32b", [128, 512], F32)]
    ENB32 = [ENB32, A("enb32b", [128, 512], F32)]
    QE16 = [QE16, A("qe16b", [128, 512], BF16)]
    KE16 = [KE16, A("ke16b", [128, 512], BF16)]
    KET16 = [KET16, A("ket16b", [128, 512], BF16)]
    T1s = [A(f"t1{i}", [128, 512], F32) for i in range(2)]

    def hg_pass(d):
        zc = 2560 if d == 0 else 3072
        tiles = list(range(0, T1)) if d == 0 else list(range(NT - 1, T0 - 1, -1))
        tiles = tiles[:ntl]
        gorder = []
        for T in tiles:
            if T // 4 not in gorder:
                gorder.append(T // 4)
        gbuf = {g: i % 2 for i, g in enumerate(gorder)}
        load_xt(gbuf[gorder[0]], gorder[0])
        if d == 0:
            load_w(512, w_in_v[:, :, zc:zc + 512], 512)
            load_bias(512, zc, 512)
            load_w(1024, w_in_v[:, :, 3584:4096], 512)
            load_bias(1024, 3584, 512)
            load_w(0, w_in_v[:, :, 2048:2560], 512)
            load_bias(0, 2048, 512)
            load_w(1536, w_in_v[:, :, 4096:4608], 512)
            load_bias(1536, 4096, 512)
        else:
            load_w(512, w_in_v[:, :, zc:zc + 512], 512)
            load_bias(512, zc, 512)
            load_w(0, w_in_v[:, :, 512:1024], 512)
        P.op("dve", lambda e: e.memset(ST32[:, :], 0.0), w=[f"ST32h{h_}" for h_ in range(4)])
        P.op("dve", lambda e: e.memset(S16[:, :], 0.0), w=["S16"])
        UD = C_UF if d == 0 else C_UB
        RD = C_RF if d == 0 else C_RB
        corder = (0, 1) if d == 0 else (1, 0)
        first_tile = {g: [t for t in tiles if t // 4 == g][0] for g in gorder}
        KVB = {0: 6, 1: 3}
        prevdec = [None]

        def a_steps(i):
            T = tiles[i]
            pp, p3 = i % 2, i % 3
            g = T // 4
            buf = gbuf[g]
            tl = T % 4
            local = T0 <= T < T1
            li = T - T0
            a32, k32, v16, q32 = A32[pp], K32[pp], V16[p3], Q32[pp]
            ka, kk, kv_, kq = f"A32{pp}", f"K32{pp}", f"V16{p3}", f"Q32{pp}"

            def s1():
                gi = gorder.index(g)
                if T == first_tile[g] and gi + 1 < len(gorder):
                    load_xt(gbuf[gorder[gi + 1]], gorder[gi + 1])
                proj_n(0, buf, tl, 1, 1)
                P.op("act", lambda e: e.activation(out=a32[:, :], in_=B[0][:, :], func=AF.Exp, scale=-1.0), r=["B0"], w=[ka])

            def s2():
                if local and d == 1:
                    return
                if local:
                    proj_n(7, buf, tl, 2, None)
                    P.op("dve", lambda e: e.tensor_tensor(out=OA16[:, li, :], in0=B7[:, :], in1=BI[:, :], op=ALU.add),
                         r=["B7"] + KBI, w=[f"OAV{li}"])
                else:
                    proj_n(7, buf, tl, 2, 2)
                    P.op("act", lambda e: e.activation(out=v16[:, :], in_=B7[:, :], func=AF.Identity,
                                                       scale=CST[:, C_TM + T:C_TM + T + 1]), r=["B7", "CST"], w=[kv_])

            def s3():
                if local and d == 0:
                    proj_n(0, buf, tl, 0, None)
                    P.op("dve", lambda e: e.tensor_tensor(out=QST[:, li, :], in0=B[0][:, :], in1=BQ[:, :], op=ALU.add),
                         r=["B0"] + KBQ, w=[f"QST{li}"])

            def s4():
                if local and d == 1:
                    proj_n(7, buf, tl, 3, 3)
                    P.op("act", lambda e: e.activation(out=SGT[:, :], in_=B7[:, :], func=AF.Exp, scale=-1.0), r=["B7"], w=["SGT"])
                    P.op("act", lambda e: e.activation(out=SGT[:, :], in_=SGT[:, :], func=AF.Ln, bias=ONE32[:, 0:1]),
                         r=["SGT", "ONE32"], w=["SGT"])
                    P.op("act", lambda e: e.activation(out=SGT[:, :], in_=SGT[:, :], func=AF.Exp, scale=-1.0), r=["SGT"], w=["SGT"])
                    P.op("dve", lambda e: e.tensor_tensor(out=SGT[:, :], in0=B7[:, :], in1=SGT[:, :], op=ALU.mult),
                         r=["B7", "SGT"], w=["SGT"])
                    P.op("pool", lambda e: e.tensor_tensor(out=SGc[p3][:, :], in0=SGT[:, :], in1=CST[:, C_GAIN:C_GAIN + 512], op=ALU.mult),
                         r=["SGT", "CST"], w=[f"SGc{p3}"])

            def s5():
                G2, ER32B = G2s[pp], T1s[pp]
                kg2, kt1 = f"G2{pp}", f"T1{pp}"
                P.op("act", lambda e: e.activation(out=G2[:, :], in_=a32[:, :], func=AF.Ln, bias=ONE32[:, 0:1]),
                     r=[ka, "ONE32"], w=[kg2])
                P.op("pool" if d == 0 else "dve", lambda e: e.tensor_tensor(out=ER32B[:, :], in0=a32[:, :], in1=LB[d][:, :], op=ALU.mult),
                     r=[ka, f"LB{d}"], w=[kt1])
                P.op("act", lambda e: e.activation(out=ER32B[:, :], in_=ER32B[:, :], func=AF.Ln, bias=ONE32[:, 0:1]),
                     r=[kt1, "ONE32"], w=[kt1])
                P.op("pool", lambda e: e.tensor_tensor(out=a32[:, :], in0=ER32B[:, :], in1=G2[:, :], op=ALU.subtract),
                     r=[kt1, kg2, ka], w=[ka])
                P.op("act", lambda e: e.activation(out=G2[:, :], in_=G2[:, :], func=AF.Exp, scale=-1.0), r=[kg2], w=[kg2])
                P.op("pool", lambda e: e.tensor_tensor(out=G2[:, :], in0=G2[:, :], in1=OML[d][:, :], op=ALU.mult),
                     r=[kg2, f"OML{d}"], w=[kg2])
                P.op("pool" if d == 0 else "dve", lambda e: e.tensor_tensor(out=k32[:, :], in0=OML[d][:, :], in1=G2[:, :], op=ALU.subtract),
                     r=[kg2, f"OML{d}"], w=[kk])
            return [s1, s2, s3, s4, s5]

        def b1_steps(i):
            T = tiles[i]
            pp, p3 = i % 2, i % 3
            local = T0 <= T < T1
            a32, k32, v16, q32 = A32[pp], K32[pp], V16[p3], Q32[pp]
            ka, kk, kv_, kq = f"A32{pp}", f"K32{pp}", f"V16{p3}", f"Q32{pp}"
            qet, at, dec = QET16[pp], AT16[pp], DEC[p3]
            kqet, kat, kdec = f"QET{pp}", f"AT{pp}", f"DEC{p3}"
            vref, vkey = (OA16[:, T - T0, :], f"OAV{T - T0}") if local else (v16, kv_)
            EB32_, ENB32_, QE16_, KE16_, KET16_ = EB32[pp], ENB32[pp], QE16[pp], KE16[pp], KET16[pp]
            keb, kenb, kqe, kke, kket = f"EB32{pp}", f"ENB32{pp}", f"QE16{pp}", f"KE16{pp}", f"KET16{pp}"

            def s1():
                def blast(e):
                    ins = None
                    for h in range(4):
                        ins = e.matmul(B[1][:, 2 * h:2 * h + 2], lhsT=a32[:, h * 128:(h + 1) * 128],
                                       rhs=CST[:, C_IND:C_IND + 2], start=True, stop=True)
                    return ins
                P.op("pe", blast, r=[ka, "CST"], w=["B1"], est=0.45)
                P.op("act", lambda e: e.activation(out=dec[:, :], in_=B[1][:, 0:8], func=AF.Exp), r=["B1"], w=[kdec])
                P.op("pe", lambda e: e.matmul(B[2][:, :], lhsT=CST[:, UD:UD + 128], rhs=a32[:, :], start=True, stop=True),
                     r=[ka, "CST"], w=["B2"], est=0.9)
                P.op("act", lambda e: e.activation(out=ENB32_[:, :], in_=B[2][:, :], func=AF.Exp, scale=-1.0),
                     r=["B2"], w=[kenb])
                if local:
                    P.op("act", lambda e: e.activation(out=EB32_[:, :], in_=B[2][:, :], func=AF.Exp), r=["B2"], w=[keb])

            def s2():
                P.op("pool" if local else "dve", lambda e: e.tensor_tensor(out=KE16_[:, :], in0=k32[:, :], in1=ENB32_[:, :], op=ALU.mult),
                     r=[kk, kenb], w=[kke])
                if local:
                    P.op("dve", lambda e: e.tensor_tensor(out=QE16_[:, :], in0=QST[:, T - T0, :], in1=EB32_[:, :], op=ALU.mult),
                         r=[f"QST{T - T0}", keb], w=[kqe])

            def s3():
                if not local:
                    return

                def trq(e):
                    ins = None
                    for h in range(4):
                        ins = e.matmul(B[4][:, h * 128:(h + 1) * 128], lhsT=QE16_[:, h * 128:(h + 1) * 128],
                                       rhs=IDENT16[:, :], start=True, stop=True)
                    return ins

                def trk(e):
                    ins = None
                    for h in range(4):
                        ins = e.matmul(B[4][:, h * 128:(h + 1) * 128], lhsT=KE16_[:, h * 128:(h + 1) * 128],
                                       rhs=IDENT16[:, :], start=True, stop=True)
                    return ins
                P.op("pe", trq, r=[kqe, "IDENT16"], w=["B4"], est=0.27)
                P.op("act", lambda e: e.activation(out=qet[:, :], in_=B[4][:, :], func=AF.Identity), r=["B4"], w=[kqet])
                P.op("pe", trk, r=[kke, "IDENT16"], w=["B4"], est=0.27)
                P.op("act", lambda e: e.activation(out=KET16_[:, :], in_=B[4][:, :], func=AF.Identity), r=["B4"], w=[kket])

            def s4():
                if not local:
                    return

                def amat(e):
                    ins = None
                    for h in range(4):
                        ins = e.matmul(B[4][:, h * 128:(h + 1) * 128], lhsT=KET16_[:, h * 128:(h + 1) * 128],
                                       rhs=qet[:, h * 128:(h + 1) * 128], start=True, stop=True)
                    return ins
                P.op("pe", amat, r=[kqet, kket], w=["B4"], est=0.27)
                P.op("dve", lambda e: e.tensor_tensor(
                    out=at[:, :].rearrange("p (h t) -> p h t", h=4),
                    in0=B[4][:, :].rearrange("p (h t) -> p h t", h=4),
                    in1=CST[:, UD:UD + 128].unsqueeze(1).broadcast_to([128, 4, 128]), op=ALU.mult),
                    r=["B4", "CST"], w=[kat])

            def s5():
                def kv(e):
                    ins = None
                    for c in range(2):
                        for h in range(4):
                            ins = e.matmul(B[KVB[c]][:, h * 128:(h + 1) * 128],
                                           lhsT=KE16_[64 * c:64 * c + 64, h * 128:(h + 1) * 128],
                                           rhs=vref[64 * c:64 * c + 64, h * 128:(h + 1) * 128], start=True, stop=True)
                    return ins
                P.op("pe", kv, r=[kke, vkey], w=["B6", "B3"], est=0.6)
            return [s1, s2, s3, s4, s5]

        def b2_steps(i):
            T = tiles[i]
            pp, p3 = i % 2, i % 3
            local = T0 <= T < T1
            li = T - T0
            v16 = V16[p3]
            kv_ = f"V16{p3}"
            qet, at, dec = QET16[pp], AT16[pp], DEC[p3]
            kqet, kat, kdec = f"QET{pp}", f"AT{pp}", f"DEC{p3}"
            vref, vkey = (OA16[:, li, :], f"OAV{li}") if local else (v16, kv_)

            def chunk(ci):
                c = corder[ci]
                if local:
                    def omat(e):
                        ins = None
                        if ci == 0:
                            for h in range(4):
                                ins = e.matmul(B[5][:, h * 128:(h + 1) * 128], lhsT=at[:, h * 128:(h + 1) * 128],
                                               rhs=vref[:, h * 128:(h + 1) * 128], start=(h == 0), stop=False,
                                               skip_group_check=True)
                        for h in range(4):
                            ins = e.matmul(B[5][64 * c:64 * c + 64, h * 128:(h + 1) * 128],
                                           lhsT=qet[:, h * 128 + 64 * c:h * 128 + 64 * c + 64],
                                           rhs=S16[:, h * 128:(h + 1) * 128], start=False, stop=True,
                                           skip_group_check=True, tile_position=(0, 64 * c))
                        return ins
                    P.op("pe", omat, r=[kat, vkey, kqet, "S16"], w=["B5"], est=0.5)
                pdec, pkey, pc = prevdec[0] if prevdec[0] is not None else (dec, kdec, c)
                for h in range(4):
                    P.op("dve", lambda e, h=h: e.scalar_tensor_tensor(
                        out=ST32[:, h * 128:(h + 1) * 128], in0=ST32[:, h * 128:(h + 1) * 128],
                        scalar=pdec[:, 2 * h + pc:2 * h + pc + 1], in1=B[KVB[c]][:, h * 128:(h + 1) * 128],
                        op0=ALU.mult, op1=ALU.add), r=[f"ST32h{h}", pkey, f"B{KVB[c]}"], w=[f"ST32h{h}"], est=0.4)
                P.op("dve", lambda e: e.tensor_tensor(
                    out=S16[:, :].rearrange("p (h t) -> p h t", h=4),
                    in0=ST32[:, :].rearrange("p (h t) -> p h t", h=4),
                    in1=dec[:, :].rearrange("p (h c) -> p h c", c=2)[:, :, c:c + 1].broadcast_to([128, 4, 128]), op=ALU.mult),
                    r=[f"ST32h{h_}" for h_ in range(4)] + [kdec], w=["S16"])
                prevdec[0] = (dec, kdec, c)

            def s1():
                chunk(0)

            def s2():
                chunk(1)

            def s3():
                if not local:
                    return
                if d == 0:
                    P.op("dve", lambda e: e.tensor_copy(out=OF32[:, li, :], in_=B[5][:, :]), r=["B5"], w=[f"OF{li}"])
                else:
                    o32, on32, ss, rstd = O32[pp], ON32[pp], SS[pp], RSTD[pp]
                    ko, kon, kss, krs = f"O32{pp}", f"ON32{pp}", f"SS{pp}", f"RSTD{pp}"
                    P.op("dve", lambda e: e.tensor_tensor(out=o32[:, :], in0=B[5][:, :], in1=OF32[:, li, :], op=ALU.add),
                         r=["B5", f"OF{li}"], w=[ko])
                    P.op("pool", lambda e: e.memset(ss[:, :], 0.0), w=[kss + str(h_) for h_ in range(4)])
                    for h in range(4):
                        P.op("act", lambda e, h=h: e.activation(out=on32[:, h * 128:(h + 1) * 128], in_=o32[:, h * 128:(h + 1) * 128],
                                                                func=AF.Square, accum_out=ss[:, h:h + 1]),
                             r=[ko], w=[kon + str(h), kss + str(h)], est=0.35)
                    P.op("dve", lambda e: e.tensor_scalar(out=rstd[:, :], in0=ss[:, :], scalar1=1.0 / 128.0, scalar2=RMS_EPS,
                                                          op0=ALU.mult, op1=ALU.add), r=[kss + str(h_) for h_ in range(4)], w=[krs], est=0.2)
                    P.op("act", lambda e: e.activation(out=rstd[:, :], in_=rstd[:, :], func=AF.Ln), r=[krs], w=[krs], est=0.2)
                    P.op("act", lambda e: e.activation(out=rstd[:, :], in_=rstd[:, :], func=AF.Exp, scale=-0.5), r=[krs], w=[krs], est=0.2)
                    P.op("pool", lambda e: e.tensor_tensor(
                        out=on32[:, :].rearrange("p (h t) -> p h t", h=4),
                        in0=o32[:, :].rearrange("p (h t) -> p h t", h=4),
                        in1=rstd[:, :].unsqueeze(2).broadcast_to([128, 4, 128]), op=ALU.mult),
                        r=[ko, krs] + [kon + str(h_) for h_ in range(4)], w=[kon + str(h_) for h_ in range(4)])
                    P.op("pool", lambda e: e.tensor_tensor(out=OH16[:, li, :], in0=on32[:, :], in1=SGc[p3][:, :], op=ALU.mult),
                         r=[kon + str(h_) for h_ in range(4)] + [f"SGc{p3}"], w=[f"OH{li}"])
            return [s1, s2, s3]

        n = len(tiles)
        nop = lambda: None
        for it in range(-2, n):
            b2 = b2_steps(it) if it >= 0 else [nop] * 3
            b1 = b1_steps(it + 1) if 0 <= it + 1 < n else [nop] * 5
            a = a_steps(it + 2) if it + 2 < n else [nop] * 5
            for st in (b2[0], a[0], b1[0], b2[1], a[1], b1[1], b2[2], a[2], b1[2], a[3], b1[3], a[4], b1[4]):
                st()

    hg_pass(0)
    if upto == 1:
        P.emit([])
        return nc
    hg_pass(1)
    if upto == 2:
        P.emit([])
        return nc
    load_xt(1, 0)
    load_w(1024, w_in_v[:, :, 1024:1536], 512)

    P.barrier()
    A.reset(phase_mark)

    KT = A("kt", [128, 4, NTOK], BF16)
    VA = A("va", [128, NT * 8, 65], BF16)
    QTG = A("qtg", [128, 4, 512], BF16)
    SGA = A("sga", [128, 4, 512], BF16)
    EBI = A("ebi", [128, 8, 5 * 128], BF16)
    EBH = [A(f"ebh{i}", [128, 7 * 128], BF16) for i in range(3)]
    STG = [A(f"stg{i}", [128, 7 * 128], F32) for i in range(2)]
    PX32s = [A(f"px32{i}", [128, 1024], F32) for i in range(3)]
    P16 = [A(f"p16{i}", [128, 1024], BF16) for i in range(3)]
    RDEN = A("rden", [128, 8], F32)
    ONE32a = A("one32a", [128, 8], F32)
    SGTa = A("sgta", [128, 512], F32)
    P.op("dve", lambda e: e.memset(ONE32a[:, :], 1.0), w=["ONE32a"])
    OA32 = A("oa32", [128, 512], F32)
    b_qk = CST[:, C_BQK:C_BQK + 8]

    load_bias(0, 1024, 512)
    load_w(1536, w_in_v[:, :, 0:512], 512)
    load_w(512, w_in_v[:, :, 1536:2048], 512)
    load_bias(512, 1536, 512)
    P.op("pool", lambda e: e.memset(VA[:, :, 64:65], 1.0), w=["VA1"])

    def load_tab(dst, dram_ap, nj, key, sidx):
        for h in range(8):
            sb = (sidx[0]) % 2
            sidx[0] += 1
            P.op("sp", lambda e, h=h, sb=sb: e.dma_start(out=STG[sb][:, 0:nj * 128], in_=dram_ap[h]),
                 w=[f"STG{sb}"], dma=f"stg{sb}")
            P.op("act", lambda e, h=h, sb=sb: e.activation(out=dst[:, h, 0:nj * 128], in_=STG[sb][:, 0:nj * 128], func=AF.Exp),
                 r=[f"STG{sb}"], w=[key])
    sidx = [0]
    load_tab(EBI, d_tabi, 5, "EBI", sidx)

    def proj_t(bank, buf, wc0, dst_fn, bias_col, keyw):
        def fn(e):
            ins = None
            for k in range(8):
                ins = e.matmul(BB[bank][:, :], lhsT=W[:, k, wc0:wc0 + 128], rhs=XT[buf][:, k, :],
                               start=(k == 0), stop=(k == 7))
            return ins
        P.op("pe", fn, r=[f"XT{buf}", f"W{wc0 // 512}"], w=[f"B{bank}"], est=1.8)
        P.op("act", lambda e: e.activation(out=dst_fn, in_=BB[bank][:, :], func=AF.Identity,
                                           bias=b_qk[:, bias_col:bias_col + 1]), r=[f"B{bank}", "CST"], w=[keyw])

    def a1_group(g, xbuf=1):
        if g > 0:
            load_xt(xbuf, g)
        for m4 in range(4):
            proj_t(7, xbuf, m4 * 128, KT[:, m4, g * 512:(g + 1) * 512], 4 + m4, f"KT{g}")
        for tl in range(4):
            T = g * 4 + tl
            if T < 2 or T > 21:
                continue
            proj_n(2, xbuf, tl, 2, 0)
            P.op("dve", lambda e, T=T: e.tensor_copy(out=VA[:, T * 8:(T + 1) * 8, 0:64], in_=B[2][:, :].rearrange("p (h c) -> p h c", h=8)),
                 r=["B2"], w=[f"VA{T}"])
    for g in range(3):
        a1_group(g, 1 - g % 2)

    w_out_v = d_wout.rearrange("(k p) c -> p k c", p=128)
    hcount = [0]
    ringc = [0]
    var_js = {0: list(range(-2, 4)), 1: list(range(-3, 4)), 14: list(range(-3, 4)), 15: list(range(-3, 3))}
    for g in range(1, 5):
        buf = 0
        load_xt(0, g)
        for m4 in range(4):
            proj_t(m4 % 2, buf, 1536 + m4 * 128, QTG[:, m4, :], m4, "QTG")
        for tl in range(4):
            proj_n(2, buf, tl, 1, 1)
            P.op("act", lambda e: e.activation(out=SGTa[:, :], in_=B[2][:, :], func=AF.Exp, scale=-1.0), r=["B2"], w=["SGTa"])
            P.op("act", lambda e: e.activation(out=SGTa[:, :], in_=SGTa[:, :], func=AF.Ln, bias=ONE32a[:, 0:1]),
                 r=["SGTa", "ONE32a"], w=["SGTa"])
            P.op("act", lambda e: e.activation(out=SGTa[:, :], in_=SGTa[:, :], func=AF.Exp, scale=-1.0), r=["SGTa"], w=["SGTa"])
            P.op("dve", lambda e, tl=tl: e.tensor_tensor(out=SGA[:, tl, :], in0=B[2][:, :], in1=SGTa[:, :], op=ALU.mult),
                 r=["B2", "SGTa"], w=[f"SGA{tl}"])
        if g + 2 <= 5:
            a1_group(g + 2)
        if g == 3:
            load_w(0, w_out_v[:, :, 0:512], 512)
            load_w(1024, w_out_v[:, :, 512:1024], 512)
            load_bias(1024, 4608, 1024)
        for tl in range(4):
            T = g * 4 + tl
            m = T - T0
            if m in var_js:
                js = var_js[m]
                bi = {0: 0, 1: 1, 14: 2, 15: 3}[m]
                border, j0 = True, -3
            else:
                js = list(range(-2, 3))
                border, j0 = False, -2
            nj = len(js)
            for h in range(8):
                m4, half = h // 2, h % 2
                hc_ = hcount[0]
                hcount[0] += 1
                pb = hc_ % 3
                p0 = 64 * half
                sb0 = 3 if hc_ % 2 == 0 else 0
                PX32 = PX32s[pb]
                if border:
                    rs = ringc[0] % 3
                    ringc[0] += 1
                    sb = sidx[0] % 2
                    sidx[0] += 1
                    P.op("sp", lambda e, h=h, sb=sb, bi=bi: e.dma_start(out=STG[sb][:, :], in_=d_tabb[bi][h]),
                         w=[f"STG{sb}"], dma=f"stg{sb}")
                    P.op("act", lambda e, sb=sb, rs=rs: e.activation(out=EBH[rs][:, :], in_=STG[sb][:, :], func=AF.Exp),
                         r=[f"STG{sb}"], w=[f"EBH{rs}"])
                    tabv, tkey = EBH[rs], f"EBH{rs}"
                else:
                    tabv, tkey = EBI[:, h, :], "EBI"

                def smat(e, js=js, m4=m4, p0=p0, T=T, tl=tl, sb0=sb0):
                    ins = None
                    for idx, j in enumerate(js):
                        bk = B[sb0 + idx // 4]
                        ins = e.matmul(bk[:, (idx % 4) * 128:(idx % 4 + 1) * 128],
                                       lhsT=KT[p0:p0 + 64, m4, (T + j) * 128:(T + j + 1) * 128],
                                       rhs=QTG[p0:p0 + 64, m4, tl * 128:(tl + 1) * 128], start=True, stop=True)
                    return ins
                P.op("pe", smat, r=["QTG"] + [f"KT{(T + j) // 4}" for j in js], w=[f"B{sb0}", f"B{sb0 + 1}"], est=0.5)
                n0 = min(nj, 4)
                P.op("act", lambda e, n0=n0, PX32=PX32, sb0=sb0: e.activation(out=PX32[:, 0:n0 * 128], in_=B[sb0][:, 0:n0 * 128],
                                                                             func=AF.Exp, scale=0.125),
                     r=[f"B{sb0}"], w=[f"PXa{pb}"])
                if nj > 4:
                    P.op("act", lambda e, nj=nj, PX32=PX32, sb0=sb0: e.activation(
                        out=PX32[:, 512:512 + (nj - 4) * 128], in_=B[sb0 + 1][:, 0:(nj - 4) * 128],
                        func=AF.Exp, scale=0.125), r=[f"B{sb0 + 1}"], w=[f"PXb{pb}"])
                c0 = (js[0] - j0) * 128
                P.op("dve", lambda e, pb=pb, nj=nj, c0=c0, tabv=tabv, PX32=PX32: e.tensor_tensor(
                    out=P16[pb][:, 0:nj * 128], in0=PX32[:, 0:nj * 128], in1=tabv[:, c0:c0 + nj * 128], op=ALU.mult),
                    r=[f"PXa{pb}", f"PXb{pb}", tkey], w=[f"P16{pb}"], est=0.95)

                def pv(e, js=js, pb=pb, h=h, T=T):
                    ins = None
                    bk = B[5 + h // 4]
                    hc = (h % 4) * 65
                    for idx, j in enumerate(js):
                        ins = e.matmul(bk[:, hc:hc + 65], lhsT=P16[pb][:, idx * 128:(idx + 1) * 128],
                                       rhs=VA[:, (T + j) * 8 + h, :], start=(idx == 0), stop=(idx == len(js) - 1))
                    return ins
                P.op("pe", pv, r=[f"P16{pb}", "VA1"] + [f"VA{T + j}" for j in js], w=[f"B{5 + h // 4}"], est=0.45)
            for hb in range(2):
                P.op("dve", lambda e, hb=hb: e.reciprocal(out=RDEN[:, 4 * hb:4 * hb + 4].unsqueeze(2),
                                                          in_=B[5 + hb][:, 0:260].rearrange("p (h c) -> p h c", h=4)[:, :, 64:65]),
                     r=[f"B{5 + hb}"], w=["RDEN"])
                P.op("dve", lambda e, hb=hb: e.tensor_tensor(
                    out=OA32[:, hb * 256:(hb + 1) * 256].rearrange("p (h c) -> p h c", h=4),
                    in0=B[5 + hb][:, 0:260].rearrange("p (h c) -> p h c", h=4)[:, :, 0:64],
                    in1=RDEN[:, 4 * hb:4 * hb + 4].unsqueeze(2).broadcast_to([128, 4, 64]), op=ALU.mult),
                    r=[f"B{5 + hb}", "RDEN"], w=[f"OA32{hb}"])
            P.op("pool", lambda e, m=m, tl=tl: e.tensor_tensor(out=OA16[:, m, :], in0=OA32[:, :], in1=SGA[:, tl, :], op=ALU.mult),
                 r=["OA320", "OA321", f"SGA{tl}"], w=[f"OA{m}"])

    P.op("sp", lambda e: e.dma_start(out=LNP[:, :], in_=d_lnp[:, :]), w=["XT1", "LNP"], dma="lnp")
    for m in range(2):
        P.op("sp", lambda e, m=m: e.dma_start(out=XN32[m][:, :], in_=d_xN[m * 128:(m + 1) * 128, :]),
             w=["XT0", f"XN{m}"] if m == 0 else [f"XN{m}"], r=[] if m == 0 else ["XT0"], dma=f"xn{m}")
    if upto == 3:
        P.emit([])
        return nc
    P.barrier()
    A.reset(phase_mark)

    ND = 3
    R32s = [A(f"r32{i}", [128, 1024], F32) for i in range(ND)]
    RN32s = [A(f"rn32{i}", [128, 1024], F32) for i in range(ND)]
    JKa = [A(f"jka{i}", [128, 1024], F32) for i in range(ND)]
    JKb = [A(f"jkb{i}", [128, 1024], F32) for i in range(ND)]
    OT16s = [A(f"ot16{i}", [128, 8, 128], BF16) for i in range(ND)]
    STATs = [A(f"stat{i}", [128, 8], F32) for i in range(ND)]
    XNs = [XN32[0], XN32[1]] + [A(f"xn{i}", [128, 1024], F32) for i in range(2, ND)]
    EPSC = A("epsc", [128, 8], F32)
    P.op("dve", lambda e: e.memset(EPSC[:, :], LN_EPS), w=["EPSC"])

    final = []
    def c_tile(m):
        xb = m % ND
        R32, RN32, OT16, STAT, XN, Ja, Jb = R32s[xb], RN32s[xb], OT16s[xb], STATs[xb], XNs[xb], JKa[xb], JKb[xb]
        sx = str(xb)
        yb = (0, 1) if m % 2 == 0 else (3, 4)
        tb = (7, 2) if m % 2 == 0 else (5, 6)
        if m >= 2:
            P.op("sp", lambda e: e.dma_start(out=XN[:, :], in_=d_xN[m * 128:(m + 1) * 128, :]),
                 w=[f"XN{xb}"], dma=f"xn{xb}")
        if debug and debug != 3:
            P.op("pool", lambda e: e.dma_start(out=d_dbg[m * 128:(m + 1) * 128, 0:512], in_=OA16[:, m, :]),
                 r=[f"OA{m}"], dma="dbg")
            if debug not in (2, 3):
                P.op("pool", lambda e: e.dma_start(out=d_dbg[m * 128:(m + 1) * 128, 512:1024], in_=OH16[:, m, :]),
                     r=[f"OH{m}"], dma="dbg")

        def trc(e):
            ins = None
            for k in range(8):
                src = OA16[:, m, k * 128:(k + 1) * 128] if k < 4 else OH16[:, m, (k - 4) * 128:(k - 3) * 128]
                dst = BB[tb[0]][:, k * 128:(k + 1) * 128] if k < 4 else BB[tb[1]][:, (k - 4) * 128:(k - 3) * 128]
                ins = e.matmul(dst, lhsT=src, rhs=IDENT16[:, :], start=True, stop=True)
            return ins
        P.op("pe", trc, r=[f"OA{m}", f"OH{m}", "IDENT16"], w=[f"B{tb[0]}", f"B{tb[1]}"], est=0.5)
        P.op("act", lambda e: e.activation(out=OT16[:, 0:4, :], in_=BB[tb[0]][:, :].rearrange("p (k t) -> p k t", k=4), func=AF.Identity),
             r=[f"B{tb[0]}"], w=["OT16a_" + sx])
        P.op("dve", lambda e: e.tensor_copy(out=OT16[:, 4:8, :], in_=BB[tb[1]][:, :].rearrange("p (k t) -> p k t", k=4)),
             r=[f"B{tb[1]}"], w=["OT16b_" + sx])
        for hf in range(2):
            def ymat(e, hf=hf):
                ins = None
                for k in range(8):
                    ins = e.matmul(BB[yb[hf]][:, :], lhsT=OT16[:, k, :], rhs=W[:, k, hf * 1024:hf * 1024 + 512],
                                   start=(k == 0), stop=False)
                ins = e.matmul(BB[yb[hf]][:, :], lhsT=ONES16[0:1, :], rhs=BIAS16[0:1, 1024 + hf * 512:1024 + (hf + 1) * 512],
                               start=False, stop=True)
                return ins
            P.op("pe", ymat, r=["OT16a_" + sx, "OT16b_" + sx, f"W{2 * hf}", f"BIAS{2 + hf}", "ONES16"], w=[f"B{yb[hf]}"], est=2.1)
            P.op("dve", lambda e, hf=hf: e.scalar_tensor_tensor(
                out=R32[:, hf * 512:(hf + 1) * 512], in0=XN[:, hf * 512:(hf + 1) * 512], scalar=ALPHA,
                in1=BB[yb[hf]][:, :], op0=ALU.mult, op1=ALU.add), r=[f"XN{xb}", f"B{yb[hf]}"], w=[f"R32{hf}_" + sx])
        P.op("pool", lambda e: e.memset(STAT[:, 0:2], 0.0), w=["STAT0_" + sx, "STAT1_" + sx], est=0.2)
        P.op("act", lambda e: e.activation(out=Ja[:, :], in_=R32[:, :], func=AF.Identity, accum_out=STAT[:, 0:1]),
             r=["R320_" + sx, "R321_" + sx], w=["JKa_" + sx, "STAT0_" + sx], est=1.2)
        P.op("act", lambda e: e.activation(out=Jb[:, :], in_=R32[:, :], func=AF.Square, accum_out=STAT[:, 1:2]),
             r=["R320_" + sx, "R321_" + sx], w=["JKb_" + sx, "STAT1_" + sx], est=1.2)
        P.op("dve", lambda e: e.tensor_scalar(out=STAT[:, 2:4], in0=STAT[:, 0:2], scalar1=1.0 / 1024.0, scalar2=None, op0=ALU.mult),
             r=["STAT0_" + sx, "STAT1_" + sx], w=["STAT2_" + sx], est=0.15)
        P.op("dve", lambda e: e.tensor_scalar(out=STAT[:, 4:5], in0=STAT[:, 2:3], scalar1=STAT[:, 2:3], scalar2=STAT[:, 3:4],
                                              op0=ALU.mult, op1=ALU.subtract), r=["STAT2_" + sx], w=["STAT4_" + sx], est=0.15)
        P.op("act", lambda e: e.activation(out=STAT[:, 5:6], in_=STAT[:, 4:5], func=AF.Ln, scale=-1.0, bias=EPSC[:, 0:1]),
             r=["STAT4_" + sx, "EPSC"], w=["STAT5_" + sx], est=0.15)
        P.op("act", lambda e: e.activation(out=STAT[:, 5:6], in_=STAT[:, 5:6], func=AF.Exp, scale=-0.5), r=["STAT5_" + sx], w=["STAT5_" + sx], est=0.15)
        P.op("dve", lambda e: e.scalar_tensor_tensor(out=STAT[:, 6:7], in0=STAT[:, 2:3], scalar=-1.0, in1=STAT[:, 5:6],
                                                      op0=ALU.mult, op1=ALU.mult), r=["STAT2_" + sx, "STAT5_" + sx], w=["STAT6_" + sx], est=0.15)
        P.op("act", lambda e: e.activation(out=RN32[:, :], in_=R32[:, :], func=AF.Identity, scale=STAT[:, 5:6], bias=STAT[:, 6:7]),
             r=["R320_" + sx, "R321_" + sx, "STAT5_" + sx, "STAT6_" + sx], w=["RN32_" + sx], est=1.2)
        P.op("dve", lambda e: e.tensor_tensor(out=RN32[:, :], in0=RN32[:, :], in1=LNP[:, 0:1024], op=ALU.mult),
             r=["RN32_" + sx, "LNP"], w=["RN32_" + sx], est=1.4)
        P.op("pool", lambda e: e.tensor_tensor(out=RN32[:, :], in0=RN32[:, :], in1=LNP[:, 1024:2048], op=ALU.add),
             r=["RN32_" + sx, "LNP"], w=["RN32_" + sx], est=2.5)
        tok = P.op("sp", lambda e: e.dma_start(out=d_out[m * 128:(m + 1) * 128, :], in_=RN32[:, :]),
                   r=["RN32_" + sx], dma=f"st{xb}")
        final.append(tok)
    for m in range(16):
        c_tile(m)
    if debug:
        final.extend(o for seg in P.segs for o in seg if o.dma == "dbg")
    P.emit(final)
    return nc


def _const_block():
    c = np.zeros((128, NCST), np.float32)
    c[:, C_ID:C_ID + 128] = np.eye(128, dtype=np.float32)
    s = np.arange(128)[:, None]
    t = np.arange(128)[None, :]
    same = (s // 64) == (t // 64)
    c[:, C_UF:C_UF + 128] = (same & (s <= t)).astype(np.float32)
    c[:, C_RF:C_RF + 128] = (same & (s > t)).astype(np.float32)
    c[:, C_UB:C_UB + 128] = (same & (s >= t)).astype(np.float32)
    c[:, C_RB:C_RB + 128] = (same & (s < t)).astype(np.float32)
    c[:, C_IND + 0] = (np.arange(128) < 64).astype(np.float32)
    c[:, C_IND + 1] = (np.arange(128) >= 64).astype(np.float32)
    return c


def _att_table(rpb, R0, m, js):
    p = np.arange(128)
    kr_i, kc = p // 64, p % 64
    qr_i, qc = p // 64, p % 64
    out = np.full((8, 128, len(js), 128), NEG, np.float32)
    qr = R0 + 2 * m + qr_i
    rs = np.clip(qr - 4, 0, 120)
    cs = np.clip(qc - 8, 0, 48)
    for ji, j in enumerate(js):
        kr = R0 + 2 * m + 2 * j + kr_i
        ok = (kr[:, None] >= 0) & (kr[:, None] < 128) & (kr[:, None] >= rs[None, :]) & (kr[:, None] < rs[None, :] + 8) \
            & (kc[:, None] >= cs[None, :]) & (kc[:, None] < cs[None, :] + 16)
        dr = np.clip(kr[:, None] - qr[None, :] + 7, 0, 14)
        dc = np.clip(kc[:, None] - qc[None, :], -15, 15) + 15
        g = rpb[:, dr, dc]
        out[:, :, ji, :] = np.where(ok[None], g, NEG)
    return out


_NC_CACHE = {}


def _prep_inputs(x, w_in, b_in, rpb, lb_fwd_logits, lb_bwd_logits, hg_norm_gain, w_out, b_out, ln_gain, ln_bias):
    x = np.asarray(x, np.float32)
    w_in0 = np.ascontiguousarray(np.asarray(w_in, np.float32)[0])
    w_out0 = np.ascontiguousarray(np.asarray(w_out, np.float32)[0])
    b_in0 = np.asarray(b_in, np.float32)[0]
    rpb0 = np.asarray(rpb, np.float32)[0]
    brow = np.concatenate([b_in0, np.asarray(b_out, np.float32)[0]])[None, :].astype(np.float32)
    lbf = np.asarray(lb_fwd_logits, np.float32)
    lbb = np.asarray(lb_bwd_logits, np.float32)
    lbl = np.ascontiguousarray(np.broadcast_to(
        np.concatenate([lbf[0], lbf[1], lbb[0], lbb[1], b_in0[2048:2560], b_in0[3584:4096]])[None, :], (128, 3072))).astype(np.float32)
    lnp = np.ascontiguousarray(np.broadcast_to(
        np.concatenate([np.asarray(ln_gain, np.float32)[0], np.asarray(ln_bias, np.float32)[0]])[None, :], (128, 2048))).astype(np.float32)
    cbase = _const_block()
    cbase[:, C_GAIN:C_GAIN + 512] = np.asarray(hg_norm_gain, np.float32)[0][None, :]
    cbase[:, C_BQK:C_BQK + 4] = b_in0[0:512].reshape(4, 128).T
    cbase[:, C_BQK + 4:C_BQK + 8] = b_in0[512:1024].reshape(4, 128).T
    in_maps = []
    for c in range(NCORES):
        b, s = c // 4, c % 4
        t0 = s * NLOC
        lo, hi = max(0, t0 - HALO), min(SEQ, t0 + NLOC + HALO)
        xT = np.zeros((DM, NTOK), np.float32)
        xT[:, lo - (t0 - HALO):hi - (t0 - HALO)] = x[b, lo:hi, :].T
        cst = cbase.copy()
        tpos = t0 - HALO + np.arange(NT)[None, :] * 128 + np.arange(128)[:, None]
        cst[:, C_TM:C_TM + NT] = ((tpos >= 0) & (tpos < SEQ)).astype(np.float32)
        R0 = 32 * s
        tabi = _att_table(rpb0, R0, 7, list(range(-2, 3))).reshape(8, 128, 5 * 128)
        tabb = np.stack([_att_table(rpb0, R0, m, list(range(-3, 4))).reshape(8, 128, 7 * 128) for m in (0, 1, 14, 15)])
        in_maps.append({
            "xT": xT, "xN": np.ascontiguousarray(x[b, t0:t0 + NLOC, :]), "w_in": w_in0, "w_out": w_out0,
            "brow": brow, "cst": cst, "lbl": lbl, "lnp": lnp,
            "tabi": np.ascontiguousarray(tabi), "tabb": np.ascontiguousarray(tabb),
        })
    return in_maps


def kernel(x, w_in, b_in, rpb, lb_fwd_logits, lb_bwd_logits, hg_norm_gain, w_out, b_out, ln_gain, ln_bias):
    in_maps = _prep_inputs(x, w_in, b_in, rpb, lb_fwd_logits, lb_bwd_logits, hg_norm_gain, w_out, b_out, ln_gain, ln_bias)
    if "nc" not in _NC_CACHE:
        _NC_CACHE["nc"] = build_nc()
    res = run_bass_kernel_spmd(_NC_CACHE["nc"], in_maps, core_ids=list(range(NCORES)))
    out = np.zeros((2, SEQ, DM), np.float32)
    for c in range(NCORES):
        b, s = c // 4, c % 4
        out[b, s * NLOC:(s + 1) * NLOC, :] = res.results[c]["out"]
    return out
```

```python
import numpy as np
from contextlib import ExitStack
import concourse.bass as bass
import concourse.mybir as mybir
from concourse.bass_utils import run_bass_kernel_spmd

F32 = mybir.dt.float32
BF16 = mybir.dt.bfloat16
AF = mybir.ActivationFunctionType
ALU = mybir.AluOpType

NCORES = 8
SEQ = 8192
DM = 1024
HALO = 512
NLOC = 2048
NTOK = NLOC + 2 * HALO
NT = NTOK // 128
T0 = HALO // 128
T1 = T0 + NLOC // 128
ALPHA = 2.0 ** 0.25
LN_EPS = 1e-5
RMS_EPS = 1e-6
NEG = -30000.0
SKIP = set()

C_ID, C_UF, C_RF, C_UB, C_RB, C_IND = 0, 128, 256, 384, 512, 640
C_GAIN = 648
C_TM = C_GAIN + 512
C_BQK = C_TM + 24
NCST = C_BQK + 8


class _Op:
    __slots__ = ("e", "fn", "deps", "dma", "est", "lat", "idx", "tok", "start", "end", "wk")

    def __init__(self, e, fn, deps, dma, est, lat, idx):
        self.e, self.fn, self.deps, self.dma, self.est, self.lat, self.idx = e, fn, deps, dma, est, lat, idx
        self.tok = None
        self.start = self.end = 0.0


class Prog:
    EST = {"pe": 0.5, "act": 0.62, "dve": 0.7, "pool": 1.3, "sp": 0.1}
    HOP = 0.35
    SLACK = 0.3

    def __init__(self, nc):
        self.nc = nc
        self.eng = {"pe": nc.tensor, "act": nc.scalar, "dve": nc.vector, "pool": nc.gpsimd, "sp": nc.sync}
        self.segs = [[]]
        self.last_w = {}
        self.readers = {}
        self.n = 0
    mute = False

    def op(self, e, fn, r=(), w=(), dma=None, est=None):
        if self.mute:
            return None
        deps = []
        seen = set()

        def add(o):
            if o is not None and id(o) not in seen:
                seen.add(id(o))
                deps.append(o)
        for k in r:
            add(self.last_w.get(k))
        for k in w:
            add(self.last_w.get(k))
            for t in self.readers.get(k, ()):
                add(t)
        if dma is not None:
            lat = est if est is not None else (6.0 if e == "pool" else 3.0)
            dur = 1.0 if e == "pool" else 0.1
        else:
            dur = est if est is not None else self.EST[e]
            lat = dur
        o = _Op(e, fn, deps, dma, dur, lat, self.n)
        o.wk = tuple(w)
        self.n += 1
        self.segs[-1].append(o)
        for k in r:
            self.readers.setdefault(k, []).append(o)
        for k in w:
            self.last_w[k] = o
            self.readers[k] = []
        return o

    def barrier(self):
        self.segs.append([])

    def _schedule(self, seg):
        inseg = {id(o) for o in seg}
        ndep = {}
        users = {}
        for o in seg:
            c = 0
            for d_ in o.deps:
                if id(d_) in inseg:
                    c += 1
                    users.setdefault(id(d_), []).append(o)
            ndep[id(o)] = c
        tail = {}
        for o in sorted(seg, key=lambda o_: -o_.idx):
            t = 0.0
            for u in users.get(id(o), ()):
                t = max(t, tail[id(u)] + (0.0 if (u.e == o.e and o.dma is None) else self.HOP))
            tail[id(o)] = t + o.lat
        free = {e: 0.0 for e in self.eng}
        ready = {}
        cand = [o for o in seg if ndep[id(o)] == 0]
        for o in cand:
            ready[id(o)] = 0.0
        order = {e: [] for e in self.eng}
        done = 0
        while cand:
            tmin = min(max(free[o.e], ready[id(o)]) for o in cand)
            best, bkey = None, None
            for o in cand:
                st = max(free[o.e], ready[id(o)])
                if st > tmin + self.SLACK:
                    continue
                key = (-tail[id(o)], o.idx)
                if bkey is None or key < bkey:
                    best, bkey = o, key
            o = best
            cand.remove(o)
            o.start = max(free[o.e], ready[id(o)])
            free[o.e] = o.start + o.est
            o.end = o.start + o.lat
            order[o.e].append(o)
            done += 1
            for u in users.get(id(o), ()):
                t = o.end + (0.0 if (u.e == o.e and o.dma is None) else self.HOP)
                if t > ready.get(id(u), 0.0):
                    ready[id(u)] = t
                ndep[id(u)] -= 1
                if ndep[id(u)] == 0:
                    cand.append(u)
        assert done == len(seg), (done, len(seg))
        return order

    def emit(self, final_ops):
        nc = self.nc
        plan = {e: [] for e in self.eng}
        count = {e: 0 for e in self.eng}
        dma_cnt = {}
        waited = {e: {} for e in self.eng}
        semnames = set()
        for seg in self.segs:
            order = self._schedule(seg)
            allops = sorted(seg, key=lambda o: (o.start, o.idx))
            for o in allops:
                if o.dma is None:
                    count[o.e] += 1
                    o.tok = ("E_" + o.e, count[o.e], o.e)
                else:
                    dma_cnt[o.dma] = dma_cnt.get(o.dma, 0) + 1
                    o.tok = ("D_" + o.dma, 16 * dma_cnt[o.dma], None)
                semnames.add(o.tok[0])
            for e in self.eng:
                c0 = count[e] - sum(1 for o in order[e] if o.dma is None)
                for o in order[e]:
                    if o.dma is None:
                        c0 += 1
                        o.tok = ("E_" + e, c0, e)
            for e in self.eng:
                for o in order[e]:
                    need = {}
                    for d_ in o.deps:
                        s_, v, se = d_.tok
                        if se == e and e == "pe" and o.dma is None:
                            continue
                        if v > need.get(s_, 0):
                            need[s_] = v
                    waits = []
                    for s_, v in need.items():
                        if waited[e].get(s_, 0) >= v:
                            continue
                        waited[e][s_] = v
                        waits.append((s_, v))
                    plan[e].append((waits, o.fn, o.tok))
            toks = [("E_" + e, count[e]) for e in self.eng if count[e] > 0] + [("D_" + k, 16 * c) for k, c in dma_cnt.items()]
            for e in self.eng:
                waits = []
                for s_, v in toks:
                    if (s_ == "E_" + e and e == "pe") or waited[e].get(s_, 0) >= v:
                        continue
                    waited[e][s_] = v
                    waits.append((s_, v))
                if waits:
                    plan[e].append((waits, None, None))
        final_tokens = [o.tok for o in final_ops]
        with ExitStack() as es:
            sems = {n_: es.enter_context(nc.semaphore(n_)) for n_ in sorted(semnames)}
            block = es.enter_context(nc.Block())

            def run(e, eng):
                for waits, fn, tok in plan[e]:
                    for s_, v in waits:
                        eng.wait_ge(sems[s_], v)
                    if fn is None:
                        continue
                    ins = fn(eng)
                    ins.then_inc(sems[tok[0]], 16 if tok[2] is None else 1)
                if e == "sp":
                    for s_, v, _ in final_tokens:
                        eng.wait_ge(sems[s_], v)

            @block.tensor
            def _(eng):
                run("pe", eng)

            @block.scalar
            def _(eng):
                run("act", eng)

            @block.vector
            def _(eng):
                run("dve", eng)

            @block.gpsimd
            def _(eng):
                run("pool", eng)

            @block.sync
            def _(eng):
                run("sp", eng)


class Alloc:
    def __init__(self, nc, base, size):
        self.nc, self.base, self.size, self.off, self.n = nc, base, size, 0, 0

    def mark(self):
        return self.off

    def reset(self, m):
        self.off = m

    def __call__(self, name, shape, dtype):
        nb = int(np.prod(shape[1:])) * (4 if dtype == F32 else 2)
        nb = (nb + 63) // 64 * 64
        assert self.off + nb <= self.size, (name, self.off, nb, self.size)
        self.n += 1
        t = self.nc.alloc_sbuf_tensor_at(f"{name}_{self.n}", list(shape), dtype, offset=self.base + self.off)
        self.off += nb
        return t


def build_nc(debug=False, upto=9, ntl=99):
    nc = bass.Bass("TRN2", target_bir_lowering=False)
    d_xT = nc.dram_tensor("xT", [DM, NTOK], F32, kind="ExternalInput").ap()
    d_xN = nc.dram_tensor("xN", [NLOC, DM], F32, kind="ExternalInput").ap()
    d_win = nc.dram_tensor("w_in", [DM, 4608], F32, kind="ExternalInput").ap()
    d_wout = nc.dram_tensor("w_out", [DM, DM], F32, kind="ExternalInput").ap()
    d_brow = nc.dram_tensor("brow", [1, 4608 + 1024], F32, kind="ExternalInput").ap()
    d_cst = nc.dram_tensor("cst", [128, NCST], F32, kind="ExternalInput").ap()
    d_lbl = nc.dram_tensor("lbl", [128, 6 * 512], F32, kind="ExternalInput").ap()
    d_lnp = nc.dram_tensor("lnp", [128, 2 * 1024], F32, kind="ExternalInput").ap()
    d_tabi = nc.dram_tensor("tabi", [8, 128, 5 * 128], F32, kind="ExternalInput").ap()
    d_tabb = nc.dram_tensor("tabb", [4, 8, 128, 7 * 128], F32, kind="ExternalInput").ap()
    d_out = nc.dram_tensor("out", [NLOC, DM], F32, kind="ExternalOutput").ap()
    d_dbg = None
    if debug:
        d_dbg = nc.dram_tensor("dbg", [NLOC, 1024], F32, kind="ExternalOutput").ap()

    arena = nc.alloc_sbuf_tensor("arena", [128, 207 * 1024], mybir.dt.uint8)
    base = nc.lookup_mloc(arena).addr
    A = Alloc(nc, base, 207 * 1024)
    P = Prog(nc)

    B = [nc.alloc_psum_tensor(f"pb{i}", [128, 512], F32) for i in range(7)]
    B7 = nc.alloc_psum_tensor("pb7", [128, 512], F32)
    BB = B + [B7]

    CST = A("cst", [128, NCST], F32)
    BIAS16 = A("bias16", [1, 2048], BF16)
    ONES16 = A("ones16", [1, 128], BF16)
    IDENT16 = A("ident16", [128, 128], BF16)
    W = A("w", [128, 8, 2048], BF16)
    _xt_off = A.mark()
    XT = [A(f"xt{i}", [128, 8, 512], BF16) for i in range(2)]
    LNP = nc.alloc_sbuf_tensor_at("lnp_alias", [128, 2048], F32, offset=A.base + _xt_off + 8192)
    XN32 = [nc.alloc_sbuf_tensor_at(f"xn_alias{i}", [128, 1024], F32, offset=A.base + _xt_off + 4096 * i) for i in range(2)]
    OH16 = A("oh16", [128, 16, 512], BF16)
    OA16 = A("oa16", [128, 16, 512], BF16)
    phase_mark = A.mark()

    w_in_v = d_win.rearrange("(k p) c -> p k c", p=128)
    xT_v = d_xT.rearrange("(k p) t -> p k t", p=128)

    def load_w(dst_c0, src_ap, ncols):
        P.op("pool", lambda e: e.dma_start(out=W[:, :, dst_c0:dst_c0 + ncols], in_=src_ap),
             w=[f"W{dst_c0 // 512}" for _ in range(1)] if ncols == 512 else [f"W{i}" for i in range(dst_c0 // 512, (dst_c0 + ncols) // 512)],
             dma=f"w{dst_c0 // 512}")

    def load_bias(dst_c0, src_c0, ncols):
        P.op("pool", lambda e: e.dma_start(out=BIAS16[0:1, dst_c0:dst_c0 + ncols], in_=d_brow[0:1, src_c0:src_c0 + ncols]),
             w=[f"BIAS{dst_c0 // 512 + i}" for i in range(ncols // 512)], dma=f"bias{dst_c0 // 512}")

    def load_xt(buf, g):
        P.op("pool", lambda e: e.dma_start(out=XT[buf][:, :, :], in_=xT_v[:, :, g * 512:(g + 1) * 512]),
             w=[f"XT{buf}"], dma=f"xt{buf}")

    def proj_n(bank, buf, tl, wslot, bias_slot):
        def fn(e):
            ins = None
            for k in range(8):
                ins = e.matmul(BB[bank][:, :], lhsT=XT[buf][:, k, tl * 128:(tl + 1) * 128],
                               rhs=W[:, k, wslot * 512:(wslot + 1) * 512], start=(k == 0), stop=(bias_slot is None and k == 7))
            if bias_slot is not None:
                ins = e.matmul(BB[bank][:, :], lhsT=ONES16[0:1, :], rhs=BIAS16[0:1, bias_slot * 512:(bias_slot + 1) * 512],
                               start=False, stop=True)
            return ins
        rk = [f"XT{buf}", f"W{wslot}"] + ([f"BIAS{bias_slot}", "ONES16"] if bias_slot is not None else [])
        P.op("pe", fn, r=rk, w=[f"B{bank}"], est=2.1 if bias_slot is not None else 1.8)

    P.op("sp", lambda e: e.dma_start(out=CST[:, :], in_=d_cst[:, :]), w=["CST"], dma="cst")
    P.op("dve", lambda e: e.memset(ONES16[:, :], 1.0), w=["ONES16"])
    P.op("dve", lambda e: e.tensor_copy(out=IDENT16[:, :], in_=CST[:, C_ID:C_ID + 128]), r=["CST"], w=["IDENT16"])

    LB = [A(f"lb{d}", [128, 512], F32) for d in range(2)]
    OML = [A(f"oml{d}", [128, 512], F32) for d in range(2)]
    _of_off = A.mark()
    OF32 = A("of32", [128, 16, 512], F32)
    LBL = nc.alloc_sbuf_tensor_at("lbl_alias", [128, 2048], F32, offset=A.base + _of_off)
    G2s = [A(f"g2{i}", [128, 512], F32) for i in range(2)]
    SGT = A("sgt", [128, 512], F32)
    ONE32 = A("one32", [128, 8], F32)
    QST = A("qst", [128, 16, 512], BF16)
    SGc = [A(f"sgc{i}", [128, 512], BF16) for i in range(3)]
    A32 = A("a32", [128, 512], F32)
    K32 = A("k32", [128, 512], F32)
    EB32 = A("eb32", [128, 512], F32)
    ENB32 = A("enb32", [128, 512], F32)
    QE16 = A("qe16", [128, 512], BF16)
    KE16 = A("ke16", [128, 512], BF16)
    V16 = A("v16", [128, 512], BF16)
    QET16 = A("qet16", [128, 512], BF16)
    KET16 = A("ket16", [128, 512], BF16)
    AT16 = A("at16", [128, 512], BF16)
    ST32 = A("st32", [128, 512], F32)
    S16 = A("s16", [128, 512], BF16)
    DEC = A("dec", [128, 8], F32)
    O32 = [A(f"o32{i}", [128, 512], F32) for i in range(2)]
    ON32 = [A(f"on32{i}", [128, 512], F32) for i in range(2)]
    BQ, BI = O32[0], ON32[0]
    KBQ, KBI = ["O320"], [f"ON320{h_}" for h_ in range(4)]
    SS = [A(f"ss{i}", [128, 4], F32) for i in range(2)]
    RSTD = [A(f"rstd{i}", [128, 4], F32) for i in range(2)]

    P.op("sp", lambda e: e.dma_start(out=LBL[:, :], in_=d_lbl[:, 0:2048]), w=["LBL"], dma="lbl")
    P.op("dve", lambda e: e.memset(ONE32[:, :], 1.0), w=["ONE32"])
    P.op("sp", lambda e: e.dma_start(out=BQ[:, :], in_=d_lbl[:, 2048:2560]), w=KBQ, dma="bq")
    P.op("sp", lambda e: e.dma_start(out=BI[:, :], in_=d_lbl[:, 2560:3072]), w=KBI, dma="bi")
    for d in range(2):
        o = d * 1024
        P.op("dve", lambda e, o=o: e.tensor_sub(out=LBL[:, o:o + 512], in0=LBL[:, o:o + 512], in1=LBL[:, o + 512:o + 1024]),
             r=["LBL"], w=["LBL"])
        P.op("act", lambda e, o=o, d=d: e.activation(out=LB[d][:, :], in_=LBL[:, o:o + 512], func=AF.Sigmoid),
             r=["LBL"], w=[f"LB{d}"])
        P.op("dve", lambda e, d=d: e.tensor_scalar(out=OML[d][:, :], in0=LB[d][:, :], scalar1=-1.0, scalar2=1.0,
                                                    op0=ALU.mult, op1=ALU.add), r=[f"LB{d}"], w=[f"OML{d}"])

    if upto == 0:
        P.emit([])
        return nc

    A32 = [A32, A("a32b", [128, 512], F32)]
    K32 = [K32, A("k32b", [128, 512], F32)]
    V16 = [V16, A("v16b", [128, 512], BF16), A("v16c", [128, 512], BF16)]
    Q32 = [None, None]
    QET16 = [QET16, A("qet16b", [128, 512], BF16)]
    AT16 = [AT16, A("at16b", [128, 512], BF16)]
    DEC = [DEC, A("decb", [128, 8], F32), A("decc", [128, 8], F32)]
    EB32 = [EB32, A("eb32b", [128, 512], F32)]
    ENB32 = [ENB32, A("enb32b", [128, 512], F32)]
    QE16 = [QE16, A("qe16b", [128, 512], BF16)]
    KE16 = [KE16, A("ke16b", [128, 512], BF16)]
    KET16 = [KET16, A("ket16b", [128, 512], BF16)]
    T1s = [A(f"t1{i}", [128, 512], F32) for i in range(2)]

    def hg_pass(d):
        zc = 2560 if d == 0 else 3072
        tiles = list(range(0, T1)) if d == 0 else list(range(NT - 1, T0 - 1, -1))
        tiles = tiles[:ntl]
        gorder = []
        for T in tiles:
            if T // 4 not in gorder:
                gorder.append(T // 4)
        gbuf = {g: i % 2 for i, g in enumerate(gorder)}
        load_xt(gbuf[gorder[0]], gorder[0])
        if d == 0:
            load_w(512, w_in_v[:, :, zc:zc + 512], 512)
            load_bias(512, zc, 512)
            load_w(1024, w_in_v[:, :, 3584:4096], 512)
            load_bias(1024, 3584, 512)
            load_w(0, w_in_v[:, :, 2048:2560], 512)
            load_bias(0, 2048, 512)
            load_w(1536, w_in_v[:, :, 4096:4608], 512)
            load_bias(1536, 4096, 512)
        else:
            load_w(512, w_in_v[:, :, zc:zc + 512], 512)
            load_bias(512, zc, 512)
            load_w(0, w_in_v[:, :, 512:1024], 512)
        P.op("dve", lambda e: e.memset(ST32[:, :], 0.0), w=[f"ST32h{h_}" for h_ in range(4)])
        P.op("dve", lambda e: e.memset(S16[:, :], 0.0), w=["S16"])
        UD = C_UF if d == 0 else C_UB
        RD = C_RF if d == 0 else C_RB
        corder = (0, 1) if d == 0 else (1, 0)
        first_tile = {g: [t for t in tiles if t // 4 == g][0] for g in gorder}
        KVB = {0: 6, 1: 3}
        prevdec = [None]

        def a_steps(i):
            T = tiles[i]
            pp, p3 = i % 2, i % 3
            g = T // 4
            buf = gbuf[g]
            tl = T % 4
            local = T0 <= T < T1
            li = T - T0
            a32, k32, v16, q32 = A32[pp], K32[pp], V16[p3], Q32[pp]
            ka, kk, kv_, kq = f"A32{pp}", f"K32{pp}", f"V16{p3}", f"Q32{pp}"

            def s1():
                gi = gorder.index(g)
                if T == first_tile[g] and gi + 1 < len(gorder):
                    load_xt(gbuf[gorder[gi + 1]], gorder[gi + 1])
                proj_n(0, buf, tl, 1, 1)
                P.op("act", lambda e: e.activation(out=a32[:, :], in_=B[0][:, :], func=AF.Exp, scale=-1.0), r=["B0"], w=[ka])

            def s2():
                if local and d == 1:
                    return
                if local:
                    proj_n(7, buf, tl, 2, None)
                    P.op("dve", lambda e: e.tensor_tensor(out=OA16[:, li, :], in0=B7[:, :], in1=BI[:, :], op=ALU.add),
                         r=["B7"] + KBI, w=[f"OAV{li}"])
                else:
                    proj_n(7, buf, tl, 2, 2)
                    P.op("act", lambda e: e.activation(out=v16[:, :], in_=B7[:, :], func=AF.Identity,
                                                       scale=CST[:, C_TM + T:C_TM + T + 1]), r=["B7", "CST"], w=[kv_])

            def s3():
                if local and d == 0:
                    proj_n(0, buf, tl, 0, None)
                    P.op("dve", lambda e: e.tensor_tensor(out=QST[:, li, :], in0=B[0][:, :], in1=BQ[:, :], op=ALU.add),
                         r=["B0"] + KBQ, w=[f"QST{li}"])

            def s4():
                if local and d == 1:
                    proj_n(7, buf, tl, 3, 3)
                    P.op("act", lambda e: e.activation(out=SGT[:, :], in_=B7[:, :], func=AF.Exp, scale=-1.0), r=["B7"], w=["SGT"])
                    P.op("act", lambda e: e.activation(out=SGT[:, :], in_=SGT[:, :], func=AF.Ln, bias=ONE32[:, 0:1]),
                         r=["SGT", "ONE32"], w=["SGT"])
                    P.op("act", lambda e: e.activation(out=SGT[:, :], in_=SGT[:, :], func=AF.Exp, scale=-1.0), r=["SGT"], w=["SGT"])
                    P.op("dve", lambda e: e.tensor_tensor(out=SGT[:, :], in0=B7[:, :], in1=SGT[:, :], op=ALU.mult),
                         r=["B7", "SGT"], w=["SGT"])
                    P.op("pool", lambda e: e.tensor_tensor(out=SGc[p3][:, :], in0=SGT[:, :], in1=CST[:, C_GAIN:C_GAIN + 512], op=ALU.mult),
                         r=["SGT", "CST"], w=[f"SGc{p3}"])

            def s5():
                G2, ER32B = G2s[pp], T1s[pp]
                kg2, kt1 = f"G2{pp}", f"T1{pp}"
                P.op("act", lambda e: e.activation(out=G2[:, :], in_=a32[:, :], func=AF.Ln, bias=ONE32[:, 0:1]),
                     r=[ka, "ONE32"], w=[kg2])
                P.op("pool" if d == 0 else "dve", lambda e: e.tensor_tensor(out=ER32B[:, :], in0=a32[:, :], in1=LB[d][:, :], op=ALU.mult),
                     r=[ka, f"LB{d}"], w=[kt1])
                P.op("act", lambda e: e.activation(out=ER32B[:, :], in_=ER32B[:, :], func=AF.Ln, bias=ONE32[:, 0:1]),
                     r=[kt1, "ONE32"], w=[kt1])
                P.op("pool", lambda e: e.tensor_tensor(out=a32[:, :], in0=ER32B[:, :], in1=G2[:, :], op=ALU.subtract),
                     r=[kt1, kg2, ka], w=[ka])
                P.op("act", lambda e: e.activation(out=G2[:, :], in_=G2[:, :], func=AF.Exp, scale=-1.0), r=[kg2], w=[kg2])
                P.op("pool", lambda e: e.tensor_tensor(out=G2[:, :], in0=G2[:, :], in1=OML[d][:, :], op=ALU.mult),
                     r=[kg2, f"OML{d}"], w=[kg2])
                P.op("pool" if d == 0 else "dve", lambda e: e.tensor_tensor(out=k32[:, :], in0=OML[d][:, :], in1=G2[:, :], op=ALU.subtract),
                     r=[kg2, f"OML{d}"], w=[kk])
            return [s1, s2, s3, s4, s5]

        def b1_steps(i):
            T = tiles[i]
            pp, p3 = i % 2, i % 3
            local = T0 <= T < T1
            a32, k32, v16, q32 = A32[pp], K32[pp], V16[p3], Q32[pp]
            ka, kk, kv_, kq = f"A32{pp}", f"K32{pp}", f"V16{p3}", f"Q32{pp}"
            qet, at, dec = QET16[pp], AT16[pp], DEC[p3]
            kqet, kat, kdec = f"QET{pp}", f"AT{pp}", f"DEC{p3}"
            vref, vkey = (OA16[:, T - T0, :], f"OAV{T - T0}") if local else (v16, kv_)
            EB32_, ENB32_, QE16_, KE16_, KET16_ = EB32[pp], ENB32[pp], QE16[pp], KE16[pp], KET16[pp]
            keb, kenb, kqe, kke, kket = f"EB32{pp}", f"ENB32{pp}", f"QE16{pp}", f"KE16{pp}", f"KET16{pp}"

            def s1():
                def blast(e):
                    ins = None
                    for h in range(4):
                        ins = e.matmul(B[1][:, 2 * h:2 * h + 2], lhsT=a32[:, h * 128:(h + 1) * 128],
                                       rhs=CST[:, C_IND:C_IND + 2], start=True, stop=True)
                    return ins
                P.op("pe", blast, r=[ka, "CST"], w=["B1"], est=0.45)
                P.op("act", lambda e: e.activation(out=dec[:, :], in_=B[1][:, 0:8], func=AF.Exp), r=["B1"], w=[kdec])
                P.op("pe", lambda e: e.matmul(B[2][:, :], lhsT=CST[:, UD:UD + 128], rhs=a32[:, :], start=True, stop=True),
                     r=[ka, "CST"], w=["B2"], est=0.9)
                P.op("act", lambda e: e.activation(out=ENB32_[:, :], in_=B[2][:, :], func=AF.Exp, scale=-1.0),
                     r=["B2"], w=[kenb])
                if local:
                    P.op("act", lambda e: e.activation(out=EB32_[:, :], in_=B[2][:, :], func=AF.Exp), r=["B2"], w=[keb])

            def s2():
                P.op("pool" if local else "dve", lambda e: e.tensor_tensor(out=KE16_[:, :], in0=k32[:, :], in1=ENB32_[:, :], op=ALU.mult),
                     r=[kk, kenb], w=[kke])
                if local:
                    P.op("dve", lambda e: e.tensor_tensor(out=QE16_[:, :], in0=QST[:, T - T0, :], in1=EB32_[:, :], op=ALU.mult),
                         r=[f"QST{T - T0}", keb], w=[kqe])

            def s3():
                if not local:
                    return

                def trq(e):
                    ins = None
                    for h in range(4):
                        ins = e.matmul(B[4][:, h * 128:(h + 1) * 128], lhsT=QE16_[:, h * 128:(h + 1) * 128],
                                       rhs=IDENT16[:, :], start=True, stop=True)
                    return ins

                def trk(e):
                    ins = None
                    for h in range(4):
                        ins = e.matmul(B[4][:, h * 128:(h + 1) * 128], lhsT=KE16_[:, h * 128:(h + 1) * 128],
                                       rhs=IDENT16[:, :], start=True, stop=True)
                    return ins
                P.op("pe", trq, r=[kqe, "IDENT16"], w=["B4"], est=0.27)
                P.op("act", lambda e: e.activation(out=qet[:, :], in_=B[4][:, :], func=AF.Identity), r=["B4"], w=[kqet])
                P.op("pe", trk, r=[kke, "IDENT16"], w=["B4"], est=0.27)
                P.op("act", lambda e: e.activation(out=KET16_[:, :], in_=B[4][:, :], func=AF.Identity), r=["B4"], w=[kket])

            def s4():
                if not local:
                    return

                def amat(e):
                    ins = None
                    for h in range(4):
                        ins = e.matmul(B[4][:, h * 128:(h + 1) * 128], lhsT=KET16_[:, h * 128:(h + 1) * 128],
                                       rhs=qet[:, h * 128:(h + 1) * 128], start=True, stop=True)
                    return ins
                P.op("pe", amat, r=[kqet, kket], w=["B4"], est=0.27)
                P.op("dve", lambda e: e.tensor_tensor(
                    out=at[:, :].rearrange("p (h t) -> p h t", h=4),
                    in0=B[4][:, :].rearrange("p (h t) -> p h t", h=4),
                    in1=CST[:, UD:UD + 128].unsqueeze(1).broadcast_to([128, 4, 128]), op=ALU.mult),
                    r=["B4", "CST"], w=[kat])

            def s5():
                def kv(e):
                    ins = None
                    for c in range(2):
                        for h in range(4):
                            ins = e.matmul(B[KVB[c]][:, h * 128:(h + 1) * 128],
                                           lhsT=KE16_[64 * c:64 * c + 64, h * 128:(h + 1) * 128],
                                           rhs=vref[64 * c:64 * c + 64, h * 128:(h + 1) * 128], start=True, stop=True)
                    return ins
                P.op("pe", kv, r=[kke, vkey], w=["B6", "B3"], est=0.6)
            return [s1, s2, s3, s4, s5]

        def b2_steps(i):
            T = tiles[i]
            pp, p3 = i % 2, i % 3
            local = T0 <= T < T1
            li = T - T0
            v16 = V16[p3]
            kv_ = f"V16{p3}"
            qet, at, dec = QET16[pp], AT16[pp], DEC[p3]
            kqet, kat, kdec = f"QET{pp}", f"AT{pp}", f"DEC{p3}"
            vref, vkey = (OA16[:, li, :], f"OAV{li}") if local else (v16, kv_)

            def chunk(ci):
                c = corder[ci]
                if local:
                    def omat(e):
                        ins = None
                        if ci == 0:
                            for h in range(4):
                                ins = e.matmul(B[5][:, h * 128:(h + 1) * 128], lhsT=at[:, h * 128:(h + 1) * 128],
                                               rhs=vref[:, h * 128:(h + 1) * 128], start=(h == 0), stop=False,
                                               skip_group_check=True)
                        for h in range(4):
                            ins = e.matmul(B[5][64 * c:64 * c + 64, h * 128:(h + 1) * 128],
                                           lhsT=qet[:, h * 128 + 64 * c:h * 128 + 64 * c + 64],
                                           rhs=S16[:, h * 128:(h + 1) * 128], start=False, stop=True,
                                           skip_group_check=True, tile_position=(0, 64 * c))
                        return ins
                    P.op("pe", omat, r=[kat, vkey, kqet, "S16"], w=["B5"], est=0.5)
                pdec, pkey, pc = prevdec[0] if prevdec[0] is not None else (dec, kdec, c)
                for h in range(4):
                    P.op("dve", lambda e, h=h: e.scalar_tensor_tensor(
                        out=ST32[:, h * 128:(h + 1) * 128], in0=ST32[:, h * 128:(h + 1) * 128],
                        scalar=pdec[:, 2 * h + pc:2 * h + pc + 1], in1=B[KVB[c]][:, h * 128:(h + 1) * 128],
                        op0=ALU.mult, op1=ALU.add), r=[f"ST32h{h}", pkey, f"B{KVB[c]}"], w=[f"ST32h{h}"], est=0.4)
                P.op("dve", lambda e: e.tensor_tensor(
                    out=S16[:, :].rearrange("p (h t) -> p h t", h=4),
                    in0=ST32[:, :].rearrange("p (h t) -> p h t", h=4),
                    in1=dec[:, :].rearrange("p (h c) -> p h c", c=2)[:, :, c:c + 1].broadcast_to([128, 4, 128]), op=ALU.mult),
                    r=[f"ST32h{h_}" for h_ in range(4)] + [kdec], w=["S16"])
                prevdec[0] = (dec, kdec, c)

            def s1():
                chunk(0)

            def s2():
                chunk(1)

            def s3():
                if not local:
                    return
                if d == 0:
                    P.op("dve", lambda e: e.tensor_copy(out=OF32[:, li, :], in_=B[5][:, :]), r=["B5"], w=[f"OF{li}"])
                else:
                    o32, on32, ss, rstd = O32[pp], ON32[pp], SS[pp], RSTD[pp]
                    ko, kon, kss, krs = f"O32{pp}", f"ON32{pp}", f"SS{pp}", f"RSTD{pp}"
                    P.op("dve", lambda e: e.tensor_tensor(out=o32[:, :], in0=B[5][:, :], in1=OF32[:, li, :], op=ALU.add),
                         r=["B5", f"OF{li}"], w=[ko])
                    P.op("pool", lambda e: e.memset(ss[:, :], 0.0), w=[kss + str(h_) for h_ in range(4)])
                    for h in range(4):
                        P.op("act", lambda e, h=h: e.activation(out=on32[:, h * 128:(h + 1) * 128], in_=o32[:, h * 128:(h + 1) * 128],
                                                                func=AF.Square, accum_out=ss[:, h:h + 1]),
                             r=[ko], w=[kon + str(h), kss + str(h)], est=0.35)
                    P.op("dve", lambda e: e.tensor_scalar(out=rstd[:, :], in0=ss[:, :], scalar1=1.0 / 128.0, scalar2=RMS_EPS,
                                                          op0=ALU.mult, op1=ALU.add), r=[kss + str(h_) for h_ in range(4)], w=[krs], est=0.2)
                    P.op("act", lambda e: e.activation(out=rstd[:, :], in_=rstd[:, :], func=AF.Ln), r=[krs], w=[krs], est=0.2)
                    P.op("act", lambda e: e.activation(out=rstd[:, :], in_=rstd[:, :], func=AF.Exp, scale=-0.5), r=[krs], w=[krs], est=0.2)
                    P.op("pool", lambda e: e.tensor_tensor(
                        out=on32[:, :].rearrange("p (h t) -> p h t", h=4),
                        in0=o32[:, :].rearrange("p (h t) -> p h t", h=4),
                        in1=rstd[:, :].unsqueeze(2).broadcast_to([128, 4, 128]), op=ALU.mult),
                        r=[ko, krs] + [kon + str(h_) for h_ in range(4)], w=[kon + str(h_) for h_ in range(4)])
                    P.op("pool", lambda e: e.tensor_tensor(out=OH16[:, li, :], in0=on32[:, :], in1=SGc[p3][:, :], op=ALU.mult),
                         r=[kon + str(h_) for h_ in range(4)] + [f"SGc{p3}"], w=[f"OH{li}"])
            return [s1, s2, s3]

        n = len(tiles)
        nop = lambda: None
        for it in range(-2, n):
            b2 = b2_steps(it) if it >= 0 else [nop] * 3
            b1 = b1_steps(it + 1) if 0 <= it + 1 < n else [nop] * 5
            a = a_steps(it + 2) if it + 2 < n else [nop] * 5
            for st in (b2[0], a[0], b1[0], b2[1], a[1], b1[1], b2[2], a[2], b1[2], a[3], b1[3], a[4], b1[4]):
                st()

    hg_pass(0)
    if upto == 1:
        P.emit([])
        return nc
    hg_pass(1)
    if upto == 2:
        P.emit([])
        return nc
    load_xt(1, 0)
    load_w(1024, w_in_v[:, :, 1024:1536], 512)

    P.barrier()
    A.reset(phase_mark)

    KT = A("kt", [128, 4, NTOK], BF16)
    VA = A("va", [128, NT * 8, 65], BF16)
    QTG = A("qtg", [128, 4, 512], BF16)
    SGA = A("sga", [128, 4, 512], BF16)
    EBI = A("ebi", [128, 8, 5 * 128], BF16)
    EBH = [A(f"ebh{i}", [128, 7 * 128], BF16) for i in range(3)]
    STG = [A(f"stg{i}", [128, 7 * 128], F32) for i in range(2)]
    PX32s = [A(f"px32{i}", [128, 1024], F32) for i in range(4)]
    P16 = [A(f"p16{i}", [128, 1024], BF16) for i in range(4)]
    RDEN = A("rden", [128, 8], F32)
    ONE32a = A("one32a", [128, 8], F32)
    SGTa = A("sgta", [128, 512], F32)
    P.op("dve", lambda e: e.memset(ONE32a[:, :], 1.0), w=["ONE32a"])
    OA32 = A("oa32", [128, 512], F32)
    b_qk = CST[:, C_BQK:C_BQK + 8]

    load_bias(0, 1024, 512)
    load_w(1536, w_in_v[:, :, 0:512], 512)
    load_w(512, w_in_v[:, :, 1536:2048], 512)
    load_bias(512, 1536, 512)
    P.op("pool", lambda e: e.memset(VA[:, :, 64:65], 1.0), w=["VA1"])

    def load_tab(dst, dram_ap, nj, key, sidx):
        for h in range(8):
            sb = (sidx[0]) % 2
            sidx[0] += 1
            P.op("sp", lambda e, h=h, sb=sb: e.dma_start(out=STG[sb][:, 0:nj * 128], in_=dram_ap[h]),
                 w=[f"STG{sb}"], dma=f"stg{sb}")
            P.op("act", lambda e, h=h, sb=sb: e.activation(out=dst[:, h, 0:nj * 128], in_=STG[sb][:, 0:nj * 128], func=AF.Exp),
                 r=[f"STG{sb}"], w=[key])
    sidx = [0]
    load_tab(EBI, d_tabi, 5, "EBI", sidx)

    def proj_t(bank, buf, wc0, dst_fn, bias_col, keyw):
        def fn(e):
            ins = None
            for k in range(8):
                ins = e.matmul(BB[bank][:, :], lhsT=W[:, k, wc0:wc0 + 128], rhs=XT[buf][:, k, :],
                               start=(k == 0), stop=(k == 7))
            return ins
        P.op("pe", fn, r=[f"XT{buf}", f"W{wc0 // 512}"], w=[f"B{bank}"], est=1.8)
        P.op("act", lambda e: e.activation(out=dst_fn, in_=BB[bank][:, :], func=AF.Identity,
                                           bias=b_qk[:, bias_col:bias_col + 1]), r=[f"B{bank}", "CST"], w=[keyw])

    def a1_group(g, xbuf=1):
        if g > 0:
            load_xt(xbuf, g)
        for m4 in range(4):
            proj_t(7, xbuf, m4 * 128, KT[:, m4, g * 512:(g + 1) * 512], 4 + m4, f"KT{g}")
        for tl in range(4):
            T = g * 4 + tl
            if T < 2 or T > 21:
                continue
            proj_n(2, xbuf, tl, 2, 0)
            P.op("dve", lambda e, T=T: e.tensor_copy(out=VA[:, T * 8:(T + 1) * 8, 0:64], in_=B[2][:, :].rearrange("p (h c) -> p h c", h=8)),
                 r=["B2"], w=[f"VA{T}"])
    for g in range(3):
        a1_group(g, 1 - g % 2)

    w_out_v = d_wout.rearrange("(k p) c -> p k c", p=128)
    hcount = [0]
    ringc = [0]
    var_js = {0: list(range(-2, 4)), 1: list(range(-3, 4)), 14: list(range(-3, 4)), 15: list(range(-3, 3))}
    for g in range(1, 5):
        buf = 0
        load_xt(0, g)
        for m4 in range(4):
            proj_t(m4 % 2, buf, 1536 + m4 * 128, QTG[:, m4, :], m4, "QTG")
        for tl in range(4):
            proj_n(2, buf, tl, 1, 1)
            P.op("act", lambda e: e.activation(out=SGTa[:, :], in_=B[2][:, :], func=AF.Exp, scale=-1.0), r=["B2"], w=["SGTa"])
            P.op("act", lambda e: e.activation(out=SGTa[:, :], in_=SGTa[:, :], func=AF.Ln, bias=ONE32a[:, 0:1]),
                 r=["SGTa", "ONE32a"], w=["SGTa"])
            P.op("act", lambda e: e.activation(out=SGTa[:, :], in_=SGTa[:, :], func=AF.Exp, scale=-1.0), r=["SGTa"], w=["SGTa"])
            P.op("dve", lambda e, tl=tl: e.tensor_tensor(out=SGA[:, tl, :], in0=B[2][:, :], in1=SGTa[:, :], op=ALU.mult),
                 r=["B2", "SGTa"], w=[f"SGA{tl}"])
        if g + 2 <= 5:
            a1_group(g + 2)
        if g == 3:
            load_w(0, w_out_v[:, :, 0:512], 512)
            load_w(1024, w_out_v[:, :, 512:1024], 512)
            load_bias(1024, 4608, 1024)
        for tl in range(4):
            T = g * 4 + tl
            m = T - T0
            if m in var_js:
                js = var_js[m]
                bi = {0: 0, 1: 1, 14: 2, 15: 3}[m]
                border, j0 = True, -3
            else:
                js = list(range(-2, 3))
                border, j0 = False, -2
            nj = len(js)
            for h in range(8):
                m4, half = h // 2, h % 2
                hc_ = hcount[0]
                hcount[0] += 1
                pb = hc_ % 4
                p0 = 64 * half
                sb0 = 3 if hc_ % 2 == 0 else 0
                PX32 = PX32s[pb]
                if border:
                    rs = ringc[0] % 3
                    ringc[0] += 1
                    sb = sidx[0] % 2
                    sidx[0] += 1
                    P.op("sp", lambda e, h=h, sb=sb, bi=bi: e.dma_start(out=STG[sb][:, :], in_=d_tabb[bi][h]),
                         w=[f"STG{sb}"], dma=f"stg{sb}")
                    P.op("act", lambda e, sb=sb, rs=rs: e.activation(out=EBH[rs][:, :], in_=STG[sb][:, :], func=AF.Exp),
                         r=[f"STG{sb}"], w=[f"EBH{rs}"])
                    tabv, tkey = EBH[rs], f"EBH{rs}"
                else:
                    tabv, tkey = EBI[:, h, :], "EBI"

                def smat(e, js=js, m4=m4, p0=p0, T=T, tl=tl, sb0=sb0):
                    ins = None
                    for idx, j in enumerate(js):
                        bk = B[sb0 + idx // 4]
                        ins = e.matmul(bk[:, (idx % 4) * 128:(idx % 4 + 1) * 128],
                                       lhsT=KT[p0:p0 + 64, m4, (T + j) * 128:(T + j + 1) * 128],
                                       rhs=QTG[p0:p0 + 64, m4, tl * 128:(tl + 1) * 128], start=True, stop=True)
                    return ins
                P.op("pe", smat, r=["QTG"] + [f"KT{(T + j) // 4}" for j in js], w=[f"B{sb0}", f"B{sb0 + 1}"], est=0.5)
                n0 = min(nj, 4)
                P.op("act", lambda e, n0=n0, PX32=PX32, sb0=sb0: e.activation(out=PX32[:, 0:n0 * 128], in_=B[sb0][:, 0:n0 * 128],
                                                                             func=AF.Exp, scale=0.125),
                     r=[f"B{sb0}"], w=[f"PXa{pb}"])
                if nj > 4:
                    P.op("act", lambda e, nj=nj, PX32=PX32, sb0=sb0: e.activation(
                        out=PX32[:, 512:512 + (nj - 4) * 128], in_=B[sb0 + 1][:, 0:(nj - 4) * 128],
                        func=AF.Exp, scale=0.125), r=[f"B{sb0 + 1}"], w=[f"PXb{pb}"])
                c0 = (js[0] - j0) * 128
                P.op("dve", lambda e, pb=pb, nj=nj, c0=c0, tabv=tabv, PX32=PX32: e.tensor_tensor(
                    out=P16[pb][:, 0:nj * 128], in0=PX32[:, 0:nj * 128], in1=tabv[:, c0:c0 + nj * 128], op=ALU.mult),
                    r=[f"PXa{pb}", f"PXb{pb}", tkey], w=[f"P16{pb}"], est=0.95)

                def pv(e, js=js, pb=pb, h=h, T=T):
                    ins = None
                    bk = B[5 + h // 4]
                    hc = (h % 4) * 65
                    for idx, j in enumerate(js):
                        ins = e.matmul(bk[:, hc:hc + 65], lhsT=P16[pb][:, idx * 128:(idx + 1) * 128],
                                       rhs=VA[:, (T + j) * 8 + h, :], start=(idx == 0), stop=(idx == len(js) - 1))
                    return ins
                P.op("pe", pv, r=[f"P16{pb}", "VA1"] + [f"VA{T + j}" for j in js], w=[f"B{5 + h // 4}"], est=0.45)
            for hb in range(2):
                P.op("dve", lambda e, hb=hb: e.reciprocal(out=RDEN[:, 4 * hb:4 * hb + 4].unsqueeze(2),
                                                          in_=B[5 + hb][:, 0:260].rearrange("p (h c) -> p h c", h=4)[:, :, 64:65]),
                     r=[f"B{5 + hb}"], w=["RDEN"])
                P.op("dve", lambda e, hb=hb: e.tensor_tensor(
                    out=OA32[:, hb * 256:(hb + 1) * 256].rearrange("p (h c) -> p h c", h=4),
                    in0=B[5 + hb][:, 0:260].rearrange("p (h c) -> p h c", h=4)[:, :, 0:64],
                    in1=RDEN[:, 4 * hb:4 * hb + 4].unsqueeze(2).broadcast_to([128, 4, 64]), op=ALU.mult),
                    r=[f"B{5 + hb}", "RDEN"], w=[f"OA32{hb}"])
            P.op("pool", lambda e, m=m, tl=tl: e.tensor_tensor(out=OA16[:, m, :], in0=OA32[:, :], in1=SGA[:, tl, :], op=ALU.mult),
                 r=["OA320", "OA321", f"SGA{tl}"], w=[f"OA{m}"])

    P.op("sp", lambda e: e.dma_start(out=LNP[:, :], in_=d_lnp[:, :]), w=["XT1", "LNP"], dma="lnp")
    for m in range(2):
        P.op("sp", lambda e, m=m: e.dma_start(out=XN32[m][:, :], in_=d_xN[m * 128:(m + 1) * 128, :]),
             w=["XT0", f"XN{m}"] if m == 0 else [f"XN{m}"], r=[] if m == 0 else ["XT0"], dma=f"xn{m}")
    if upto == 3:
        P.emit([])
        return nc
    P.barrier()
    A.reset(phase_mark)

    ND = 4
    R32s = [A(f"r32{i}", [128, 1024], F32) for i in range(ND)]
    RN32s = [A(f"rn32{i}", [128, 1024], F32) for i in range(ND)]
    JKa = [A(f"jka{i}", [128, 1024], F32) for i in range(ND)]
    JKb = [A(f"jkb{i}", [128, 1024], F32) for i in range(ND)]
    OT16s = [A(f"ot16{i}", [128, 8, 128], BF16) for i in range(ND)]
    STATs = [A(f"stat{i}", [128, 8], F32) for i in range(ND)]
    XNs = [XN32[0], XN32[1]] + [A(f"xn{i}", [128, 1024], F32) for i in range(2, ND)]
    EPSC = A("epsc", [128, 8], F32)
    P.op("dve", lambda e: e.memset(EPSC[:, :], LN_EPS), w=["EPSC"])

    final = []
    def c_tile(m):
        xb = m % ND
        R32, RN32, OT16, STAT, XN, Ja, Jb = R32s[xb], RN32s[xb], OT16s[xb], STATs[xb], XNs[xb], JKa[xb], JKb[xb]
        sx = str(xb)
        yb = (0, 1) if m % 2 == 0 else (3, 4)
        tb = (7, 2) if m % 2 == 0 else (5, 6)
        if m >= 2:
            P.op("sp", lambda e: e.dma_start(out=XN[:, :], in_=d_xN[m * 128:(m + 1) * 128, :]),
                 w=[f"XN{xb}"], dma=f"xn{xb}")
        if debug and debug != 3:
            P.op("pool", lambda e: e.dma_start(out=d_dbg[m * 128:(m + 1) * 128, 0:512], in_=OA16[:, m, :]),
                 r=[f"OA{m}"], dma="dbg")
            if debug not in (2, 3):
                P.op("pool", lambda e: e.dma_start(out=d_dbg[m * 128:(m + 1) * 128, 512:1024], in_=OH16[:, m, :]),
                     r=[f"OH{m}"], dma="dbg")

        def trc(e):
            ins = None
            for k in range(8):
                src = OA16[:, m, k * 128:(k + 1) * 128] if k < 4 else OH16[:, m, (k - 4) * 128:(k - 3) * 128]
                dst = BB[tb[0]][:, k * 128:(k + 1) * 128] if k < 4 else BB[tb[1]][:, (k - 4) * 128:(k - 3) * 128]
                ins = e.matmul(dst, lhsT=src, rhs=IDENT16[:, :], start=True, stop=True)
            return ins
        P.op("pe", trc, r=[f"OA{m}", f"OH{m}", "IDENT16"], w=[f"B{tb[0]}", f"B{tb[1]}"], est=0.5)
        P.op("act", lambda e: e.activation(out=OT16[:, 0:4, :], in_=BB[tb[0]][:, :].rearrange("p (k t) -> p k t", k=4), func=AF.Identity),
             r=[f"B{tb[0]}"], w=["OT16a_" + sx])
        P.op("dve", lambda e: e.tensor_copy(out=OT16[:, 4:8, :], in_=BB[tb[1]][:, :].rearrange("p (k t) -> p k t", k=4)),
             r=[f"B{tb[1]}"], w=["OT16b_" + sx])
        for hf in range(2):
            def ymat(e, hf=hf):
                ins = None
                for k in range(8):
                    ins = e.matmul(BB[yb[hf]][:, :], lhsT=OT16[:, k, :], rhs=W[:, k, hf * 1024:hf * 1024 + 512],
                                   start=(k == 0), stop=False)
                ins = e.matmul(BB[yb[hf]][:, :], lhsT=ONES16[0:1, :], rhs=BIAS16[0:1, 1024 + hf * 512:1024 + (hf + 1) * 512],
                               start=False, stop=True)
                return ins
            P.op("pe", ymat, r=["OT16a_" + sx, "OT16b_" + sx, f"W{2 * hf}", f"BIAS{2 + hf}", "ONES16"], w=[f"B{yb[hf]}"], est=2.1)
            P.op("dve", lambda e, hf=hf: e.scalar_tensor_tensor(
                out=R32[:, hf * 512:(hf + 1) * 512], in0=XN[:, hf * 512:(hf + 1) * 512], scalar=ALPHA,
                in1=BB[yb[hf]][:, :], op0=ALU.mult, op1=ALU.add), r=[f"XN{xb}", f"B{yb[hf]}"], w=[f"R32{hf}_" + sx])
        P.op("pool", lambda e: e.memset(STAT[:, 0:2], 0.0), w=["STAT0_" + sx, "STAT1_" + sx], est=0.2)
        P.op("act", lambda e: e.activation(out=Ja[:, :], in_=R32[:, :], func=AF.Identity, accum_out=STAT[:, 0:1]),
             r=["R320_" + sx, "R321_" + sx], w=["JKa_" + sx, "STAT0_" + sx], est=1.2)
        P.op("act", lambda e: e.activation(out=Jb[:, :], in_=R32[:, :], func=AF.Square, accum_out=STAT[:, 1:2]),
             r=["R320_" + sx, "R321_" + sx], w=["JKb_" + sx, "STAT1_" + sx], est=1.2)
        P.op("dve", lambda e: e.tensor_scalar(out=STAT[:, 2:4], in0=STAT[:, 0:2], scalar1=1.0 / 1024.0, scalar2=None, op0=ALU.mult),
             r=["STAT0_" + sx, "STAT1_" + sx], w=["STAT2_" + sx], est=0.15)
        P.op("dve", lambda e: e.tensor_scalar(out=STAT[:, 4:5], in0=STAT[:, 2:3], scalar1=STAT[:, 2:3], scalar2=STAT[:, 3:4],
                                              op0=ALU.mult, op1=ALU.subtract), r=["STAT2_" + sx], w=["STAT4_" + sx], est=0.15)
        P.op("act", lambda e: e.activation(out=STAT[:, 5:6], in_=STAT[:, 4:5], func=AF.Ln, scale=-1.0, bias=EPSC[:, 0:1]),
             r=["STAT4_" + sx, "EPSC"], w=["STAT5_" + sx], est=0.15)
        P.op("act", lambda e: e.activation(out=STAT[:, 5:6], in_=STAT[:, 5:6], func=AF.Exp, scale=-0.5), r=["STAT5_" + sx], w=["STAT5_" + sx], est=0.15)
        P.op("dve", lambda e: e.scalar_tensor_tensor(out=STAT[:, 6:7], in0=STAT[:, 2:3], scalar=-1.0, in1=STAT[:, 5:6],
                                                      op0=ALU.mult, op1=ALU.mult), r=["STAT2_" + sx, "STAT5_" + sx], w=["STAT6_" + sx], est=0.15)
        P.op("act", lambda e: e.activation(out=RN32[:, :], in_=R32[:, :], func=AF.Identity, scale=STAT[:, 5:6], bias=STAT[:, 6:7]),
             r=["R320_" + sx, "R321_" + sx, "STAT5_" + sx, "STAT6_" + sx], w=["RN32_" + sx], est=1.2)
        P.op("dve", lambda e: e.tensor_tensor(out=RN32[:, :], in0=RN32[:, :], in1=LNP[:, 0:1024], op=ALU.mult),
             r=["RN32_" + sx, "LNP"], w=["RN32_" + sx], est=1.4)
        P.op("pool", lambda e: e.tensor_tensor(out=RN32[:, :], in0=RN32[:, :], in1=LNP[:, 1024:2048], op=ALU.add),
             r=["RN32_" + sx, "LNP"], w=["RN32_" + sx], est=2.5)
        tok = P.op("sp", lambda e: e.dma_start(out=d_out[m * 128:(m + 1) * 128, :], in_=RN32[:, :]),
                   r=["RN32_" + sx], dma=f"st{xb}")
        final.append(tok)
    for m in range(16):
        c_tile(m)
    if debug:
        final.extend(o for seg in P.segs for o in seg if o.dma == "dbg")
    P.emit(final)
    return nc


def _const_block():
    c = np.zeros((128, NCST), np.float32)
    c[:, C_ID:C_ID + 128] = np.eye(128, dtype=np.float32)
    s = np.arange(128)[:, None]
    t = np.arange(128)[None, :]
    same = (s // 64) == (t // 64)
    c[:, C_UF:C_UF + 128] = (same & (s <= t)).astype(np.float32)
    c[:, C_RF:C_RF + 128] = (same & (s > t)).astype(np.float32)
    c[:, C_UB:C_UB + 128] = (same & (s >= t)).astype(np.float32)
    c[:, C_RB:C_RB + 128] = (same & (s < t)).astype(np.float32)
    c[:, C_IND + 0] = (np.arange(128) < 64).astype(np.float32)
    c[:, C_IND + 1] = (np.arange(128) >= 64).astype(np.float32)
    return c


def _att_table(rpb, R0, m, js):
    p = np.arange(128)
    kr_i, kc = p // 64, p % 64
    qr_i, qc = p // 64, p % 64
    out = np.full((8, 128, len(js), 128), NEG, np.float32)
    qr = R0 + 2 * m + qr_i
    rs = np.clip(qr - 4, 0, 120)
    cs = np.clip(qc - 8, 0, 48)
    for ji, j in enumerate(js):
        kr = R0 + 2 * m + 2 * j + kr_i
        ok = (kr[:, None] >= 0) & (kr[:, None] < 128) & (kr[:, None] >= rs[None, :]) & (kr[:, None] < rs[None, :] + 8) \
            & (kc[:, None] >= cs[None, :]) & (kc[:, None] < cs[None, :] + 16)
        dr = np.clip(kr[:, None] - qr[None, :] + 7, 0, 14)
        dc = np.clip(kc[:, None] - qc[None, :], -15, 15) + 15
        g = rpb[:, dr, dc]
        out[:, :, ji, :] = np.where(ok[None], g, NEG)
    return out


_NC_CACHE = {}


def _prep_inputs(x, w_in, b_in, rpb, lb_fwd_logits, lb_bwd_logits, hg_norm_gain, w_out, b_out, ln_gain, ln_bias):
    x = np.asarray(x, np.float32)
    w_in0 = np.ascontiguousarray(np.asarray(w_in, np.float32)[0])
    w_out0 = np.ascontiguousarray(np.asarray(w_out, np.float32)[0])
    b_in0 = np.asarray(b_in, np.float32)[0]
    rpb0 = np.asarray(rpb, np.float32)[0]
    brow = np.concatenate([b_in0, np.asarray(b_out, np.float32)[0]])[None, :].astype(np.float32)
    lbf = np.asarray(lb_fwd_logits, np.float32)
    lbb = np.asarray(lb_bwd_logits, np.float32)
    lbl = np.ascontiguousarray(np.broadcast_to(
        np.concatenate([lbf[0], lbf[1], lbb[0], lbb[1], b_in0[2048:2560], b_in0[3584:4096]])[None, :], (128, 3072))).astype(np.float32)
    lnp = np.ascontiguousarray(np.broadcast_to(
        np.concatenate([np.asarray(ln_gain, np.float32)[0], np.asarray(ln_bias, np.float32)[0]])[None, :], (128, 2048))).astype(np.float32)
    cbase = _const_block()
    cbase[:, C_GAIN:C_GAIN + 512] = np.asarray(hg_norm_gain, np.float32)[0][None, :]
    cbase[:, C_BQK:C_BQK + 4] = b_in0[0:512].reshape(4, 128).T
    cbase[:, C_BQK + 4:C_BQK + 8] = b_in0[512:1024].reshape(4, 128).T
    in_maps = []
    for c in range(NCORES):
        b, s = c // 4, c % 4
        t0 = s * NLOC
        lo, hi = max(0, t0 - HALO), min(SEQ, t0 + NLOC + HALO)
        xT = np.zeros((DM, NTOK), np.float32)
        xT[:, lo - (t0 - HALO):hi - (t0 - HALO)] = x[b, lo:hi, :].T
        cst = cbase.copy()
        tpos = t0 - HALO + np.arange(NT)[None, :] * 128 + np.arange(128)[:, None]
        cst[:, C_TM:C_TM + NT] = ((tpos >= 0) & (tpos < SEQ)).astype(np.float32)
        R0 = 32 * s
        tabi = _att_table(rpb0, R0, 7, list(range(-2, 3))).reshape(8, 128, 5 * 128)
        tabb = np.stack([_att_table(rpb0, R0, m, list(range(-3, 4))).reshape(8, 128, 7 * 128) for m in (0, 1, 14, 15)])
        in_maps.append({
            "xT": xT, "xN": np.ascontiguousarray(x[b, t0:t0 + NLOC, :]), "w_in": w_in0, "w_out": w_out0,
            "brow": brow, "cst": cst, "lbl": lbl, "lnp": lnp,
            "tabi": np.ascontiguousarray(tabi), "tabb": np.ascontiguousarray(tabb),
        })
    return in_maps


def kernel(x, w_in, b_in, rpb, lb_fwd_logits, lb_bwd_logits, hg_norm_gain, w_out, b_out, ln_gain, ln_bias):
    in_maps = _prep_inputs(x, w_in, b_in, rpb, lb_fwd_logits, lb_bwd_logits, hg_norm_gain, w_out, b_out, ln_gain, ln_bias)
    if "nc" not in _NC_CACHE:
        _NC_CACHE["nc"] = build_nc()
    res = run_bass_kernel_spmd(_NC_CACHE["nc"], in_maps, core_ids=list(range(NCORES)))
    out = np.zeros((2, SEQ, DM), np.float32)
    for c in range(NCORES):
        b, s = c // 4, c % 4
        out[b, s * NLOC:(s + 1) * NLOC, :] = res.results[c]["out"]
    return out
```
